# Optimizing a Trainium2 kernel written in Bass

```python
import math
import jax
import jax.numpy as jnp
from jax import lax
import numpy as np

D_MODEL = 2048
BATCH = 4
SEQ = 4096
DEPTH = 1
DEC_BATCH = 4
DEC_SEQ = 8192
PAST_LEN = 128

ATTN_HEADS = 8
ATTN_QK_DIM = 64
ATTN_V_DIM = 2 * ATTN_QK_DIM
ATTN_QK_WIDTH = ATTN_HEADS * 2 * ATTN_QK_DIM
ATTN_WIDTH = ATTN_HEADS * ATTN_V_DIM
HYENA_WIDTH = D_MODEL // 2
HYENA_PROJ = 3
HYENA_EMB_DIM = 33
HYENA_BANDS = (HYENA_EMB_DIM - 1) // 2
FILTER_WIDTH = 64
DECAY_TARGET = 1e-2
FAST_DECAY_PCT = 0.3
SLOW_DECAY_PCT = 1.5
SHORT_CONV = 3
N_BRANCH = 2
D_FF = 5632
ROPE_THETA = 10000.0
NORM_EPS = 1e-6
Q_BLOCK = 128
IN_WIDTH = 2 * ATTN_QK_WIDTH + ATTN_WIDTH + HYENA_PROJ * HYENA_WIDTH + N_BRANCH * D_MODEL

kernel_name = "hybrid_diffattn_hyena_encoder"


def rmsnorm(x, g):
    xf = x.astype(jnp.float32)
    xf = xf * lax.rsqrt(jnp.mean(xf * xf, axis=-1, keepdims=True) + NORM_EPS)
    return (xf * g.astype(jnp.float32)).astype(x.dtype)


def dwconv3(x, w, b):
    L = x.shape[1]
    xp = jnp.pad(x, ((0, 0), (1, 1), (0, 0)))
    return xp[:, 0:L] * w[0] + xp[:, 1:L + 1] * w[1] + xp[:, 2:L + 2] * w[2] + b


def rope(x):
    L, d = x.shape[1], x.shape[-1]
    inv = ROPE_THETA ** (-jnp.arange(0, d, 2, dtype=jnp.float32) / d)
    ang = jnp.arange(L, dtype=jnp.float32)[:, None] * inv[None]
    ang = jnp.concatenate([ang, ang], -1)[:, None, None, :]
    xf = x.astype(jnp.float32)
    rot = jnp.concatenate([-xf[..., d // 2:], xf[..., :d // 2]], -1)
    return (xf * jnp.cos(ang) + rot * jnp.sin(ang)).astype(x.dtype)


def diff_attention(q, k, v, lam):
    B, L, H = q.shape[:3]
    nb = L // Q_BLOCK
    scale = ATTN_QK_DIM ** -0.5
    qb = jnp.moveaxis(q.reshape(B, nb, Q_BLOCK, H, 2, ATTN_QK_DIM), 1, 0)
    vf = v.astype(jnp.float32)

    def block(qi):
        s = jnp.einsum("bqhmd,bkhmd->bhmqk", qi, k, preferred_element_type=jnp.float32) * scale
        p = jax.nn.softmax(s, axis=-1)
        a = p[:, :, 0] - lam * p[:, :, 1]
        return jnp.einsum("bhqk,bkhd->bqhd", a, vf).astype(v.dtype)

    o = lax.map(block, qb)
    return jnp.moveaxis(o, 0, 1).reshape(B, L, H, ATTN_V_DIM)


def hyena_filters(L, w1, b1, w2, b2, w3, b3, w4, freq):
    t = jnp.linspace(0.0, 1.0, L, dtype=jnp.float32)[:, None]
    w = 2.0 * math.pi * jnp.arange(L, dtype=jnp.float32)[:, None] / L
    f = jnp.linspace(1e-4, HYENA_BANDS - 1, HYENA_BANDS, dtype=jnp.float32)[None]
    z = jnp.concatenate([t, jnp.cos(f * w), -jnp.sin(f * w)], -1)
    h = jnp.sin(freq * (z @ w1 + b1))
    h = jnp.sin(freq * (h @ w2 + b2))
    h = jnp.sin(freq * (h @ w3 + b3))
    h = (h @ w4).astype(jnp.float32).reshape(L, 2, HYENA_WIDTH)
    max_decay = math.log(DECAY_TARGET) / FAST_DECAY_PCT
    min_decay = math.log(DECAY_TARGET) / SLOW_DECAY_PCT
    deltas = jnp.abs(jnp.linspace(min_decay, max_decay, HYENA_WIDTH, dtype=jnp.float32))
    h = h * jnp.exp(-t * deltas[None])[:, None, :]
    return h[:, 0], h[:, 1]


def fftconv_bidir(u, h_fwd, h_bwd, bias):
    L, C = h_fwd.shape
    n = 2 * L
    filt = jnp.concatenate([h_fwd, jnp.zeros((1, C), jnp.float32), h_bwd[1:][::-1]], 0)
    Hf = jnp.fft.rfft(filt, n=n, axis=0)
    U = jnp.fft.rfft(u.astype(jnp.float32), n=n, axis=1)
    y = jnp.fft.irfft(U * Hf[None], n=n, axis=1)[:, :L]
    return (y + u.astype(jnp.float32) * bias.astype(jnp.float32)).astype(u.dtype)


def encoder_layer(x, lambda_init, norm1, w_in, in_conv_w, in_conv_b, gate_b,
                  lambda_q1, lambda_k1, lambda_q2, lambda_k2, subln_g,
                  filt_w1, filt_b1, filt_w2, filt_b2, filt_w3, filt_b3, filt_w4, filt_freq,
                  hyena_bias, w_attn_out, w_hyena_out, w_out,
                  norm2, w_up, ffn_conv_w, ffn_conv_b, w_down):
    B, L, _ = x.shape
    hn = rmsnorm(x, norm1)
    proj = hn @ w_in
    o1 = ATTN_QK_WIDTH
    o2 = o1 + ATTN_QK_WIDTH
    o3 = o2 + ATTN_WIDTH
    o4 = o3 + HYENA_PROJ * HYENA_WIDTH
    q, k, v, hy, gl = jnp.split(proj, [o1, o2, o3, o4], axis=-1)

    q = rope(q.reshape(B, L, ATTN_HEADS, 2, ATTN_QK_DIM))
    k = rope(k.reshape(B, L, ATTN_HEADS, 2, ATTN_QK_DIM))
    v = v.reshape(B, L, ATTN_HEADS, ATTN_V_DIM)
    lam = (jnp.exp(jnp.sum(lambda_q1.astype(jnp.float32) * lambda_k1.astype(jnp.float32)))
           - jnp.exp(jnp.sum(lambda_q2.astype(jnp.float32) * lambda_k2.astype(jnp.float32)))
           + lambda_init)
    attn = diff_attention(q, k, v, lam)
    attn = (rmsnorm(attn, subln_g) * (1.0 - lambda_init)).reshape(B, L, ATTN_WIDTH)

    hy = dwconv3(hy, in_conv_w, in_conv_b)
    x0, x1, hv = jnp.split(hy, HYENA_PROJ, axis=-1)
    h_fwd, h_bwd = hyena_filters(L, filt_w1, filt_b1, filt_w2, filt_b2, filt_w3, filt_b3,
                                 filt_w4, filt_freq)
    hyena = fftconv_bidir(hv * x1, h_fwd, h_bwd, hyena_bias) * x0

    gates = jax.nn.sigmoid(gl.reshape(B, L, N_BRANCH, D_MODEL) + gate_b.reshape(N_BRANCH, D_MODEL))
    merged = gates[:, :, 0] * (attn @ w_attn_out) + gates[:, :, 1] * (hyena @ w_hyena_out)
    x = x + merged @ w_out

    hn = rmsnorm(x, norm2)
    u = dwconv3(hn @ w_up, ffn_conv_w, ffn_conv_b)
    ug, uv = jnp.split(u, 2, axis=-1)
    return x + (jax.nn.silu(ug) * uv) @ w_down


def setup_inputs(seed: int = 0) -> dict:
    key = jax.random.key(seed)
    ks = jax.random.split(key, 40)
    counter = [0]

    def nrm(shape, scale):
        k = ks[counter[0]]
        counter[0] += 1
        return jax.random.normal(k, shape, jnp.float32) * scale

    def gain(shape):
        return 1.0 + nrm(shape, 0.02)

    C = HYENA_WIDTH
    inputs = {}
    inputs["x_prompt"] = nrm((BATCH, SEQ, D_MODEL), 1.0)
    inputs["x_sample"] = nrm((DEC_BATCH, DEC_SEQ, D_MODEL), 1.0)
    inputs["norm1"] = gain((DEPTH, D_MODEL))
    inputs["w_in"] = nrm((DEPTH, D_MODEL, IN_WIDTH), D_MODEL ** -0.5)
    inputs["in_conv_w"] = nrm((DEPTH, SHORT_CONV, HYENA_PROJ * C), 0.5)
    inputs["in_conv_b"] = nrm((DEPTH, HYENA_PROJ * C), 0.02)
    inputs["gate_b"] = nrm((DEPTH, N_BRANCH * D_MODEL), 0.1)
    inputs["lambda_q1"] = nrm((DEPTH, ATTN_QK_DIM), 0.1)
    inputs["lambda_k1"] = nrm((DEPTH, ATTN_QK_DIM), 0.1)
    inputs["lambda_q2"] = nrm((DEPTH, ATTN_QK_DIM), 0.1)
    inputs["lambda_k2"] = nrm((DEPTH, ATTN_QK_DIM), 0.1)
    inputs["subln_g"] = gain((DEPTH, ATTN_V_DIM))
    inputs["filt_w1"] = nrm((DEPTH, HYENA_EMB_DIM, FILTER_WIDTH), HYENA_EMB_DIM ** -0.5)
    inputs["filt_b1"] = nrm((DEPTH, FILTER_WIDTH), 0.1)
    inputs["filt_w2"] = nrm((DEPTH, FILTER_WIDTH, FILTER_WIDTH), FILTER_WIDTH ** -0.5)
    inputs["filt_b2"] = nrm((DEPTH, FILTER_WIDTH), 0.1)
    inputs["filt_w3"] = nrm((DEPTH, FILTER_WIDTH, FILTER_WIDTH), FILTER_WIDTH ** -0.5)
    inputs["filt_b3"] = nrm((DEPTH, FILTER_WIDTH), 0.1)
    inputs["filt_w4"] = nrm((DEPTH, FILTER_WIDTH, 2 * C), 0.1 * FILTER_WIDTH ** -0.5)
    inputs["filt_freq"] = 1.0 + nrm((DEPTH, FILTER_WIDTH), 0.1)
    inputs["hyena_bias"] = nrm((DEPTH, C), 0.1)
    inputs["w_attn_out"] = nrm((DEPTH, ATTN_WIDTH, D_MODEL), ATTN_WIDTH ** -0.5)
    inputs["w_hyena_out"] = nrm((DEPTH, C, D_MODEL), C ** -0.5)
    inputs["w_out"] = nrm((DEPTH, D_MODEL, D_MODEL), D_MODEL ** -0.5)
    inputs["norm2"] = gain((DEPTH, D_MODEL))
    inputs["w_up"] = nrm((DEPTH, D_MODEL, 2 * D_FF), D_MODEL ** -0.5)
    inputs["ffn_conv_w"] = nrm((DEPTH, SHORT_CONV, 2 * D_FF), 0.5)
    inputs["ffn_conv_b"] = nrm((DEPTH, 2 * D_FF), 0.02)
    inputs["w_down"] = nrm((DEPTH, D_FF, D_MODEL), D_FF ** -0.5)
    inputs["norm_f"] = gain((D_MODEL,))
    return inputs


def reference(x_prompt, x_sample, norm1, w_in, in_conv_w, in_conv_b, gate_b,
              lambda_q1, lambda_k1, lambda_q2, lambda_k2, subln_g,
              filt_w1, filt_b1, filt_w2, filt_b2, filt_w3, filt_b3, filt_w4, filt_freq,
              hyena_bias, w_attn_out, w_hyena_out, w_out,
              norm2, w_up, ffn_conv_w, ffn_conv_b, w_down, norm_f):
    params = (norm1, w_in, in_conv_w, in_conv_b, gate_b,
              lambda_q1, lambda_k1, lambda_q2, lambda_k2, subln_g,
              filt_w1, filt_b1, filt_w2, filt_b2, filt_w3, filt_b3, filt_w4, filt_freq,
              hyena_bias, w_attn_out, w_hyena_out, w_out,
              norm2, w_up, ffn_conv_w, ffn_conv_b, w_down)

    def trunk(x):
        for i in range(DEPTH):
            lambda_init = 0.8 - 0.6 * math.exp(-0.3 * i)
            x = encoder_layer(x, lambda_init, *[p[i] for p in params])
        return rmsnorm(x, norm_f)

    y_prompt = trunk(x_prompt)
    y_sample = trunk(x_sample)
    return (y_prompt, y_sample)
```

```python
import math
from contextlib import ExitStack

import numpy as np
import ml_dtypes

import concourse.bass as bass
import concourse.mybir as mybir
from concourse.bass_utils import run_bass_kernel_spmd

F32 = mybir.dt.float32
BF16 = mybir.dt.bfloat16
AF = mybir.ActivationFunctionType
ALU = mybir.AluOpType
NPBF = ml_dtypes.bfloat16

D = 2048
NH = 8
CH = 1024
DFF = 5632
INW = 10240
EPS = 1e-6
LAMBDA_INIT = 0.8 - 0.6 * math.exp(0.0)
QW = 510


class Tok:
    __slots__ = ("w", "r", "excl")

    def __init__(self, excl=False):
        self.w = {}
        self.r = {}
        self.excl = excl


class Prog:
    def __init__(self, nc):
        self.nc = nc
        self.engs = {"pe": nc.tensor, "act": nc.scalar, "dve": nc.vector,
                     "pool": nc.gpsimd, "sp": nc.sync}
        self.sems = {}
        self.cnt = {}
        for e in self.engs:
            self.sems[e] = nc.semaphore("s_" + e).__enter__()
            self.cnt[e] = 0
        self.NS = 12
        for q in ("sp", "pool"):
            for i in range(self.NS):
                k = "d%s%d" % (q, i)
                self.sems[k] = nc.semaphore(k).__enter__()
                self.cnt[k] = 0
        self.rr = {"sp": 0, "pool": 0}
        self.seen = {e: {} for e in self.engs}
        self.uid = 0
        self.psi = -1

    def nextps(self):
        self.psi = (self.psi + 1) % 7
        return self.psi

    def name(self, base="t"):
        self.uid += 1
        return "%s%d" % (base, self.uid)

    def _wait(self, e, deps):
        for k, v in deps.items():
            if self.seen[e].get(k, 0) < v:
                self.engs[e].wait_ge(self.sems[k], v)
                self.seen[e][k] = v

    @staticmethod
    def _deps(reads, writes, joins=(), eng=None):
        d = {}
        for t in reads:
            for k, v in t.w.items():
                if d.get(k, 0) < v:
                    d[k] = v
            if t.excl:
                for k, v in t.r.items():
                    if k != eng and d.get(k, 0) < v:
                        d[k] = v
        for t in writes:
            for m in (t.w, t.r):
                for k, v in m.items():
                    if d.get(k, 0) < v:
                        d[k] = v
        for t in joins:
            for k, v in t.r.items():
                if d.get(k, 0) < v:
                    d[k] = v
        if eng == "pe":
            d.pop("pe", None)
        return d

    @staticmethod
    def _commit(reads, writes, k, v, joins=()):
        for t in reads:
            if t.r.get(k, 0) < v:
                t.r[k] = v
        for t in writes:
            t.w = {k: v}
            t.r = {}
        for t in joins:
            if t.w.get(k, 0) < v:
                t.w[k] = v

    def op(self, e, fn, reads=(), writes=(), joins=()):
        self._wait(e, self._deps(reads, writes, joins, e))
        ins = fn(self.engs[e])
        self.cnt[e] += 1
        ins.then_inc(self.sems[e], 1)
        self._commit(reads, writes, e, self.cnt[e], joins)

    def dma(self, q, out, in_, reads=(), writes=(), joins=()):
        i = self.rr[q]
        self.rr[q] = (i + 1) % self.NS
        k = "d%s%d" % (q, i)
        deps = self._deps(reads, writes, joins)
        if self.cnt[k] > 0:
            deps[k] = max(deps.get(k, 0), self.cnt[k])
        self._wait(q, deps)
        self.engs[q].dma_start(out=out, in_=in_, allow_slow_non_contiguous=True).then_inc(self.sems[k], 16)
        self.cnt[k] += 16
        self._commit(reads, writes, k, self.cnt[k], joins)

    def barrier(self):
        allv = {k: v for k, v in self.cnt.items() if v > 0}
        for e in self.engs:
            self._wait(e, allv)


class Ring:
    def __init__(self, bufs):
        self.bufs = bufs
        self.toks = [Tok() for _ in bufs]
        self.i = -1

    def next(self):
        self.i = (self.i + 1) % len(self.bufs)
        return self.bufs[self.i], self.toks[self.i]


class Job:
    pass


def build(Ls):
    nc = bass.Bass("TRN2", target_bir_lowering=False)
    p = Prog(nc)

    def din(name, shape, dt=F32):
        return nc.dram_tensor(name, list(shape), dt, kind="ExternalInput").ap()

    def dscr(name, shape, dt=BF16):
        return nc.dram_tensor(name, list(shape), dt).ap()

    w_in_f = din("w_in", [D, INW])
    w_ao_f = din("w_ao", [CH, D])
    w_ho_f = din("w_ho", [CH, D])
    w_out_f = din("w_out", [D, D])
    w_up_f = din("w_up", [D, 2 * DFF])
    w_dn_f = din("w_dn", [DFF, D])
    norm1_d = din("norm1", [128, 16])
    norm2_d = din("norm2", [128, 16])
    normf_d = din("normf", [128, D])
    gateb_d = din("gateb", [128, 32])
    icw_d = din("icw", [128, 24, 3])
    icb_d = din("icb", [128, 24])
    fcw_d = din("fcw", [128, 88, 3])
    fcb_d = din("fcb", [128, 88])
    subg_d = din("subg", [128, 1])
    hbias_d = din("hbias", [128, CH])
    hbiasp_d = din("hbiasp", [128, 8])
    lamv_d = din("lamv", [64, 4])
    fw1_d = din("fw1", [33, 64])
    fw2_d = din("fw2", [64, 64])
    fw3_d = din("fw3", [64, 64])
    fw4_d = din("fw4", [64, 2 * CH])
    fvec_d = din("fvec", [64, 4])
    ndelta_d = din("ndelta", [128, 8])
    ident_d = din("ident", [128, 128], BF16)
    rotm_d = din("rotm", [128, 128], BF16)
    flags_d = din("flags", [128, 2])

    w_in = dscr("w_in_b", [D, INW])
    w_ao = dscr("w_ao_b", [CH, D])
    w_ho = dscr("w_ho_b", [CH, D])
    w_out = dscr("w_out_b", [D, D])
    w_up = dscr("w_up_b", [D, 2 * DFF])
    w_dn = dscr("w_dn_b", [DFF, D])

    jobs = []
    for ji, L in enumerate(Ls):
        J = Job()
        J.i = ji
        J.L = L
        J.Lq = L // 2
        J.NQ = J.Lq + 2
        J.N2 = L // 64
        N2 = J.N2
        s = "_%d" % ji
        J.x = din("x" + s, [L, D])
        J.cos = din("cos" + s, [128, L])
        J.sin = din("sin" + s, [128, L])
        J.zT = din("zT" + s, [33, 2 * L])
        J.trow = din("trow" + s, [128, 2 * L])
        J.F1u = din("F1u" + s, [64, 256], BF16)
        J.F1f = din("F1f" + s, [128, 256], BF16)
        J.F1ub = din("F1ub" + s, [64, 256], BF16)
        J.F1fb = din("F1fb" + s, [128, 256], BF16)
        J.G = din("G" + s, [128, N2, 3, N2], BF16)
        J.T1 = din("T1" + s, [N2, 4 * N2], BF16)
        J.Ginv = din("Ginv" + s, [128, N2, 2, 34], BF16)
        J.y = nc.dram_tensor("y" + s, [J.Lq, D], F32, kind="ExternalOutput").ap()
        J.qT = dscr("qT" + s, [NH, 128, J.NQ])
        J.kT = dscr("kT" + s, [NH, 128, L])
        J.V = dscr("V" + s, [L, CH])
        J.xh = dscr("xh" + s, [3 * CH, L])
        J.gates = dscr("gates" + s, [2 * D, J.NQ])
        J.hT = dscr("hT" + s, [CH, 2 * L])
        J.Fs = dscr("Fs" + s, [CH // 64, N2, 128, 2, 64])
        J.uT = dscr("uT" + s, [CH, L])
        J.x0c = dscr("x0c" + s, [CH, J.NQ])
        J.hy = dscr("hy" + s, [CH, J.NQ])
        J.attn = dscr("attn" + s, [CH, J.NQ])
        jobs.append(J)

    def galloc(shape, dt):
        return nc.alloc_sbuf_tensor(p.name("g"), list(shape), dt)

    ident = galloc([128, 128], BF16)
    rotm = galloc([128, 128], BF16)
    ones_bf = galloc([128, 128], BF16)
    ones_f = galloc([128, 128], F32)
    norm1 = galloc([128, 16], F32)
    norm2 = galloc([128, 16], F32)
    gateb = galloc([128, 32], F32)
    icw = galloc([128, 24, 3], F32)
    icb = galloc([128, 24], F32)
    fcw = galloc([128, 88, 3], F32)
    fcb = galloc([128, 88], F32)
    subg = galloc([128, 1], F32)
    flags = galloc([128, 2], F32)
    lamv = galloc([64, 4], F32)
    lam = galloc([128, 4], F32)
    cst = Tok()

    psall = nc.alloc_psum_tensor("psall", [128, 4096], F32)
    ps = [psall[:, i * 512:(i + 1) * 512] for i in range(8)]
    pst = [Tok(excl=True) for _ in range(8)]
    psb = ps[7].bitcast(BF16)
    psbt = pst[7]

    for dst, src in ((ident, ident_d), (rotm, rotm_d), (norm1, norm1_d), (norm2, norm2_d),
                     (gateb, gateb_d), (icb, icb_d), (fcb, fcb_d), (subg, subg_d),
                     (flags, flags_d), (lamv, lamv_d)):
        p.dma("sp", out=dst[:], in_=src[:, :], writes=[cst])
    p.dma("sp", out=icw[:], in_=icw_d[:, :, :], writes=[cst])
    p.dma("sp", out=fcw[:], in_=fcw_d[:, :, :], writes=[cst])
    p.op("dve", lambda e: e.memset(ones_bf[:], 1.0), writes=[cst])
    p.op("dve", lambda e: e.memset(ones_f[:], 1.0), writes=[cst])
    p.barrier()

    prod = galloc([64, 2], F32)
    p.op("dve", lambda e: e.tensor_tensor(out=prod[:, 0:1], in0=lamv[:, 0:1], in1=lamv[:, 1:2], op=ALU.mult), writes=[cst])
    p.op("dve", lambda e: e.tensor_tensor(out=prod[:, 1:2], in0=lamv[:, 2:3], in1=lamv[:, 3:4], op=ALU.mult), writes=[cst])
    p.barrier()
    p.op("pe", lambda e: e.matmul(ps[0][:, 0:2], ones_f[0:64, :], prod[:, :], start=True, stop=True), writes=[pst[0]])
    p.op("act", lambda e: e.activation(out=lam[:, 1:3], in_=ps[0][:, 0:2], func=AF.Exp), reads=[pst[0]], writes=[cst])
    p.barrier()
    p.op("dve", lambda e: e.scalar_tensor_tensor(out=lam[:, 0:1], in0=lam[:, 2:3], scalar=-LAMBDA_INIT,
                                                 in1=lam[:, 1:2], op0=ALU.add, op1=ALU.subtract), writes=[cst])
    p.barrier()

    G = Job()
    G.nc = nc
    G.p = p
    G.ps, G.pst, G.psb, G.psbt, G.psall = ps, pst, psb, psbt, psall
    G.ident, G.rotm, G.ones_bf, G.ones_f = ident, rotm, ones_bf, ones_f
    G.norm1, G.norm2, G.gateb, G.icw, G.icb, G.fcw, G.fcb = norm1, norm2, gateb, icw, icb, fcw, fcb
    G.subg, G.flags, G.lam, G.cst = subg, flags, lam, cst
    G.w_in, G.w_ao, G.w_ho, G.w_out, G.w_up, G.w_dn = w_in, w_ao, w_ho, w_out, w_up, w_dn
    G.normf_d, G.hbias_d = normf_d, hbias_d
    G.hbiasp_d = hbiasp_d
    G.fw = (fw1_d, fw2_d, fw3_d, fw4_d, fvec_d, ndelta_d)

    phase_W(G, [(w_in_f, w_in, D, INW, norm1), (w_ao_f, w_ao, CH, D, None), (w_ho_f, w_ho, CH, D, None),
                (w_out_f, w_out, D, D, None), (w_up_f, w_up, D, 2 * DFF, norm2), (w_dn_f, w_dn, DFF, D, None)])
    p.barrier()
    import os as _os
    _ph = _os.environ.get("KPH", "FfABbCD")
    for J in jobs:
        for letter, fn in (("F", lambda: phase_F(G, J)), ("f", lambda: phase_fft(G, J, filt=True)), ("A", lambda: phase_A(G, J)),
                           ("B", lambda: phase_B1(G, J)), ("b", lambda: phase_fft(G, J, filt=False)), ("C", lambda: phase_C(G, J)),
                           ("D", lambda: phase_D(G, J))):
            if letter in _ph:
                fn()
                p.barrier()
    return nc


def mk_alloc(G, es):
    def alloc(shape, dt):
        return es.enter_context(G.nc.sbuf_tensor(G.p.name("s"), list(shape), dt))
    return alloc


def mk_ring(alloc, n, shape, dt):
    return Ring([alloc(shape, dt) for _ in range(n)])


def phase_W(G, items):
    p = G.p
    CW = 2048
    with ExitStack() as es:
        A = mk_alloc(G, es)
        rf = mk_ring(A, 3, [128, CW], F32)
        rb = mk_ring(A, 3, [128, CW], BF16)
        n = 0
        for src, dst, K, N, gain in items:
            for kc in range(K // 128):
                for c0 in range(0, N, CW):
                    cw = min(CW, N - c0)
                    wf, tf = rf.next()
                    wb, tb = rb.next()
                    p.dma("sp", out=wf[:, :cw], in_=src[kc * 128:(kc + 1) * 128, c0:c0 + cw], writes=[tf])
                    eng = "act" if n % 2 == 0 else "dve"
                    n += 1
                    if gain is None:
                        if eng == "act":
                            p.op("act", lambda e: e.activation(out=wb[:, :cw], in_=wf[:, :cw], func=AF.Copy), [tf], [tb])
                        else:
                            p.op("dve", lambda e: e.tensor_copy(out=wb[:, :cw], in_=wf[:, :cw]), [tf], [tb])
                    else:
                        g = gain[:, kc:kc + 1]
                        if eng == "act":
                            p.op("act", lambda e: e.activation(out=wb[:, :cw], in_=wf[:, :cw], func=AF.Copy, scale=g), [tf, G.cst], [tb])
                        else:
                            p.op("dve", lambda e: e.tensor_scalar(out=wb[:, :cw], in0=wf[:, :cw], scalar1=g, scalar2=None, op0=ALU.mult), [tf, G.cst], [tb])
                    p.dma("pool", out=dst[kc * 128:(kc + 1) * 128, c0:c0 + cw], in_=wb[:, :cw], reads=[tb])


def rstd_ops(p, src, tmp, dst, tok, scale, extra_reads=()):
    p.op("dve", lambda e: e.tensor_scalar(out=tmp, in0=src, scalar1=scale, scalar2=EPS, op0=ALU.mult, op1=ALU.add),
         list(extra_reads) + [tok], [tok])
    p.op("act", lambda e: e.activation(out=tmp, in_=tmp, func=AF.Sqrt), [tok], [tok])
    p.op("dve", lambda e: e.reciprocal(out=dst, in_=tmp), [tok], [tok])


def copy_op(p, eng, out, in_, reads, writes=(), joins=()):
    if eng == "act":
        p.op("act", lambda e: e.activation(out=out, in_=in_, func=AF.Copy), reads, writes, joins)
    else:
        p.op(eng, lambda e: e.tensor_copy(out=out, in_=in_), reads, writes, joins)


def tt_op(p, eng, out, in0, in1, op, reads, writes=(), joins=()):
    p.op(eng, lambda e: e.tensor_tensor(out=out, in0=in0, in1=in1, op=op), reads, writes, joins)


def phase_F(G, J):
    p = G.p
    L = J.L
    fw1_d, fw2_d, fw3_d, fw4_d, fvec_d, ndelta_d = G.fw
    ps, pst = G.ps, G.pst
    with ExitStack() as es:
        A = mk_alloc(G, es)
        w1 = A([33, 64], F32)
        w2 = A([64, 64], F32)
        w3 = A([64, 64], F32)
        w4 = A([64, 2 * CH], F32)
        fv = A([64, 8], F32)
        nd = A([128, 8], F32)
        ct = Tok()
        for dst, src in ((w1, fw1_d), (w2, fw2_d), (w3, fw3_d), (w4, fw4_d), (nd, ndelta_d)):
            p.dma("sp", out=dst[:], in_=src[:, :], writes=[ct])
        p.dma("sp", out=fv[:, 0:4], in_=fvec_d[:, :], writes=[ct])
        for i in range(3):
            tt_op(p, "dve", fv[:, 4 + i:5 + i], fv[:, i:i + 1], fv[:, 3:4], ALU.mult, [ct], [ct])
        zr = mk_ring(A, 2, [33, 512], F32)
        tr = mk_ring(A, 2, [128, 512], F32)
        pre = mk_ring(A, 2, [64, 512], F32)
        sa = mk_ring(A, 2, [64, 512], F32)
        sb = mk_ring(A, 2, [64, 512], F32)
        hr = mk_ring(A, 3, [64, 512], F32)
        dec = mk_ring(A, 2, [128, 512], F32)
        ob = mk_ring(A, 3, [128, 512], BF16)
        wl = (w1, w2, w3)
        hbp = A([128, 8], F32)
        p.dma("sp", out=hbp[:], in_=G.hbiasp_d[:, :], writes=[ct])
        for c in range(2 * L // 512):
            z, zt = zr.next()
            p.dma("sp", out=z[:], in_=J.zT[:, c * 512:(c + 1) * 512], writes=[zt])
            trw, trt = tr.next()
            p.dma("sp", out=trw[:], in_=J.trow[:, c * 512:(c + 1) * 512], writes=[trt])
            hin, hint = z, zt
            kdim = 33
            for li in range(3):
                pi = p.nextps()
                w = wl[li]
                p.op("pe", lambda e: e.matmul(ps[pi][0:64, :], w[0:kdim, :], hin[0:kdim, :], start=True, stop=True),
                     [ct, hint], [pst[pi]])
                pr, prt = pre.next()
                p.op("dve", lambda e: e.tensor_scalar(out=pr[:], in0=ps[pi][0:64, :], scalar1=fv[:, 3:4], scalar2=fv[:, 4 + li:5 + li],
                                                      op0=ALU.mult, op1=ALU.add), [pst[pi], ct], [prt])
                a, at = sa.next()
                b, bt = sb.next()
                p.op("act", lambda e: e.activation(out=a[:], in_=pr[:], func=AF.Sin, scale=0.25), [prt], [at])
                p.op("act", lambda e: e.activation(out=b[:], in_=pr[:], func=AF.Sin, scale=0.125), [prt], [bt])
                tt_op(p, "dve", b[:], b[:], b[:], ALU.mult, [bt], [bt])
                p.op("dve", lambda e: e.tensor_scalar(out=b[:], in0=b[:], scalar1=-2.0, scalar2=1.0, op0=ALU.mult, op1=ALU.add), [bt], [bt])
                tt_op(p, "dve", b[:], b[:], a[:], ALU.mult, [bt, at], [bt])
                tt_op(p, "dve", a[:], a[:], a[:], ALU.mult, [at, bt], [at])
                p.op("dve", lambda e: e.tensor_scalar(out=a[:], in0=a[:], scalar1=-2.0, scalar2=1.0, op0=ALU.mult, op1=ALU.add), [at], [at])
                h, ht = hr.next()
                p.op("dve", lambda e: e.scalar_tensor_tensor(out=h[:], in0=b[:], scalar=4.0, in1=a[:], op0=ALU.mult, op1=ALU.mult),
                     [at, bt], [ht])
                hin, hint = h, ht
                kdim = 64
            bwd = (c >= L // 512)
            for blk in (range(8, 16) if bwd else range(8)):
                pi = p.nextps()
                p.op("pe", lambda e: e.matmul(ps[pi][:, :], w4[:, blk * 128:(blk + 1) * 128], hin[:, :], start=True, stop=True),
                     [ct, hint], [pst[pi]])
                d, dt_ = dec.next()
                p.op("act", lambda e: e.activation(out=d[:], in_=trw[:], func=AF.Exp, scale=nd[:, blk % 8:blk % 8 + 1]), [trt, ct], [dt_])
                o, ot = ob.next()
                tt_op(p, "dve", o[:], ps[pi][:, :], d[:], ALU.mult, [pst[pi], dt_], [ot])
                if bwd and c == L // 512:
                    p.op("dve", lambda e: e.memset(o[:, 0:1], 0.0), [], [ot])
                if c == 0:
                    p.op("dve", lambda e: e.scalar_tensor_tensor(out=o[:, 0:1], in0=ps[pi][:, 0:1], scalar=d[:, 0:1], in1=hbp[:, blk:blk + 1],
                                                                 op0=ALU.mult, op1=ALU.add), [pst[pi], dt_, ct], [ot])
                p.dma("pool", out=J.hT[(blk % 8) * 128:(blk % 8 + 1) * 128, c * 512:(c + 1) * 512], in_=o[:], reads=[ot])


def phase_fft(G, J, filt):
    p = G.p
    N2, Lq, NQ = J.N2, J.Lq, J.NQ
    ps, pst = G.ps, G.pst
    GC = 64
    KB = 512 // GC
    src = J.hT if filt else J.uT
    ngrp = CH // GC
    KR = 128 if filt else 64
    psall = G.psall
    with ExitStack() as es:
        A = mk_alloc(G, es)
        F1 = A([KR, 256], BF16)
        F1b = A([KR, 256], BF16)
        ct = Tok()
        p.dma("sp", out=F1[:], in_=(J.F1f if filt else J.F1u)[:, :], writes=[ct])
        p.dma("sp", out=F1b[:], in_=(J.F1fb if filt else J.F1ub)[:, :], writes=[ct])
        xr = mk_ring(A, 2, [KR, 32, N2], BF16)
        AC = A([128, GC * 2 * 128], BF16)
        AC2 = A([128, GC * 2 * 128], BF16)
        act_ = Tok()
        A_sb = AC[0:N2, :].rearrange("p (c r k) -> p c r k", c=GC, r=2)
        A_sb2 = AC2[0:N2, :].rearrange("p (c r k) -> p c r k", c=GC, r=2)
        gr = mk_ring(A, 3, [N2, KB, 2 * N2], BF16)
        obr = mk_ring(A, 3, [N2, KB, 2, GC], BF16)
        if not filt:
            C_sb = AC[:, 0:2 * GC * N2].rearrange("p (r c n) -> p r c n", r=2, c=GC)
            Y_sb = A([N2, 2, GC, 128], BF16)
            yt = Tok()
            T1 = A([N2, 4 * N2], BF16)
            Gi = A([128, N2, 2, 34], BF16)
            p.dma("sp", out=T1[:], in_=J.T1[:, :], writes=[ct])
            p.dma("sp", out=Gi[:], in_=J.Ginv[:, :, :, :], writes=[ct])
            fr = [mk_ring(A, 2, [N2, KB, 2, GC], BF16) for _ in range(2)]
            tf = [mk_ring(A, 2, [N2, KB, GC], F32) for _ in range(4)]
            bias = A([N2, KB, GC], F32)
            bt = Tok()
            x0r = mk_ring(A, 1, [GC, NQ], BF16)
            hyr = mk_ring(A, 1, [GC, NQ], BF16)
        s1 = 0
        s3 = 0
        for g in range(ngrp):
            ch0 = g * GC
            for half in range(GC // 32):
                x, xt = xr.next()
                p.dma("sp", out=x[:], in_=src[ch0 + half * 32:ch0 + half * 32 + 32, :].rearrange("c (a b) -> a c b", a=KR), writes=[xt])
                for c2 in range(16):
                    b1 = 4 + 2 * (s1 % 2)
                    s1 += 1

                    def mm(e):
                        for u in range(2):
                            e.matmul(ps[b1][0:N2, u * 256:(u + 1) * 256], x[:, c2 * 2 + u, :], F1[:, :], start=True, stop=True)
                        for u in range(2):
                            ins = e.matmul(ps[b1 + 1][0:N2, u * 256:(u + 1) * 256], x[:, c2 * 2 + u, :], F1b[:, :], start=True, stop=True)
                        return ins
                    p.op("pe", mm, [xt, ct], [pst[b1], pst[b1 + 1]])
                    cl = half * 32 + c2 * 2
                    copy_op(p, "act", A_sb[:, cl:cl + 2, :, :],
                            ps[b1][0:N2, :].rearrange("p (c r k) -> p c r k", c=2, r=2), [pst[b1]], joins=[act_])
                    copy_op(p, "dve", A_sb2[:, cl:cl + 2, :, :],
                            ps[b1 + 1][0:N2, :].rearrange("p (c r k) -> p c r k", c=2, r=2), [pst[b1 + 1]], joins=[act_])
            for kb in range(128 // KB):
                gt, gtt = gr.next()
                p.dma("sp", out=gt[:].rearrange("n k (t m) -> n k t m", t=2),
                      in_=J.G[kb * KB:(kb + 1) * KB, :, 0:2, :].rearrange("k n t m -> n k t m"), writes=[gtt])
                b3 = 2 * (s3 % 2)
                s3 += 1

                def mm3(e):
                    for kk in range(KB):
                        k1 = kb * KB + kk
                        o = psall[0:N2, b3 * 512 + kk * 2 * GC:b3 * 512 + (kk + 1) * 2 * GC]
                        e.matmul(o, gt[:, kk, 0:N2], A_sb[:, :, :, k1].rearrange("p c r -> p r c"), start=True, stop=False)
                        ins = e.matmul(o, gt[:, kk, N2:2 * N2], A_sb2[:, :, :, k1].rearrange("p c r -> p r c"), start=False, stop=True)
                    return ins
                p.op("pe", mm3, [gtt, act_], [pst[b3], pst[b3 + 1]])
                X = psall[0:N2, b3 * 512:(b3 + 2) * 512].rearrange("p (k r c) -> p k r c", k=KB, r=2)
                xr_v = X[:, :, 0, :]
                xi_v = X[:, :, 1, :]
                xtoks = [pst[b3], pst[b3 + 1]]
                k1a = kb * KB
                if filt:
                    o, ot = obr.next()
                    copy_op(p, "act" if kb % 2 == 0 else "dve", o[:], X, xtoks, [ot])
                    p.dma("pool", out=J.Fs[g, :, k1a:k1a + KB, :, :], in_=o[:], reads=[ot])
                else:
                    ff, fft_ = fr[0].next()
                    p.dma("sp", out=ff[:], in_=J.Fs[g, :, k1a:k1a + KB, :, :], writes=[fft_])
                    hr_, hi_ = ff[:, :, 0, :], ff[:, :, 1, :]
                    hrt = hit = fft_
                    t1, t1t = tf[2].next()
                    t2, t2t = tf[3].next()
                    yr_o = Y_sb[:, 0, :, k1a:k1a + KB].rearrange("p c k -> p k c")
                    yi_o = Y_sb[:, 1, :, k1a:k1a + KB].rearrange("p c k -> p k c")
                    tt_op(p, "dve", t1[:], xr_v, hr_, ALU.mult, xtoks + [hrt], [t1t])
                    tt_op(p, "dve", t2[:], xi_v, hi_, ALU.mult, xtoks + [hit], [t2t])
                    tt_op(p, "pool", yr_o, t1[:], t2[:], ALU.subtract, [t1t, t2t], joins=[yt])
                    t1, t1t = tf[2].next()
                    t2, t2t = tf[3].next()
                    tt_op(p, "dve", t1[:], xr_v, hi_, ALU.mult, xtoks + [hit], [t1t])
                    tt_op(p, "dve", t2[:], xi_v, hr_, ALU.mult, xtoks + [hrt], [t2t])
                    tt_op(p, "pool", yi_o, t1[:], t2[:], ALU.add, [t1t, t2t], joins=[yt])
            if filt:
                continue
            cpb = 512 // (2 * N2)
            for c0 in range(0, GC, cpb):
                pi = p.nextps()

                def mmi(e):
                    for u in range(cpb):
                        c = c0 + u
                        e.matmul(ps[pi][:, u * 2 * N2:(u + 1) * 2 * N2], Y_sb[:, 0, c, :], T1[:, 0:2 * N2], start=True, stop=False)
                        ins = e.matmul(ps[pi][:, u * 2 * N2:(u + 1) * 2 * N2], Y_sb[:, 1, c, :], T1[:, 2 * N2:4 * N2], start=False, stop=True)
                    return ins
                p.op("pe", mmi, [yt, ct, act_], [pst[pi]])
                copy_op(p, "act" if (c0 // cpb) % 2 == 0 else "dve", C_sb[:, :, c0:c0 + cpb, :],
                        ps[pi][:, 0:cpb * 2 * N2].rearrange("p (c r n) -> p r c n", c=cpb, r=2), [pst[pi]], joins=[act_])
            x0, x0t = x0r.next()
            hy, hyt = hyr.next()
            p.dma("sp", out=x0[:], in_=J.x0c[ch0:ch0 + GC, :], writes=[x0t])
            own_o = hy[:, 1:1 + Lq].rearrange("p (a b) -> p a b", b=N2)
            own_x = x0[:, 1:1 + Lq].rearrange("p (a b) -> p a b", b=N2)
            first = True
            for nb in range(0, N2, 16):
                pi = p.nextps()

                def mm2(e):
                    for u in range(16):
                        n2 = nb + u
                        e.matmul(ps[pi][0:GC, u * 32:(u + 1) * 32], C_sb[:, 0, :, n2], Gi[:, n2, 0, 1:33], start=True, stop=False)
                        ins = e.matmul(ps[pi][0:GC, u * 32:(u + 1) * 32], C_sb[:, 1, :, n2], Gi[:, n2, 1, 1:33], start=False, stop=True)
                    return ins
                p.op("pe", mm2, [act_, ct], [pst[pi]])
                w_ = [hyt] if first else []
                j_ = [] if first else [hyt]
                first = False
                tt_op(p, "dve", own_o[:, :, nb:nb + 16], ps[pi][0:GC, :].rearrange("p (b a) -> p a b", a=32),
                      own_x[:, :, nb:nb + 16], ALU.mult, [pst[pi], x0t], w_, j_)
            pi = p.nextps()

            def mmh(e):
                e.matmul(ps[pi][0:GC, 0:1], C_sb[:, 0, :, N2 - 1], Gi[:, N2 - 1, 0, 0:1], start=True, stop=False)
                e.matmul(ps[pi][0:GC, 0:1], C_sb[:, 1, :, N2 - 1], Gi[:, N2 - 1, 1, 0:1], start=False, stop=True)
                e.matmul(ps[pi][0:GC, 1:2], C_sb[:, 0, :, 0], Gi[:, 0, 0, 33:34], start=True, stop=False)
                return e.matmul(ps[pi][0:GC, 1:2], C_sb[:, 1, :, 0], Gi[:, 0, 1, 33:34], start=False, stop=True)
            p.op("pe", mmh, [act_, ct], [pst[pi]])
            tt_op(p, "dve", hy[:, 0:1], ps[pi][0:GC, 0:1], x0[:, 0:1], ALU.mult, [pst[pi], x0t], joins=[hyt])
            tt_op(p, "dve", hy[:, NQ - 1:NQ], ps[pi][0:GC, 1:2], x0[:, NQ - 1:NQ], ALU.mult, [pst[pi], x0t], joins=[hyt])
            p.dma("pool", out=J.hy[ch0:ch0 + GC, :], in_=hy[:], reads=[hyt])


def phase_A(G, J):
    p = G.p
    L, Lq, NQ = J.L, J.Lq, J.NQ
    ps, pst = G.ps, G.pst
    nchunks = L // 512
    with ExitStack() as es:
        A = mk_alloc(G, es)
        xr = mk_ring(A, 2, [128, D], F32)
        junk = mk_ring(A, 2, [128, D], BF16)
        hnr = mk_ring(A, 2, [128, D], BF16)
        st = mk_ring(A, 4, [128, 4], F32)
        hnT = [A([128, 16, 512], BF16) for _ in range(2)]
        hnT_t = [[Tok() for _ in range(8)] for _ in range(2)]
        wr = mk_ring(A, 2, [128, 16, 512], BF16)
        cosr = mk_ring(A, 2, [128, 512], F32)
        sinr = mk_ring(A, 2, [128, 512], F32)
        qraw = mk_ring(A, 2, [128, 512], BF16)
        t1r = mk_ring(A, 2, [128, 512], F32)
        t2r = mk_ring(A, 2, [128, 512], F32)
        osb = mk_ring(A, 4, [128, 512], BF16)
        for c in range(nchunks):
            s = c % 2
            own = (c * 512 < Lq)
            for i in range(4):
                r0 = c * 512 + i * 128
                xt, xtt = xr.next()
                p.dma("sp", out=xt[:], in_=J.x[r0:r0 + 128, :], writes=[xtt])
                jk, jkt = junk.next()
                sv, svt = st.next()
                p.op("dve", lambda e: e.memset(sv[:, 0:1], 0.0), [], [svt])
                p.op("act", lambda e: e.activation(out=jk[:], in_=xt[:], func=AF.Square, accum_out=sv[:, 0:1]), [xtt], [jkt, svt])
                rstd_ops(p, sv[:, 0:1], sv[:, 1:2], sv[:, 2:3], svt, 1.0 / D)
                hn, hnt = hnr.next()
                p.op("act", lambda e: e.activation(out=hn[:], in_=xt[:], func=AF.Copy, scale=sv[:, 2:3]), [xtt, svt], [hnt])
                for half in range(2):
                    def tr(e):
                        for k in range(8):
                            kc = half * 8 + k
                            ins = e.transpose(G.psb[:, k * 128:(k + 1) * 128], hn[:, kc * 128:(kc + 1) * 128], G.ident[:])
                        return ins
                    p.op("pe", tr, [hnt, G.cst], [G.psbt])
                    copy_op(p, "dve" if half == 0 else "act", hnT[s][:, half * 8:(half + 1) * 8, i * 128:(i + 1) * 128],
                            G.psb[:, :].rearrange("p (k t) -> p k t", k=8), [G.psbt], [hnT_t[s][i * 2 + half]])
            cs, cst_ = cosr.next()
            sn, snt = sinr.next()
            p.dma("sp", out=cs[:], in_=J.cos[:, c * 512:(c + 1) * 512], writes=[cst_])
            p.dma("sp", out=sn[:], in_=J.sin[:, c * 512:(c + 1) * 512], writes=[snt])

            import os as _os
            _ka = _os.environ.get("KA", "")

            def qstore(dst_fn, ob, obt):
                if "c" in _ka:
                    if own:
                        p.dma("pool", out=dst_fn(1 + c * 512, 1 + (c + 1) * 512), in_=ob[:], reads=[obt])
                    return
                if own:
                    p.dma("pool", out=dst_fn(1 + c * 512, 1 + (c + 1) * 512), in_=ob[:], reads=[obt])
                if c * 512 == Lq:
                    p.dma("pool", out=dst_fn(NQ - 1, NQ), in_=ob[:, 0:1], reads=[obt])
                if c == nchunks - 1:
                    p.dma("pool", out=dst_fn(0, 1), in_=ob[:, 511:512], reads=[obt])

            first_other = (c * 512 == Lq)
            last_chunk = (c == nchunks - 1)
            need_q = own or first_other or last_chunk
            for b in range(20):
                qblk = (b < 2 or b >= 12 or b in (6, 7))
                if not need_q and qblk:
                    continue
                if ("r" in _ka and b < 4) or ("v" in _ka and b in (4, 5)) or ("h" in _ka and 6 <= b < 12) or ("g" in _ka and b >= 12):
                    continue
                lo, hi = 0, 512
                if qblk and not own:
                    if first_other and not last_chunk:
                        lo, hi = 0, 2
                    elif last_chunk and not first_other:
                        lo, hi = 510, 512
                wb, wbt = wr.next()
                p.dma("sp", out=wb[:], in_=G.w_in[:, b * 512:(b + 1) * 512].rearrange("(kc q) n -> q kc n", q=128), writes=[wbt])
                if b in (4, 5):
                    for i in range(4):
                        pi = p.nextps()

                        def mmv(e):
                            for kc in range(16):
                                ins = e.matmul(ps[pi][:, :], hnT[s][:, kc, i * 128:(i + 1) * 128], wb[:, kc, :],
                                               start=(kc == 0), stop=(kc == 15))
                            return ins
                        p.op("pe", mmv, [wbt] + hnT_t[s], [pst[pi]])
                        ob, obt = osb.next()
                        copy_op(p, "act", ob[:], ps[pi][:, :], [pst[pi]], [obt])
                        r0 = c * 512 + i * 128
                        p.dma("pool", out=J.V[r0:r0 + 128, (b - 4) * 512:(b - 3) * 512], in_=ob[:], reads=[obt])
                    continue
                for j in range(4):
                    g = b * 4 + j
                    pi = p.nextps()

                    def mmf(e):
                        for kc in range(16):
                            ins = e.matmul(ps[pi][:, lo:hi], wb[:, kc, j * 128:(j + 1) * 128], hnT[s][:, kc, lo:hi],
                                           start=(kc == 0), stop=(kc == 15))
                        return ins
                    p.op("pe", mmf, [wbt] + hnT_t[s], [pst[pi]])
                    ob, obt = osb.next()
                    if b < 4:
                        head = g % 8
                        qr, qrt = qraw.next()
                        copy_op(p, "act", qr[:, lo:hi], ps[pi][:, lo:hi], [pst[pi]], [qrt])
                        pr = p.nextps()
                        p.op("pe", lambda e: e.matmul(ps[pr][:, lo:hi], G.rotm[:], qr[:, lo:hi], start=True, stop=True), [qrt, G.cst], [pst[pr]])
                        t1, t1t = t1r.next()
                        t2, t2t = t2r.next()
                        tt_op(p, "dve", t1[:, lo:hi], ps[pi][:, lo:hi], cs[:, lo:hi], ALU.mult, [pst[pi], cst_], [t1t])
                        tt_op(p, "dve", t2[:, lo:hi], ps[pr][:, lo:hi], sn[:, lo:hi], ALU.mult, [pst[pr], snt], [t2t])
                        tt_op(p, "dve" if "P" in _ka else "pool", ob[:, lo:hi], t1[:, lo:hi], t2[:, lo:hi], ALU.add, [t1t, t2t], [obt])
                        if b >= 2:
                            p.dma("pool", out=J.kT[head, :, c * 512:(c + 1) * 512], in_=ob[:], reads=[obt])
                        else:
                            qstore(lambda a, bb: J.qT[head, :, a:bb], ob, obt)
                    elif b < 12:
                        copy_op(p, "dve", ob[:, lo:hi], ps[pi][:, lo:hi], [pst[pi]], [obt])
                        r0 = (g - 24) * 128
                        p.dma("pool", out=J.xh[r0:r0 + 128, c * 512 + lo:c * 512 + hi], in_=ob[:, lo:hi], reads=[obt])
                    else:
                        gi = g - 48
                        p.op("act", lambda e: e.activation(out=ob[:, lo:hi], in_=ps[pi][:, lo:hi], func=AF.Sigmoid, bias=G.gateb[:, gi:gi + 1]),
                             [pst[pi], G.cst], [obt])
                        qstore(lambda a, bb: J.gates[gi * 128:(gi + 1) * 128, a:bb], ob, obt)


def phase_B1(G, J):
    p = G.p
    L, Lq, NQ = J.L, J.Lq, J.NQ
    SEG = min(2048, Lq)
    m_wrap = G.flags[:, 0:1]
    m_mid = G.flags[:, 1:2]
    with ExitStack() as es:
        A = mk_alloc(G, es)
        xin = mk_ring(A, 3, [128, SEG + 2], BF16)
        cv = mk_ring(A, 3, [128, SEG], F32)
        ub = mk_ring(A, 2, [128, SEG], BF16)
        hx = mk_ring(A, 2, [128, 8], BF16)
        ho = mk_ring(A, 2, [128, 4], F32)
        hb = mk_ring(A, 2, [128, 2], BF16)

        def conv_seg(row0, gi, s0, pm, nm):
            xi, xit = xin.next()
            pv = (s0 - 1) % L
            nx = (s0 + SEG) % L
            p.dma("sp", out=xi[:, 1:SEG + 1], in_=J.xh[row0:row0 + 128, s0:s0 + SEG], writes=[xit])
            p.dma("sp", out=xi[:, 0:1], in_=J.xh[row0:row0 + 128, pv:pv + 1], joins=[xit])
            p.dma("sp", out=xi[:, SEG + 1:SEG + 2], in_=J.xh[row0:row0 + 128, nx:nx + 1], joins=[xit])
            if pm is not None:
                p.op("dve", lambda e: e.tensor_scalar(out=xi[:, 0:1], in0=xi[:, 0:1], scalar1=pm, scalar2=None, op0=ALU.mult),
                     [xit, G.cst], [xit])
            if nm is not None:
                p.op("dve", lambda e: e.tensor_scalar(out=xi[:, SEG + 1:SEG + 2], in0=xi[:, SEG + 1:SEG + 2], scalar1=nm, scalar2=None,
                                                      op0=ALU.mult), [xit, G.cst], [xit])
            c_, ct_ = cv.next()
            w0 = G.icw[:, gi, 0:1]
            w1 = G.icw[:, gi, 1:2]
            w2 = G.icw[:, gi, 2:3]
            bb = G.icb[:, gi:gi + 1]
            p.op("act", lambda e: e.activation(out=c_[:], in_=xi[:, 1:SEG + 1], func=AF.Identity, scale=w1, bias=bb),
                 [xit, G.cst], [ct_])
            p.op("dve", lambda e: e.scalar_tensor_tensor(out=c_[:], in0=xi[:, 0:SEG], scalar=w0, in1=c_[:], op0=ALU.mult, op1=ALU.add),
                 [xit, G.cst], [ct_])
            p.op("dve", lambda e: e.scalar_tensor_tensor(out=c_[:], in0=xi[:, 2:SEG + 2], scalar=w2, in1=c_[:], op0=ALU.mult, op1=ALU.add),
                 [xit, G.cst], [ct_])
            return c_, ct_

        def masks(s0):
            pm = m_wrap if s0 == 0 else (m_mid if s0 == Lq else None)
            e_ = s0 + SEG
            nm = m_wrap if e_ == L else (m_mid if e_ == Lq else None)
            return pm, nm

        for g in range(8):
            for s0 in range(0, L, SEG):
                pm, nm = masks(s0)
                c1, c1t = conv_seg(CH + g * 128, 8 + g, s0, pm, nm)
                c2, c2t = conv_seg(2 * CH + g * 128, 16 + g, s0, pm, nm)
                u, ut = ub.next()
                tt_op(p, "pool", u[:], c1[:], c2[:], ALU.mult, [c1t, c2t], [ut])
                p.dma("pool", out=J.uT[g * 128:(g + 1) * 128, s0:s0 + SEG], in_=u[:], reads=[ut])
            for s0 in range(0, Lq, SEG):
                pm, nm = masks(s0)
                c0, c0t = conv_seg(g * 128, g, s0, pm, nm)
                u, ut = ub.next()
                copy_op(p, "act", u[:], c0[:], [c0t], [ut])
                p.dma("pool", out=J.x0c[g * 128:(g + 1) * 128, 1 + s0:1 + s0 + SEG], in_=u[:], reads=[ut])
            h, ht = hx.next()
            p.dma("sp", out=h[:, 0:2], in_=J.xh[g * 128:(g + 1) * 128, L - 2:L], writes=[ht])
            p.dma("sp", out=h[:, 2:3], in_=J.xh[g * 128:(g + 1) * 128, 0:1], joins=[ht])
            p.dma("sp", out=h[:, 3:6], in_=J.xh[g * 128:(g + 1) * 128, Lq - 1:Lq + 2], joins=[ht])
            o, ot = ho.next()
            w0 = G.icw[:, g, 0:1]
            w1 = G.icw[:, g, 1:2]
            w2 = G.icw[:, g, 2:3]
            bb = G.icb[:, g:g + 1]
            hv_ = h[:, 0:6].rearrange("p (a b) -> p a b", b=3)
            p.op("dve", lambda e: e.tensor_scalar(out=o[:, 0:2], in0=hv_[:, :, 1], scalar1=w1, scalar2=bb, op0=ALU.mult, op1=ALU.add),
                 [ht, G.cst], [ot])
            p.op("dve", lambda e: e.scalar_tensor_tensor(out=o[:, 0:2], in0=hv_[:, :, 0], scalar=w0, in1=o[:, 0:2], op0=ALU.mult, op1=ALU.add),
                 [ht, G.cst], [ot])
            p.op("dve", lambda e: e.scalar_tensor_tensor(out=o[:, 0:2], in0=hv_[:, :, 2], scalar=w2, in1=o[:, 0:2], op0=ALU.mult, op1=ALU.add),
                 [ht, G.cst], [ot])
            ob_, obt_ = hb.next()
            copy_op(p, "dve", ob_[:], o[:, 0:2], [ot], [obt_])
            p.dma("pool", out=J.x0c[g * 128:(g + 1) * 128, 0:1], in_=ob_[:, 0:1], reads=[obt_])
            p.dma("pool", out=J.x0c[g * 128:(g + 1) * 128, NQ - 1:NQ], in_=ob_[:, 1:2], reads=[obt_])


def phase_C(G, J):
    p = G.p
    L, Lq, NQ = J.L, J.Lq, J.NQ
    ps, pst = G.ps, G.pst
    nkt = L // 128
    qchunks = [(a, min(a + 512, NQ)) for a in range(0, NQ, 512)]
    scale = 64.0 ** -0.5
    nlam = G.lam[:, 0:1]
    with ExitStack() as es:
        A = mk_alloc(G, es)
        kr = mk_ring(A, 2, [128, L], BF16)
        vr = mk_ring(A, 2, [128, nkt, 128], BF16)
        qr = mk_ring(A, 2, [128, NQ], BF16)
        er = mk_ring(A, 3, [128, 2, 512], BF16)
        rz = mk_ring(A, 2, [128, 512], F32)
        rbr = mk_ring(A, 2, [128, 512], F32)
        sel = A([64, 256], F32)
        selt = Tok()
        p.op("dve", lambda e: e.memset(sel[:], 0.0), [], [selt])
        p.op("dve", lambda e: e.memset(sel[0:32, 0:128], 1.0 / 32), [], [selt])
        p.op("dve", lambda e: e.memset(sel[32:64, 128:256], 1.0 / 32), [], [selt])
        a0r = mk_ring(A, 2, [128, 512], F32)
        a1r = mk_ring(A, 2, [128, 512], F32)
        sqr = mk_ring(A, 2, [128, 512], BF16)
        rsr = mk_ring(A, 2, [128, 512], F32)
        obr = mk_ring(A, 2, [128, 512], BF16)
        psall = G.psall
        for h in range(NH):
            k, kt_ = kr.next()
            p.dma("sp", out=k[:], in_=J.kT[h, :, :], writes=[kt_])
            v, vt = vr.next()
            p.dma("sp", out=v[:], in_=J.V[:, h * 128:(h + 1) * 128].rearrange("(t q) d -> q t d", q=128), writes=[vt])
            q, qt = qr.next()
            p.dma("sp", out=q[:], in_=J.qT[h, :, :], writes=[qt])
            for (a, b) in qchunks:
                n = b - a

                def s_pair(kt, slot):
                    b0 = 4 + 2 * slot

                    def mm(e):
                        e.matmul(ps[b0][:, 0:n], k[0:64, kt * 128:(kt + 1) * 128], q[0:64, a:b], start=True, stop=True)
                        return e.matmul(ps[b0 + 1][:, 0:n], k[64:128, kt * 128:(kt + 1) * 128], q[64:128, a:b], start=True, stop=True)
                    p.op("pe", mm, [kt_, qt], [pst[b0], pst[b0 + 1]])
                    e_, et = er.next()
                    src = psall[:, b0 * 512:(b0 + 2) * 512].rearrange("p (m n) -> p m n", m=2)[:, :, 0:n]
                    p.op("act", lambda e: e.activation(out=e_[:, :, 0:n], in_=src, func=AF.Exp, scale=scale),
                         [pst[b0], pst[b0 + 1]], [et])
                    return e_, et

                def av_mm(kt, e_, et):
                    def mm(e):
                        for m in range(2):
                            e.matmul(ps[m][:, 0:n], v[:, kt, :], e_[:, m, 0:n], start=(kt == 0), stop=(kt == nkt - 1))
                        for m in range(2):
                            ins = e.matmul(ps[2][32 * m:32 * m + 32, 0:n], G.ones_bf[:, 0:32], e_[:, m, 0:n],
                                           start=(kt == 0), stop=(kt == nkt - 1), tile_position=(0, 32 * m))
                        return ins
                    if kt == 0:
                        p.op("pe", mm, [vt, et, G.cst], [pst[0], pst[1], pst[2]])
                    else:
                        p.op("pe", mm, [vt, et, G.cst], [], [pst[0], pst[1], pst[2]])

                cur = s_pair(0, 0)
                for kt in range(nkt):
                    nxt = s_pair(kt + 1, (kt + 1) % 2) if kt + 1 < nkt else None
                    av_mm(kt, *cur)
                    cur = nxt
                r_, rt = rz.next()
                a0, a0t = a0r.next()
                a1, a1t = a1r.next()
                p.op("dve", lambda e: e.reciprocal(out=r_[0:64, 0:n], in_=ps[2][0:64, 0:n]), [pst[2]], [rt])
                for m, (am, amt) in enumerate(((a0, a0t), (a1, a1t))):
                    pb = 3 if m == 0 else 4
                    p.op("pe", lambda e: e.matmul(ps[pb][:, 0:n], sel[:, m * 128:(m + 1) * 128], r_[0:64, 0:n], start=True, stop=True),
                         [rt, selt], [pst[pb]])
                    rb, rbt = rbr.next()
                    copy_op(p, "act", rb[:, 0:n], ps[pb][:, 0:n], [pst[pb]], [rbt])
                    tt_op(p, "dve", am[:, 0:n], ps[m][:, 0:n], rb[:, 0:n], ALU.mult, [pst[m], rbt], [amt])
                p.op("dve", lambda e: e.scalar_tensor_tensor(out=a0[:, 0:n], in0=a1[:, 0:n], scalar=nlam, in1=a0[:, 0:n],
                                                              op0=ALU.mult, op1=ALU.add), [a1t, G.cst], [a0t])
                sq, sqt = sqr.next()
                tt_op(p, "pool", sq[:, 0:n], a0[:, 0:n], a0[:, 0:n], ALU.mult, [a0t], [sqt])
                si = 4
                p.op("pe", lambda e: e.matmul(ps[si][:, 0:n], G.ones_bf[:], sq[:, 0:n], start=True, stop=True), [sqt, G.cst], [pst[si]])
                rs, rst = rsr.next()
                rstd_ops(p, ps[si][:, 0:n], rs[:, 0:n], rs[:, 0:n], rst, 1.0 / 128, extra_reads=[pst[si]])
                tt_op(p, "pool", a0[:, 0:n], a0[:, 0:n], rs[:, 0:n], ALU.mult, [rst], [a0t])
                o, ot = obr.next()
                p.op("dve", lambda e: e.tensor_scalar(out=o[:, 0:n], in0=a0[:, 0:n], scalar1=G.subg[:, 0:1], scalar2=1.0 - LAMBDA_INIT,
                                                      op0=ALU.mult, op1=ALU.mult), [a0t, G.cst], [ot])
                p.dma("pool", out=J.attn[h * 128:(h + 1) * 128, a:b], in_=o[:, 0:n], reads=[ot])


def phase_D(G, J):
    p = G.p
    L, Lq, NQ = J.L, J.Lq, J.NQ
    ps, pst = G.ps, G.pst
    chunks = []
    p0 = 0
    while p0 < Lq:
        chunks.append((p0, min(QW, Lq - p0) + 2))
        p0 += QW
    with ExitStack() as es:
        A = mk_alloc(G, es)
        raw = A([128, 44 * 512], BF16)
        rawt = Tok()
        gT = raw[:, :].rearrange("p (k n) -> p k n", k=44)
        at_ = raw[:, 0:8 * 512].rearrange("p (k n) -> p k n", k=8)
        hy_ = raw[:, 8 * 512:16 * 512].rearrange("p (k n) -> p k n", k=8)
        mg = raw[:, 16 * 512:32 * 512].rearrange("p (k n) -> p k n", k=16)
        g0r = mk_ring(A, 2, [128, 512], BF16)
        g1r = mk_ring(A, 2, [128, 512], BF16)
        m0r = mk_ring(A, 1, [128, 512], F32)
        m1r = mk_ring(A, 1, [128, 512], F32)
        x1 = A([128, 4, D], F32)
        x1t = [Tok() for _ in range(4)]
        hnr = mk_ring(A, 2, [128, D], BF16)
        st = mk_ring(A, 4, [128, 4], F32)
        hn2T = A([128, 16, 512], BF16)
        hn2t = Tok()
        wr = mk_ring(A, 3, [128, 16 * 512], BF16)
        ugr = mk_ring(A, 2, [128, 512], F32)
        uvr = mk_ring(A, 2, [128, 512], F32)
        sgr = mk_ring(A, 2, [128, 512], F32)
        normf = A([128, D], F32)
        w0m = A([128, 88], F32)
        w2m = A([128, 88], F32)
        ct = Tok()
        p.dma("sp", out=normf[:], in_=G.normf_d[:, :], writes=[ct])
        p.op("dve", lambda e: e.tensor_scalar(out=w0m[:], in0=G.fcw[:, :, 0], scalar1=G.flags[:, 0:1], scalar2=None, op0=ALU.mult),
             [G.cst], [ct])
        p.op("dve", lambda e: e.tensor_scalar(out=w2m[:], in0=G.fcw[:, :, 2], scalar1=G.flags[:, 1:2], scalar2=None, op0=ALU.mult),
             [G.cst], [ct])

        for ci, (p0, n) in enumerate(chunks):
            first_c = (ci == 0)
            last_c = (ci == len(chunks) - 1)
            ntile = (n + 127) // 128
            tns = [min(128, n - 128 * i) for i in range(ntile)]
            p.dma("sp", out=at_[:, :, 0:n], in_=J.attn[:, p0:p0 + n].rearrange("(g q) n -> q g n", q=128), writes=[rawt])
            p.dma("sp", out=hy_[:, :, 0:n], in_=J.hy[:, p0:p0 + n].rearrange("(g q) n -> q g n", q=128), joins=[rawt])
            for i in range(ntile):
                tn = tns[i]
                pr0 = p0 + 128 * i
                if pr0 == 0:
                    p.dma("sp", out=x1[0:1, i, :], in_=J.x[L - 1:L, :], writes=[x1t[i]])
                    if tn > 1:
                        p.dma("sp", out=x1[1:tn, i, :], in_=J.x[0:tn - 1, :], joins=[x1t[i]])
                else:
                    p.dma("sp", out=x1[0:tn, i, :], in_=J.x[pr0 - 1:pr0 - 1 + tn, :], writes=[x1t[i]])
            for blk in range(4):
                wa, wat = wr.next()
                wh, wht = wr.next()
                wa_v = wa[:, 0:8 * 512].rearrange("p (k n) -> p k n", k=8)
                wh_v = wh[:, 0:8 * 512].rearrange("p (k n) -> p k n", k=8)
                p.dma("sp", out=wa_v, in_=G.w_ao[:, blk * 512:(blk + 1) * 512].rearrange("(kc q) n -> q kc n", q=128), writes=[wat])
                p.dma("sp", out=wh_v, in_=G.w_ho[:, blk * 512:(blk + 1) * 512].rearrange("(kc q) n -> q kc n", q=128), writes=[wht])
                for j in range(4):
                    og = blk * 4 + j
                    g0, g0t = g0r.next()
                    g1, g1t = g1r.next()
                    p.dma("sp", out=g0[:, 0:n], in_=J.gates[og * 128:(og + 1) * 128, p0:p0 + n], writes=[g0t])
                    p.dma("sp", out=g1[:, 0:n], in_=J.gates[D + og * 128:D + (og + 1) * 128, p0:p0 + n], writes=[g1t])
                    pa, pb = p.nextps(), p.nextps()

                    def mma(e):
                        for kc in range(8):
                            e.matmul(ps[pa][:, 0:n], wa_v[:, kc, j * 128:(j + 1) * 128], at_[:, kc, 0:n], start=(kc == 0), stop=(kc == 7))
                        for kc in range(8):
                            ins = e.matmul(ps[pb][:, 0:n], wh_v[:, kc, j * 128:(j + 1) * 128], hy_[:, kc, 0:n], start=(kc == 0), stop=(kc == 7))
                        return ins
                    p.op("pe", mma, [wat, wht, rawt], [pst[pa], pst[pb]])
                    m0, m0t = m0r.next()
                    m1, m1t = m1r.next()
                    tt_op(p, "dve", m0[:, 0:n], ps[pa][:, 0:n], g0[:, 0:n], ALU.mult, [pst[pa], g0t], [m0t])
                    tt_op(p, "dve", m1[:, 0:n], ps[pb][:, 0:n], g1[:, 0:n], ALU.mult, [pst[pb], g1t], [m1t])
                    tt_op(p, "pool", mg[:, og, 0:n], m0[:, 0:n], m1[:, 0:n], ALU.add, [m0t, m1t], joins=[rawt])
            for cb in range(4):
                wb, wbt = wr.next()
                wb_v = wb[:, :].rearrange("p (k n) -> p k n", k=16)
                p.dma("sp", out=wb_v, in_=G.w_out[:, cb * 512:(cb + 1) * 512].rearrange("(kc q) n -> q kc n", q=128), writes=[wbt])
                for i in range(ntile):
                    tn = tns[i]
                    pi = p.nextps()

                    def mmo(e):
                        for kc in range(16):
                            ins = e.matmul(ps[pi][0:tn, :], mg[:, kc, i * 128:i * 128 + tn], wb_v[:, kc, :], start=(kc == 0), stop=(kc == 15))
                        return ins
                    p.op("pe", mmo, [wbt, rawt], [pst[pi]])
                    dst = x1[0:tn, i, cb * 512:(cb + 1) * 512]
                    tt_op(p, "dve", dst, ps[pi][0:tn, :], dst, ALU.add, [pst[pi]], [x1t[i]])
            for i in range(ntile):
                tn = tns[i]
                hn, hnt = hnr.next()
                sv, svt = st.next()
                p.op("dve", lambda e: e.memset(sv[:, 0:1], 0.0), [], [svt])
                p.op("act", lambda e: e.activation(out=hn[0:tn, :], in_=x1[0:tn, i, :], func=AF.Square, accum_out=sv[0:tn, 0:1]),
                     [x1t[i]], [hnt, svt])
                rstd_ops(p, sv[0:tn, 0:1], sv[0:tn, 1:2], sv[0:tn, 2:3], svt, 1.0 / D)
                p.op("act", lambda e: e.activation(out=hn[0:tn, :], in_=x1[0:tn, i, :], func=AF.Copy, scale=sv[0:tn, 2:3]),
                     [x1t[i], svt], [hnt])
                for half in range(2):
                    def tr(e):
                        for k in range(8):
                            kc = half * 8 + k
                            ins = e.transpose(G.psb[:, k * 128:k * 128 + tn], hn[0:tn, kc * 128:(kc + 1) * 128], G.ident[0:tn, 0:tn])
                        return ins
                    p.op("pe", tr, [hnt, G.cst], [G.psbt])
                    copy_op(p, "dve" if half == 0 else "act", hn2T[:, half * 8:(half + 1) * 8, i * 128:i * 128 + tn],
                            G.psb[:, :].rearrange("p (k t) -> p k t", k=8)[:, :, 0:tn], [G.psbt],
                            [hn2t] if (i == 0 and half == 0) else [], [] if (i == 0 and half == 0) else [hn2t])
            for bg in range(11):
                wg, wgt = wr.next()
                wv, wvt = wr.next()
                wg_v = wg[:, :].rearrange("p (k n) -> p k n", k=16)
                wv_v = wv[:, :].rearrange("p (k n) -> p k n", k=16)
                p.dma("sp", out=wg_v, in_=G.w_up[:, bg * 512:(bg + 1) * 512].rearrange("(kc q) n -> q kc n", q=128), writes=[wgt])
                p.dma("sp", out=wv_v, in_=G.w_up[:, DFF + bg * 512:DFF + (bg + 1) * 512].rearrange("(kc q) n -> q kc n", q=128), writes=[wvt])
                for j in range(4):
                    kk = bg * 4 + j
                    res = []
                    for (w_v, wt_, zg, ring) in ((wg_v, wgt, kk, ugr), (wv_v, wvt, 44 + kk, uvr)):
                        pi = p.nextps()

                        def mmu(e):
                            for kc in range(16):
                                ins = e.matmul(ps[pi][:, 0:n], w_v[:, kc, j * 128:(j + 1) * 128], hn2T[:, kc, 0:n], start=(kc == 0), stop=(kc == 15))
                            return ins
                        p.op("pe", mmu, [wt_, hn2t], [pst[pi]])
                        u, ut = ring.next()
                        z = ps[pi]
                        p.op("act", lambda e: e.activation(out=u[:, 1:n - 1], in_=z[:, 1:n - 1], func=AF.Identity,
                                                           scale=G.fcw[:, zg, 1:2], bias=G.fcb[:, zg:zg + 1]), [pst[pi], G.cst], [ut])
                        lo = 2 if first_c else 1
                        hi = n - 2 if last_c else n - 1
                        if n - 1 > lo:
                            p.op("dve", lambda e: e.scalar_tensor_tensor(out=u[:, lo:n - 1], in0=z[:, lo - 1:n - 2], scalar=G.fcw[:, zg, 0:1],
                                                                         in1=u[:, lo:n - 1], op0=ALU.mult, op1=ALU.add), [pst[pi], G.cst], [ut])
                        if first_c:
                            p.op("dve", lambda e: e.scalar_tensor_tensor(out=u[:, 1:2], in0=z[:, 0:1], scalar=w0m[:, zg:zg + 1],
                                                                         in1=u[:, 1:2], op0=ALU.mult, op1=ALU.add), [pst[pi], ct], [ut])
                        if hi > 1:
                            p.op("dve", lambda e: e.scalar_tensor_tensor(out=u[:, 1:hi], in0=z[:, 2:hi + 1], scalar=G.fcw[:, zg, 2:3],
                                                                         in1=u[:, 1:hi], op0=ALU.mult, op1=ALU.add), [pst[pi], G.cst], [ut])
                        if last_c:
                            p.op("dve", lambda e: e.scalar_tensor_tensor(out=u[:, n - 2:n - 1], in0=z[:, n - 1:n], scalar=w2m[:, zg:zg + 1],
                                                                         in1=u[:, n - 2:n - 1], op0=ALU.mult, op1=ALU.add), [pst[pi], ct], [ut])
                        res.append((u, ut))
                    (ug, ugt), (uv, uvt) = res
                    sg, sgt = sgr.next()
                    p.op("act", lambda e: e.activation(out=sg[:, 1:n - 1], in_=ug[:, 1:n - 1], func=AF.Silu), [ugt], [sgt])
                    tt_op(p, "pool", gT[:, kk, 1:n - 1], sg[:, 1:n - 1], uv[:, 1:n - 1], ALU.mult, [sgt, uvt], joins=[rawt])
                    p.op("pool", lambda e: e.memset(gT[:, kk, 0:1], 0.0), [], [], [rawt])
                    p.op("pool", lambda e: e.memset(gT[:, kk, n - 1:n], 0.0), [], [], [rawt])
            for cb in range(4):
                banks = [p.nextps() for _ in range(ntile)]
                for q4 in range(4):
                    wd, wdt = wr.next()
                    wd_v = wd[:, 0:11 * 512].rearrange("p (k n) -> p k n", k=11)
                    p.dma("sp", out=wd_v, in_=G.w_dn[q4 * 1408:(q4 + 1) * 1408, cb * 512:(cb + 1) * 512].rearrange("(kc q) n -> q kc n", q=128),
                          writes=[wdt])
                    for i in range(ntile):
                        tn = tns[i]
                        pi = banks[i]

                        def mmd(e):
                            for kc in range(11):
                                kk = q4 * 11 + kc
                                ins = e.matmul(ps[pi][0:tn, :], gT[:, kk, i * 128:i * 128 + tn], wd_v[:, kc, :], start=(kk == 0), stop=(kk == 43))
                            return ins
                        if q4 == 0:
                            p.op("pe", mmd, [wdt, rawt], [pst[pi]])
                        else:
                            p.op("pe", mmd, [wdt, rawt], [], [pst[pi]])
                for i in range(ntile):
                    tn = tns[i]
                    pi = banks[i]
                    dst = x1[0:tn, i, cb * 512:(cb + 1) * 512]
                    tt_op(p, "dve", dst, ps[pi][0:tn, :], dst, ALU.add, [pst[pi]], [x1t[i]])
            for i in range(ntile):
                tn = tns[i]
                hn, hnt = hnr.next()
                sv, svt = st.next()
                p.op("dve", lambda e: e.memset(sv[:, 0:1], 0.0), [], [svt])
                p.op("act", lambda e: e.activation(out=hn[0:tn, :], in_=x1[0:tn, i, :], func=AF.Square, accum_out=sv[0:tn, 0:1]),
                     [x1t[i]], [hnt, svt])
                rstd_ops(p, sv[0:tn, 0:1], sv[0:tn, 1:2], sv[0:tn, 2:3], svt, 1.0 / D)
                p.op("act", lambda e: e.activation(out=x1[0:tn, i, :], in_=x1[0:tn, i, :], func=AF.Copy, scale=sv[0:tn, 2:3]),
                     [svt], [x1t[i]])
                tt_op(p, "dve", x1[0:tn, i, :], x1[0:tn, i, :], normf[0:tn, :], ALU.mult, [ct], [x1t[i]])
                r_lo = 1 if i == 0 else 0
                r_hi = tn - 1 if i == ntile - 1 else tn
                if r_hi > r_lo:
                    o0 = p0 + 128 * i + r_lo - 1
                    p.dma("pool", out=J.y[o0:o0 + (r_hi - r_lo), :], in_=x1[r_lo:r_hi, i, :], reads=[x1t[i]])


def host_consts(L, h):
    Lq = L // 2
    N2 = L // 64
    N = 2 * L
    f32 = np.float32
    pos = ((np.arange(L) + h * Lq) % L).astype(f32)
    inv = (10000.0 ** (-np.arange(0, 64, 2, dtype=f32) / 64.0)).astype(f32)
    pidx = np.arange(128) % 64 % 32
    ang = (pos[None, :] * inv[pidx][:, None]).astype(f32)
    c = {}
    c["cos"] = np.cos(ang).astype(f32)
    c["sin"] = np.sin(ang).astype(f32)
    t = np.linspace(0.0, 1.0, L, dtype=f32)[:, None]
    w = (2.0 * math.pi * np.arange(L, dtype=f32)[:, None] / L).astype(f32)
    f = np.linspace(1e-4, 15, 16, dtype=f32)[None]
    z = np.concatenate([t, np.cos(f * w), -np.sin(f * w)], -1).astype(f32)
    vidx = np.concatenate([np.arange(L), [0], np.arange(L - 1, 0, -1)])
    c["zT"] = np.ascontiguousarray(z[vidx].T)
    c["trow"] = np.ascontiguousarray(np.tile(t[vidx].T, (128, 1))).astype(f32)
    n1 = np.arange(64, dtype=np.float64)[:, None]
    k1 = np.arange(128, dtype=np.float64)[None, :]
    n1f = np.arange(128, dtype=np.float64)[:, None]
    th = 2 * np.pi * n1f * k1 / 128
    F1f = np.concatenate([np.cos(th), -np.sin(th)], 1)
    n1t = ((np.arange(64) + 32 * h) % 64).astype(np.float64)[:, None]
    th = 2 * np.pi * n1t * k1 / 128
    F1u = np.concatenate([np.cos(th), -np.sin(th)], 1)
    c["F1f"] = F1f.astype(NPBF)
    c["F1u"] = F1u.astype(NPBF)
    th0 = 2 * np.pi * n1f * k1 / 128
    c["F1fb"] = np.concatenate([np.sin(th0), np.cos(th0)], 1).astype(NPBF)
    c["F1ub"] = np.concatenate([np.sin(th), np.cos(th)], 1).astype(NPBF)
    k1v = np.arange(128, dtype=np.float64)[:, None, None]
    n2v = np.arange(N2, dtype=np.float64)[None, :, None]
    k2v = np.arange(N2, dtype=np.float64)[None, None, :]
    th = 2 * np.pi * n2v * (k1v + 128 * k2v) / N
    Gt = np.stack([np.cos(th), -np.sin(th), np.sin(th)], 2)
    c["G"] = np.ascontiguousarray(Gt).astype(NPBF)
    ph = 2 * np.pi * np.arange(N2, dtype=np.float64)[:, None] * np.arange(N2, dtype=np.float64)[None, :] / N2
    c["T1"] = np.concatenate([np.cos(ph), np.sin(ph), -np.sin(ph), np.cos(ph)], 1).astype(NPBF)
    n1rot = np.array([63] + list(range(32)) + [32])
    n1true = ((n1rot + 32 * h) % 64).astype(np.float64)
    th = 2 * np.pi * (N2 * n1true[None, None, :] + np.arange(N2, dtype=np.float64)[None, :, None]) * \
        np.arange(128, dtype=np.float64)[:, None, None] / N
    c["Ginv"] = np.ascontiguousarray(np.stack([np.cos(th) / N, -np.sin(th) / N], 2)).astype(NPBF)
    return c


def shared_inputs(inp):
    f32 = np.float32
    a = lambda x: np.ascontiguousarray(np.asarray(x, dtype=f32))
    m = {}
    m["w_in"] = a(inp["w_in"][0])
    m["w_ao"] = a(inp["w_attn_out"][0])
    m["w_ho"] = a(inp["w_hyena_out"][0])
    m["w_out"] = a(inp["w_out"][0])
    m["w_up"] = a(inp["w_up"][0])
    m["w_dn"] = a(inp["w_down"][0])
    m["norm1"] = a(np.asarray(inp["norm1"][0]).reshape(16, 128).T)
    m["norm2"] = a(np.asarray(inp["norm2"][0]).reshape(16, 128).T)
    m["normf"] = a(np.tile(np.asarray(inp["norm_f"])[None, :], (128, 1)))
    m["gateb"] = a(np.asarray(inp["gate_b"][0]).reshape(32, 128).T)
    m["icw"] = a(np.asarray(inp["in_conv_w"][0]).reshape(3, 24, 128).transpose(2, 1, 0))
    m["icb"] = a(np.asarray(inp["in_conv_b"][0]).reshape(24, 128).T)
    m["fcw"] = a(np.asarray(inp["ffn_conv_w"][0]).reshape(3, 88, 128).transpose(2, 1, 0))
    m["fcb"] = a(np.asarray(inp["ffn_conv_b"][0]).reshape(88, 128).T)
    m["subg"] = a(np.asarray(inp["subln_g"][0]).reshape(128, 1))
    m["hbias"] = a(np.tile(np.asarray(inp["hyena_bias"][0])[None, :], (128, 1)))
    m["hbiasp"] = a(np.asarray(inp["hyena_bias"][0]).reshape(8, 128).T)
    m["lamv"] = a(np.stack([np.asarray(inp[k][0]) for k in ("lambda_q1", "lambda_k1", "lambda_q2", "lambda_k2")], 1))
    m["fw1"] = a(inp["filt_w1"][0])
    m["fw2"] = a(inp["filt_w2"][0])
    m["fw3"] = a(inp["filt_w3"][0])
    m["fw4"] = a(inp["filt_w4"][0])
    m["fvec"] = a(np.stack([np.asarray(inp[k][0]) for k in ("filt_b1", "filt_b2", "filt_b3", "filt_freq")], 1))
    max_decay = math.log(1e-2) / 0.3
    min_decay = math.log(1e-2) / 1.5
    deltas = np.abs(np.linspace(min_decay, max_decay, CH, dtype=f32))
    m["ndelta"] = a((-deltas).reshape(8, 128).T)
    m["ident"] = np.eye(128, dtype=f32).astype(NPBF)
    R = np.zeros((128, 128), f32)
    for blk in range(2):
        for d in range(64):
            if d < 32:
                R[blk * 64 + d + 32, blk * 64 + d] = -1.0
            else:
                R[blk * 64 + d - 32, blk * 64 + d] = 1.0
    m["rotm"] = R.astype(NPBF)
    return m


_NC_CACHE = {}


def run(inp, seqs, Ls, n_cores, trace=False, runner=None):
    key = tuple(Ls)
    if key not in _NC_CACHE:
        _NC_CACHE[key] = build(Ls)
    nc = _NC_CACHE[key]
    sh = shared_inputs(inp)
    in_maps = []
    for c in range(n_cores):
        pair, h = c // 2, c % 2
        m = dict(sh)
        m["flags"] = np.tile(np.array([[1.0 - (h == 0), 1.0 * (h == 0)]], np.float32), (128, 1))
        for j, L in enumerate(Ls):
            s = "_%d" % j
            x = np.asarray(seqs[j][pair], dtype=np.float32)
            m["x" + s] = np.ascontiguousarray(np.roll(x, -h * (L // 2), axis=0))
            for k, v in host_consts(L, h).items():
                m[k + s] = v
        in_maps.append(m)
    if runner is not None:
        res = runner(nc, in_maps)
    else:
        res = run_bass_kernel_spmd(nc, in_maps, core_ids=list(range(n_cores)), trace=trace)
    outs = [np.zeros(np.asarray(seqs[j]).shape, np.float32) for j in range(len(Ls))]
    for c in range(n_cores):
        pair, h = c // 2, c % 2
        for j, L in enumerate(Ls):
            Lq = L // 2
            outs[j][pair, h * Lq:(h + 1) * Lq] = res.results[c]["y_%d" % j]
    return outs, res


def kernel(**inputs):
    xs = np.asarray(inputs["x_sample"])
    xp = np.asarray(inputs["x_prompt"])
    outs, _ = run(inputs, [xs, xp], [xs.shape[1], xp.shape[1]], 8)
    return (outs[1], outs[0])
```

```python
import math
from contextlib import ExitStack

import numpy as np
import ml_dtypes

import concourse.bass as bass
import concourse.mybir as mybir
from concourse.bass_utils import run_bass_kernel_spmd

F32 = mybir.dt.float32
BF16 = mybir.dt.bfloat16
AF = mybir.ActivationFunctionType
ALU = mybir.AluOpType
NPBF = ml_dtypes.bfloat16

D = 2048
NH = 8
CH = 1024
DFF = 5632
INW = 10240
EPS = 1e-6
LAMBDA_INIT = 0.8 - 0.6 * math.exp(0.0)
QW = 510


class Tok:
    __slots__ = ("w", "r", "excl")

    def __init__(self, excl=False):
        self.w = {}
        self.r = {}
        self.excl = excl


class Prog:
    def __init__(self, nc):
        self.nc = nc
        self.engs = {"pe": nc.tensor, "act": nc.scalar, "dve": nc.vector,
                     "pool": nc.gpsimd, "sp": nc.sync}
        self.sems = {}
        self.cnt = {}
        for e in self.engs:
            self.sems[e] = nc.semaphore("s_" + e).__enter__()
            self.cnt[e] = 0
        self.NS = 12
        for q in ("sp", "pool"):
            for i in range(self.NS):
                k = "d%s%d" % (q, i)
                self.sems[k] = nc.semaphore(k).__enter__()
                self.cnt[k] = 0
        self.rr = {"sp": 0, "pool": 0}
        self.seen = {e: {} for e in self.engs}
        self.uid = 0
        self.psi = -1

    def nextps(self):
        self.psi = (self.psi + 1) % 7
        return self.psi

    def name(self, base="t"):
        self.uid += 1
        return "%s%d" % (base, self.uid)

    def _wait(self, e, deps):
        for k, v in deps.items():
            if self.seen[e].get(k, 0) < v:
                self.engs[e].wait_ge(self.sems[k], v)
                self.seen[e][k] = v

    @staticmethod
    def _deps(reads, writes, joins=(), eng=None):
        d = {}
        for t in reads:
            for k, v in t.w.items():
                if d.get(k, 0) < v:
                    d[k] = v
            if t.excl:
                for k, v in t.r.items():
                    if k != eng and d.get(k, 0) < v:
                        d[k] = v
        for t in writes:
            for m in (t.w, t.r):
                for k, v in m.items():
                    if d.get(k, 0) < v:
                        d[k] = v
        for t in joins:
            for k, v in t.r.items():
                if d.get(k, 0) < v:
                    d[k] = v
        if eng == "pe":
            d.pop("pe", None)
        return d

    @staticmethod
    def _commit(reads, writes, k, v, joins=()):
        for t in reads:
            if t.r.get(k, 0) < v:
                t.r[k] = v
        for t in writes:
            t.w = {k: v}
            t.r = {}
        for t in joins:
            if t.w.get(k, 0) < v:
                t.w[k] = v

    def op(self, e, fn, reads=(), writes=(), joins=()):
        self._wait(e, self._deps(reads, writes, joins, e))
        ins = fn(self.engs[e])
        self.cnt[e] += 1
        ins.then_inc(self.sems[e], 1)
        self._commit(reads, writes, e, self.cnt[e], joins)

    def dma(self, q, out, in_, reads=(), writes=(), joins=()):
        i = self.rr[q]
        self.rr[q] = (i + 1) % self.NS
        k = "d%s%d" % (q, i)
        deps = self._deps(reads, writes, joins)
        if self.cnt[k] > 0:
            deps[k] = max(deps.get(k, 0), self.cnt[k])
        self._wait(q, deps)
        self.engs[q].dma_start(out=out, in_=in_, allow_slow_non_contiguous=True).then_inc(self.sems[k], 16)
        self.cnt[k] += 16
        self._commit(reads, writes, k, self.cnt[k], joins)

    def barrier(self):
        allv = {k: v for k, v in self.cnt.items() if v > 0}
        for e in self.engs:
            self._wait(e, allv)


class Ring:
    def __init__(self, bufs):
        self.bufs = bufs
        self.toks = [Tok() for _ in bufs]
        self.i = -1

    def next(self):
        self.i = (self.i + 1) % len(self.bufs)
        return self.bufs[self.i], self.toks[self.i]


class Job:
    pass


def build(Ls):
    nc = bass.Bass("TRN2", target_bir_lowering=False)
    p = Prog(nc)

    def din(name, shape, dt=F32):
        return nc.dram_tensor(name, list(shape), dt, kind="ExternalInput").ap()

    def dscr(name, shape, dt=BF16):
        return nc.dram_tensor(name, list(shape), dt).ap()

    w_in_f = din("w_in", [D, INW])
    w_ao_f = din("w_ao", [CH, D])
    w_ho_f = din("w_ho", [CH, D])
    w_out_f = din("w_out", [D, D])
    w_up_f = din("w_up", [D, 2 * DFF])
    w_dn_f = din("w_dn", [DFF, D])
    norm1_d = din("norm1", [128, 16])
    norm2_d = din("norm2", [128, 16])
    normf_d = din("normf", [128, D])
    gateb_d = din("gateb", [128, 32])
    icw_d = din("icw", [128, 24, 3])
    icb_d = din("icb", [128, 24])
    fcw_d = din("fcw", [128, 88, 3])
    fcb_d = din("fcb", [128, 88])
    subg_d = din("subg", [128, 1])
    hbias_d = din("hbias", [128, CH])
    hbiasp_d = din("hbiasp", [128, 8])
    lamv_d = din("lamv", [64, 4])
    fw1_d = din("fw1", [33, 64])
    fw2_d = din("fw2", [64, 64])
    fw3_d = din("fw3", [64, 64])
    fw4_d = din("fw4", [64, 2 * CH])
    fvec_d = din("fvec", [64, 4])
    ndelta_d = din("ndelta", [128, 8])
    ident_d = din("ident", [128, 128], BF16)
    rotm_d = din("rotm", [128, 128], BF16)
    flags_d = din("flags", [128, 2])

    w_in = dscr("w_in_b", [D, INW])
    w_ao = dscr("w_ao_b", [CH, D])
    w_ho = dscr("w_ho_b", [CH, D])
    w_out = dscr("w_out_b", [D, D])
    w_up = dscr("w_up_b", [D, 2 * DFF])
    w_dn = dscr("w_dn_b", [DFF, D])

    jobs = []
    for ji, L in enumerate(Ls):
        J = Job()
        J.i = ji
        J.L = L
        J.Lq = L // 2
        J.NQ = J.Lq + 2
        J.N2 = L // 64
        N2 = J.N2
        s = "_%d" % ji
        J.x = din("x" + s, [L, D])
        J.cos = din("cos" + s, [128, L])
        J.sin = din("sin" + s, [128, L])
        J.zT = din("zT" + s, [33, 2 * L])
        J.trow = din("trow" + s, [128, 2 * L])
        J.F1u = din("F1u" + s, [64, 256], BF16)
        J.F1f = din("F1f" + s, [128, 256], BF16)
        J.F1ub = din("F1ub" + s, [64, 256], BF16)
        J.F1fb = din("F1fb" + s, [128, 256], BF16)
        J.G = din("G" + s, [128, N2, 3, N2], BF16)
        J.T1 = din("T1" + s, [N2, 4 * N2], BF16)
        J.Ginv = din("Ginv" + s, [128, N2, 2, 34], BF16)
        J.y = nc.dram_tensor("y" + s, [J.Lq, D], F32, kind="ExternalOutput").ap()
        J.qT = dscr("qT" + s, [NH, 128, J.NQ])
        J.kT = dscr("kT" + s, [NH, 128, L])
        J.V = dscr("V" + s, [L, CH])
        J.xh = dscr("xh" + s, [3 * CH, L])
        J.gates = dscr("gates" + s, [2 * D, J.NQ])
        J.hT = dscr("hT" + s, [CH, 2 * L])
        J.Fs = dscr("Fs" + s, [CH // 64, N2, 128, 2, 64])
        J.uT = dscr("uT" + s, [CH, L])
        J.x0c = dscr("x0c" + s, [CH, J.NQ])
        J.hy = dscr("hy" + s, [CH, J.NQ])
        J.attn = dscr("attn" + s, [CH, J.NQ])
        jobs.append(J)

    def galloc(shape, dt):
        return nc.alloc_sbuf_tensor(p.name("g"), list(shape), dt)

    ident = galloc([128, 128], BF16)
    rotm = galloc([128, 128], BF16)
    ones_bf = galloc([128, 128], BF16)
    ones_f = galloc([128, 128], F32)
    norm1 = galloc([128, 16], F32)
    norm2 = galloc([128, 16], F32)
    gateb = galloc([128, 32], F32)
    icw = galloc([128, 24, 3], F32)
    icb = galloc([128, 24], F32)
    fcw = galloc([128, 88, 3], F32)
    fcb = galloc([128, 88], F32)
    subg = galloc([128, 1], F32)
    flags = galloc([128, 2], F32)
    lamv = galloc([64, 4], F32)
    lam = galloc([128, 4], F32)
    cst = Tok()

    psall = nc.alloc_psum_tensor("psall", [128, 4096], F32)
    ps = [psall[:, i * 512:(i + 1) * 512] for i in range(8)]
    pst = [Tok(excl=True) for _ in range(8)]
    psb = ps[7].bitcast(BF16)
    psbt = pst[7]

    for dst, src in ((ident, ident_d), (rotm, rotm_d), (norm1, norm1_d), (norm2, norm2_d),
                     (gateb, gateb_d), (icb, icb_d), (fcb, fcb_d), (subg, subg_d),
                     (flags, flags_d), (lamv, lamv_d)):
        p.dma("sp", out=dst[:], in_=src[:, :], writes=[cst])
    p.dma("sp", out=icw[:], in_=icw_d[:, :, :], writes=[cst])
    p.dma("sp", out=fcw[:], in_=fcw_d[:, :, :], writes=[cst])
    p.op("dve", lambda e: e.memset(ones_bf[:], 1.0), writes=[cst])
    p.op("dve", lambda e: e.memset(ones_f[:], 1.0), writes=[cst])
    p.barrier()

    prod = galloc([64, 2], F32)
    p.op("dve", lambda e: e.tensor_tensor(out=prod[:, 0:1], in0=lamv[:, 0:1], in1=lamv[:, 1:2], op=ALU.mult), writes=[cst])
    p.op("dve", lambda e: e.tensor_tensor(out=prod[:, 1:2], in0=lamv[:, 2:3], in1=lamv[:, 3:4], op=ALU.mult), writes=[cst])
    p.barrier()
    p.op("pe", lambda e: e.matmul(ps[0][:, 0:2], ones_f[0:64, :], prod[:, :], start=True, stop=True), writes=[pst[0]])
    p.op("act", lambda e: e.activation(out=lam[:, 1:3], in_=ps[0][:, 0:2], func=AF.Exp), reads=[pst[0]], writes=[cst])
    p.barrier()
    p.op("dve", lambda e: e.scalar_tensor_tensor(out=lam[:, 0:1], in0=lam[:, 2:3], scalar=-LAMBDA_INIT,
                                                 in1=lam[:, 1:2], op0=ALU.add, op1=ALU.subtract), writes=[cst])
    p.barrier()

    G = Job()
    G.nc = nc
    G.p = p
    G.ps, G.pst, G.psb, G.psbt, G.psall = ps, pst, psb, psbt, psall
    G.ident, G.rotm, G.ones_bf, G.ones_f = ident, rotm, ones_bf, ones_f
    G.norm1, G.norm2, G.gateb, G.icw, G.icb, G.fcw, G.fcb = norm1, norm2, gateb, icw, icb, fcw, fcb
    G.subg, G.flags, G.lam, G.cst = subg, flags, lam, cst
    G.w_in, G.w_ao, G.w_ho, G.w_out, G.w_up, G.w_dn = w_in, w_ao, w_ho, w_out, w_up, w_dn
    G.normf_d, G.hbias_d = normf_d, hbias_d
    G.hbiasp_d = hbiasp_d
    G.fw = (fw1_d, fw2_d, fw3_d, fw4_d, fvec_d, ndelta_d)

    G.wes = ExitStack()
    G.wgen = phase_W(G, [(w_in_f, w_in, D, INW, norm1), (w_ao_f, w_ao, CH, D, None), (w_ho_f, w_ho, CH, D, None),
                         (w_out_f, w_out, D, D, None), (w_up_f, w_up, D, 2 * DFF, norm2), (w_dn_f, w_dn, DFF, D, None)], G.wes)
    pump(G, 2)
    import os as _os
    _ph = _os.environ.get("KPH", "FfABbCD")
    for J in jobs:
        for letter, fn in (("F", lambda: phase_F(G, J)), ("f", lambda: phase_fft(G, J, filt=True)), ("A", lambda: phase_A(G, J)),
                           ("B", lambda: phase_B1(G, J)), ("b", lambda: phase_fft(G, J, filt=False)), ("C", lambda: phase_C(G, J)),
                           ("D", lambda: phase_D(G, J))):
            if letter == "A" and G.wes is not None:
                pump(G, 10 ** 6)
                p.barrier()
                G.wes.close()
                G.wes = None
            if letter in _ph:
                fn()
                p.barrier()
    return nc


def mk_alloc(G, es):
    def alloc(shape, dt):
        return es.enter_context(G.nc.sbuf_tensor(G.p.name("s"), list(shape), dt))
    return alloc


def mk_ring(alloc, n, shape, dt):
    return Ring([alloc(shape, dt) for _ in range(n)])


def phase_W(G, items, es):
    p = G.p
    CW = 2048
    if True:
        A = mk_alloc(G, es)
        rf = mk_ring(A, 3, [128, CW], F32)
        rb = mk_ring(A, 3, [128, CW], BF16)
        n = 0
        for src, dst, K, N, gain in items:
            for kc in range(K // 128):
                for c0 in range(0, N, CW):
                    cw = min(CW, N - c0)
                    wf, tf = rf.next()
                    wb, tb = rb.next()
                    p.dma("sp", out=wf[:, :cw], in_=src[kc * 128:(kc + 1) * 128, c0:c0 + cw], writes=[tf])
                    eng = "act" if n % 2 == 0 else "dve"
                    n += 1
                    if gain is None:
                        if eng == "act":
                            p.op("act", lambda e: e.activation(out=wb[:, :cw], in_=wf[:, :cw], func=AF.Copy), [tf], [tb])
                        else:
                            p.op("dve", lambda e: e.tensor_copy(out=wb[:, :cw], in_=wf[:, :cw]), [tf], [tb])
                    else:
                        g = gain[:, kc:kc + 1]
                        if eng == "act":
                            p.op("act", lambda e: e.activation(out=wb[:, :cw], in_=wf[:, :cw], func=AF.Copy, scale=g), [tf, G.cst], [tb])
                        else:
                            p.op("dve", lambda e: e.tensor_scalar(out=wb[:, :cw], in0=wf[:, :cw], scalar1=g, scalar2=None, op0=ALU.mult), [tf, G.cst], [tb])
                    p.dma("pool", out=dst[kc * 128:(kc + 1) * 128, c0:c0 + cw], in_=wb[:, :cw], reads=[tb])
                    yield


def pump(G, k):
    g = getattr(G, "wgen", None)
    if g is None:
        return
    for _ in range(k):
        try:
            next(g)
        except StopIteration:
            G.wgen = None
            return


def rstd_ops(p, src, tmp, dst, tok, scale, extra_reads=()):
    p.op("dve", lambda e: e.tensor_scalar(out=tmp, in0=src, scalar1=scale, scalar2=EPS, op0=ALU.mult, op1=ALU.add),
         list(extra_reads) + [tok], [tok])
    p.op("act", lambda e: e.activation(out=tmp, in_=tmp, func=AF.Sqrt), [tok], [tok])
    p.op("dve", lambda e: e.reciprocal(out=dst, in_=tmp), [tok], [tok])


def copy_op(p, eng, out, in_, reads, writes=(), joins=()):
    if eng == "act":
        p.op("act", lambda e: e.activation(out=out, in_=in_, func=AF.Copy), reads, writes, joins)
    else:
        p.op(eng, lambda e: e.tensor_copy(out=out, in_=in_), reads, writes, joins)


def tt_op(p, eng, out, in0, in1, op, reads, writes=(), joins=()):
    p.op(eng, lambda e: e.tensor_tensor(out=out, in0=in0, in1=in1, op=op), reads, writes, joins)


def phase_F(G, J):
    p = G.p
    L = J.L
    fw1_d, fw2_d, fw3_d, fw4_d, fvec_d, ndelta_d = G.fw
    ps, pst = G.ps, G.pst
    with ExitStack() as es:
        A = mk_alloc(G, es)
        w1 = A([33, 64], F32)
        w2 = A([64, 64], F32)
        w3 = A([64, 64], F32)
        w4 = A([64, 2 * CH], F32)
        fv = A([64, 8], F32)
        nd = A([128, 8], F32)
        ct = Tok()
        for dst, src in ((w1, fw1_d), (w2, fw2_d), (w3, fw3_d), (w4, fw4_d), (nd, ndelta_d)):
            p.dma("sp", out=dst[:], in_=src[:, :], writes=[ct])
        p.dma("sp", out=fv[:, 0:4], in_=fvec_d[:, :], writes=[ct])
        for i in range(3):
            tt_op(p, "dve", fv[:, 4 + i:5 + i], fv[:, i:i + 1], fv[:, 3:4], ALU.mult, [ct], [ct])
        zr = mk_ring(A, 2, [33, 512], F32)
        tr = mk_ring(A, 2, [128, 512], F32)
        pre = mk_ring(A, 2, [64, 512], F32)
        sa = mk_ring(A, 2, [64, 512], F32)
        sb = mk_ring(A, 2, [64, 512], F32)
        hr = mk_ring(A, 3, [64, 512], F32)
        dec = mk_ring(A, 2, [128, 512], F32)
        ob = mk_ring(A, 3, [128, 512], BF16)
        wl = (w1, w2, w3)
        hbp = A([128, 8], F32)
        p.dma("sp", out=hbp[:], in_=G.hbiasp_d[:, :], writes=[ct])
        for c in range(2 * L // 512):
            pump(G, 4)
            z, zt = zr.next()
            p.dma("sp", out=z[:], in_=J.zT[:, c * 512:(c + 1) * 512], writes=[zt])
            trw, trt = tr.next()
            p.dma("sp", out=trw[:], in_=J.trow[:, c * 512:(c + 1) * 512], writes=[trt])
            hin, hint = z, zt
            kdim = 33
            for li in range(3):
                pi = p.nextps()
                w = wl[li]
                p.op("pe", lambda e: e.matmul(ps[pi][0:64, :], w[0:kdim, :], hin[0:kdim, :], start=True, stop=True),
                     [ct, hint], [pst[pi]])
                pr, prt = pre.next()
                p.op("dve", lambda e: e.tensor_scalar(out=pr[:], in0=ps[pi][0:64, :], scalar1=fv[:, 3:4], scalar2=fv[:, 4 + li:5 + li],
                                                      op0=ALU.mult, op1=ALU.add), [pst[pi], ct], [prt])
                a, at = sa.next()
                b, bt = sb.next()
                p.op("act", lambda e: e.activation(out=a[:], in_=pr[:], func=AF.Sin, scale=0.25), [prt], [at])
                p.op("act", lambda e: e.activation(out=b[:], in_=pr[:], func=AF.Sin, scale=0.125), [prt], [bt])
                tt_op(p, "dve", b[:], b[:], b[:], ALU.mult, [bt], [bt])
                p.op("dve", lambda e: e.tensor_scalar(out=b[:], in0=b[:], scalar1=-2.0, scalar2=1.0, op0=ALU.mult, op1=ALU.add), [bt], [bt])
                tt_op(p, "dve", b[:], b[:], a[:], ALU.mult, [bt, at], [bt])
                tt_op(p, "dve", a[:], a[:], a[:], ALU.mult, [at, bt], [at])
                p.op("dve", lambda e: e.tensor_scalar(out=a[:], in0=a[:], scalar1=-2.0, scalar2=1.0, op0=ALU.mult, op1=ALU.add), [at], [at])
                h, ht = hr.next()
                p.op("dve", lambda e: e.scalar_tensor_tensor(out=h[:], in0=b[:], scalar=4.0, in1=a[:], op0=ALU.mult, op1=ALU.mult),
                     [at, bt], [ht])
                hin, hint = h, ht
                kdim = 64
            bwd = (c >= L // 512)
            for blk in (range(8, 16) if bwd else range(8)):
                pi = p.nextps()
                p.op("pe", lambda e: e.matmul(ps[pi][:, :], w4[:, blk * 128:(blk + 1) * 128], hin[:, :], start=True, stop=True),
                     [ct, hint], [pst[pi]])
                d, dt_ = dec.next()
                p.op("act", lambda e: e.activation(out=d[:], in_=trw[:], func=AF.Exp, scale=nd[:, blk % 8:blk % 8 + 1]), [trt, ct], [dt_])
                o, ot = ob.next()
                tt_op(p, "dve", o[:], ps[pi][:, :], d[:], ALU.mult, [pst[pi], dt_], [ot])
                if bwd and c == L // 512:
                    p.op("dve", lambda e: e.memset(o[:, 0:1], 0.0), [], [ot])
                if c == 0:
                    p.op("dve", lambda e: e.scalar_tensor_tensor(out=o[:, 0:1], in0=ps[pi][:, 0:1], scalar=d[:, 0:1], in1=hbp[:, blk:blk + 1],
                                                                 op0=ALU.mult, op1=ALU.add), [pst[pi], dt_, ct], [ot])
                p.dma("pool", out=J.hT[(blk % 8) * 128:(blk % 8 + 1) * 128, c * 512:(c + 1) * 512], in_=o[:], reads=[ot])


def phase_fft(G, J, filt):
    p = G.p
    N2, Lq, NQ = J.N2, J.Lq, J.NQ
    ps, pst = G.ps, G.pst
    GC = 64
    KB = 512 // GC
    src = J.hT if filt else J.uT
    ngrp = CH // GC
    KR = 128 if filt else 64
    psall = G.psall
    with ExitStack() as es:
        A = mk_alloc(G, es)
        F1 = A([KR, 256], BF16)
        F1b = A([KR, 256], BF16)
        ct = Tok()
        p.dma("sp", out=F1[:], in_=(J.F1f if filt else J.F1u)[:, :], writes=[ct])
        p.dma("sp", out=F1b[:], in_=(J.F1fb if filt else J.F1ub)[:, :], writes=[ct])
        xr = mk_ring(A, 2, [KR, 32, N2], BF16)
        AC = A([128, GC * 2 * 128], BF16)
        AC2 = A([128, GC * 2 * 128], BF16)
        act_ = Tok()
        A_sb = AC[0:N2, :].rearrange("p (c r k) -> p c r k", c=GC, r=2)
        A_sb2 = AC2[0:N2, :].rearrange("p (c r k) -> p c r k", c=GC, r=2)
        gr = mk_ring(A, 3, [N2, KB, 2 * N2], BF16)
        obr = mk_ring(A, 3, [N2, KB, 2, GC], BF16)
        if not filt:
            C_sb = AC[:, 0:2 * GC * N2].rearrange("p (r c n) -> p r c n", r=2, c=GC)
            Y_sb = A([N2, 2, GC, 128], BF16)
            yt = Tok()
            T1 = A([N2, 4 * N2], BF16)
            Gi = A([128, N2, 2, 34], BF16)
            p.dma("sp", out=T1[:], in_=J.T1[:, :], writes=[ct])
            p.dma("sp", out=Gi[:], in_=J.Ginv[:, :, :, :], writes=[ct])
            fr = [mk_ring(A, 2, [N2, KB, 2, GC], BF16) for _ in range(2)]
            tf = [mk_ring(A, 2, [N2, KB, GC], F32) for _ in range(4)]
            bias = A([N2, KB, GC], F32)
            bt = Tok()
            x0r = mk_ring(A, 1, [GC, NQ], BF16)
            hyr = mk_ring(A, 1, [GC, NQ], BF16)
        s1 = 0
        s3 = 0
        for g in range(ngrp):
            ch0 = g * GC
            for half in range(GC // 32):
                x, xt = xr.next()
                p.dma("sp", out=x[:], in_=src[ch0 + half * 32:ch0 + half * 32 + 32, :].rearrange("c (a b) -> a c b", a=KR), writes=[xt])
                for c2 in range(16):
                    b1 = 4 + 2 * (s1 % 2)
                    s1 += 1

                    def mm(e):
                        for u in range(2):
                            e.matmul(ps[b1][0:N2, u * 256:(u + 1) * 256], x[:, c2 * 2 + u, :], F1[:, :], start=True, stop=True)
                        for u in range(2):
                            ins = e.matmul(ps[b1 + 1][0:N2, u * 256:(u + 1) * 256], x[:, c2 * 2 + u, :], F1b[:, :], start=True, stop=True)
                        return ins
                    p.op("pe", mm, [xt, ct], [pst[b1], pst[b1 + 1]])
                    cl = half * 32 + c2 * 2
                    copy_op(p, "act", A_sb[:, cl:cl + 2, :, :],
                            ps[b1][0:N2, :].rearrange("p (c r k) -> p c r k", c=2, r=2), [pst[b1]], joins=[act_])
                    copy_op(p, "dve", A_sb2[:, cl:cl + 2, :, :],
                            ps[b1 + 1][0:N2, :].rearrange("p (c r k) -> p c r k", c=2, r=2), [pst[b1 + 1]], joins=[act_])
            for kb in range(128 // KB):
                if filt:
                    pump(G, 1)
                gt, gtt = gr.next()
                p.dma("sp", out=gt[:].rearrange("n k (t m) -> n k t m", t=2),
                      in_=J.G[kb * KB:(kb + 1) * KB, :, 0:2, :].rearrange("k n t m -> n k t m"), writes=[gtt])
                b3 = 2 * (s3 % 2)
                s3 += 1

                def mm3(e):
                    for kk in range(KB):
                        k1 = kb * KB + kk
                        o = psall[0:N2, b3 * 512 + kk * 2 * GC:b3 * 512 + (kk + 1) * 2 * GC]
                        e.matmul(o, gt[:, kk, 0:N2], A_sb[:, :, :, k1].rearrange("p c r -> p r c"), start=True, stop=False)
                        ins = e.matmul(o, gt[:, kk, N2:2 * N2], A_sb2[:, :, :, k1].rearrange("p c r -> p r c"), start=False, stop=True)
                    return ins
                p.op("pe", mm3, [gtt, act_], [pst[b3], pst[b3 + 1]])
                X = psall[0:N2, b3 * 512:(b3 + 2) * 512].rearrange("p (k r c) -> p k r c", k=KB, r=2)
                xr_v = X[:, :, 0, :]
                xi_v = X[:, :, 1, :]
                xtoks = [pst[b3], pst[b3 + 1]]
                k1a = kb * KB
                if filt:
                    o, ot = obr.next()
                    copy_op(p, "act" if kb % 2 == 0 else "dve", o[:], X, xtoks, [ot])
                    p.dma("pool", out=J.Fs[g, :, k1a:k1a + KB, :, :], in_=o[:], reads=[ot])
                else:
                    ff, fft_ = fr[0].next()
                    p.dma("sp", out=ff[:], in_=J.Fs[g, :, k1a:k1a + KB, :, :], writes=[fft_])
                    hr_, hi_ = ff[:, :, 0, :], ff[:, :, 1, :]
                    hrt = hit = fft_
                    t1, t1t = tf[2].next()
                    t2, t2t = tf[3].next()
                    yr_o = Y_sb[:, 0, :, k1a:k1a + KB].rearrange("p c k -> p k c")
                    yi_o = Y_sb[:, 1, :, k1a:k1a + KB].rearrange("p c k -> p k c")
                    tt_op(p, "dve", t1[:], xr_v, hr_, ALU.mult, xtoks + [hrt], [t1t])
                    tt_op(p, "dve", t2[:], xi_v, hi_, ALU.mult, xtoks + [hit], [t2t])
                    tt_op(p, "pool", yr_o, t1[:], t2[:], ALU.subtract, [t1t, t2t], joins=[yt])
                    t1, t1t = tf[2].next()
                    t2, t2t = tf[3].next()
                    tt_op(p, "dve", t1[:], xr_v, hi_, ALU.mult, xtoks + [hit], [t1t])
                    tt_op(p, "dve", t2[:], xi_v, hr_, ALU.mult, xtoks + [hrt], [t2t])
                    tt_op(p, "pool", yi_o, t1[:], t2[:], ALU.add, [t1t, t2t], joins=[yt])
            if filt:
                continue
            cpb = 512 // (2 * N2)
            for c0 in range(0, GC, cpb):
                pi = p.nextps()

                def mmi(e):
                    for u in range(cpb):
                        c = c0 + u
                        e.matmul(ps[pi][:, u * 2 * N2:(u + 1) * 2 * N2], Y_sb[:, 0, c, :], T1[:, 0:2 * N2], start=True, stop=False)
                        ins = e.matmul(ps[pi][:, u * 2 * N2:(u + 1) * 2 * N2], Y_sb[:, 1, c, :], T1[:, 2 * N2:4 * N2], start=False, stop=True)
                    return ins
                p.op("pe", mmi, [yt, ct, act_], [pst[pi]])
                copy_op(p, "act" if (c0 // cpb) % 2 == 0 else "dve", C_sb[:, :, c0:c0 + cpb, :],
                        ps[pi][:, 0:cpb * 2 * N2].rearrange("p (c r n) -> p r c n", c=cpb, r=2), [pst[pi]], joins=[act_])
            x0, x0t = x0r.next()
            hy, hyt = hyr.next()
            p.dma("sp", out=x0[:], in_=J.x0c[ch0:ch0 + GC, :], writes=[x0t])
            own_o = hy[:, 1:1 + Lq].rearrange("p (a b) -> p a b", b=N2)
            own_x = x0[:, 1:1 + Lq].rearrange("p (a b) -> p a b", b=N2)
            first = True
            for nb in range(0, N2, 16):
                pi = p.nextps()

                def mm2(e):
                    for u in range(16):
                        n2 = nb + u
                        e.matmul(ps[pi][0:GC, u * 32:(u + 1) * 32], C_sb[:, 0, :, n2], Gi[:, n2, 0, 1:33], start=True, stop=False)
                        ins = e.matmul(ps[pi][0:GC, u * 32:(u + 1) * 32], C_sb[:, 1, :, n2], Gi[:, n2, 1, 1:33], start=False, stop=True)
                    return ins
                p.op("pe", mm2, [act_, ct], [pst[pi]])
                w_ = [hyt] if first else []
                j_ = [] if first else [hyt]
                first = False
                tt_op(p, "dve", own_o[:, :, nb:nb + 16], ps[pi][0:GC, :].rearrange("p (b a) -> p a b", a=32),
                      own_x[:, :, nb:nb + 16], ALU.mult, [pst[pi], x0t], w_, j_)
            pi = p.nextps()

            def mmh(e):
                e.matmul(ps[pi][0:GC, 0:1], C_sb[:, 0, :, N2 - 1], Gi[:, N2 - 1, 0, 0:1], start=True, stop=False)
                e.matmul(ps[pi][0:GC, 0:1], C_sb[:, 1, :, N2 - 1], Gi[:, N2 - 1, 1, 0:1], start=False, stop=True)
                e.matmul(ps[pi][0:GC, 1:2], C_sb[:, 0, :, 0], Gi[:, 0, 0, 33:34], start=True, stop=False)
                return e.matmul(ps[pi][0:GC, 1:2], C_sb[:, 1, :, 0], Gi[:, 0, 1, 33:34], start=False, stop=True)
            p.op("pe", mmh, [act_, ct], [pst[pi]])
            tt_op(p, "dve", hy[:, 0:1], ps[pi][0:GC, 0:1], x0[:, 0:1], ALU.mult, [pst[pi], x0t], joins=[hyt])
            tt_op(p, "dve", hy[:, NQ - 1:NQ], ps[pi][0:GC, 1:2], x0[:, NQ - 1:NQ], ALU.mult, [pst[pi], x0t], joins=[hyt])
            p.dma("pool", out=J.hy[ch0:ch0 + GC, :], in_=hy[:], reads=[hyt])


def phase_A(G, J):
    p = G.p
    L, Lq, NQ = J.L, J.Lq, J.NQ
    ps, pst = G.ps, G.pst
    nchunks = L // 512
    with ExitStack() as es:
        A = mk_alloc(G, es)
        xr = mk_ring(A, 2, [128, D], F32)
        junk = mk_ring(A, 2, [128, D], BF16)
        hnr = mk_ring(A, 2, [128, D], BF16)
        st = mk_ring(A, 4, [128, 4], F32)
        hnT = [A([128, 16, 512], BF16) for _ in range(2)]
        hnT_t = [[Tok() for _ in range(8)] for _ in range(2)]
        wr = mk_ring(A, 2, [128, 16, 512], BF16)
        cosr = mk_ring(A, 2, [128, 512], F32)
        sinr = mk_ring(A, 2, [128, 512], F32)
        qraw = mk_ring(A, 2, [128, 512], BF16)
        t1r = mk_ring(A, 2, [128, 512], F32)
        t2r = mk_ring(A, 2, [128, 512], F32)
        osb = mk_ring(A, 4, [128, 512], BF16)
        for c in range(nchunks):
            s = c % 2
            own = (c * 512 < Lq)
            for i in range(4):
                r0 = c * 512 + i * 128
                xt, xtt = xr.next()
                p.dma("sp", out=xt[:], in_=J.x[r0:r0 + 128, :], writes=[xtt])
                jk, jkt = junk.next()
                sv, svt = st.next()
                p.op("dve", lambda e: e.memset(sv[:, 0:1], 0.0), [], [svt])
                p.op("act", lambda e: e.activation(out=jk[:], in_=xt[:], func=AF.Square, accum_out=sv[:, 0:1]), [xtt], [jkt, svt])
                rstd_ops(p, sv[:, 0:1], sv[:, 1:2], sv[:, 2:3], svt, 1.0 / D)
                hn, hnt = hnr.next()
                p.op("act", lambda e: e.activation(out=hn[:], in_=xt[:], func=AF.Copy, scale=sv[:, 2:3]), [xtt, svt], [hnt])
                for half in range(2):
                    def tr(e):
                        for k in range(8):
                            kc = half * 8 + k
                            ins = e.transpose(G.psb[:, k * 128:(k + 1) * 128], hn[:, kc * 128:(kc + 1) * 128], G.ident[:])
                        return ins
                    p.op("pe", tr, [hnt, G.cst], [G.psbt])
                    copy_op(p, "dve" if half == 0 else "act", hnT[s][:, half * 8:(half + 1) * 8, i * 128:(i + 1) * 128],
                            G.psb[:, :].rearrange("p (k t) -> p k t", k=8), [G.psbt], [hnT_t[s][i * 2 + half]])
            cs, cst_ = cosr.next()
            sn, snt = sinr.next()
            p.dma("sp", out=cs[:], in_=J.cos[:, c * 512:(c + 1) * 512], writes=[cst_])
            p.dma("sp", out=sn[:], in_=J.sin[:, c * 512:(c + 1) * 512], writes=[snt])

            import os as _os
            _ka = _os.environ.get("KA", "")

            def qstore(dst_fn, ob, obt):
                if "c" in _ka:
                    if own:
                        p.dma("pool", out=dst_fn(1 + c * 512, 1 + (c + 1) * 512), in_=ob[:], reads=[obt])
                    return
                if own:
                    p.dma("pool", out=dst_fn(1 + c * 512, 1 + (c + 1) * 512), in_=ob[:], reads=[obt])
                if c * 512 == Lq:
                    p.dma("pool", out=dst_fn(NQ - 1, NQ), in_=ob[:, 0:1], reads=[obt])
                if c == nchunks - 1:
                    p.dma("pool", out=dst_fn(0, 1), in_=ob[:, 511:512], reads=[obt])

            first_other = (c * 512 == Lq)
            last_chunk = (c == nchunks - 1)
            need_q = own or first_other or last_chunk
            for b in range(20):
                qblk = (b < 2 or b >= 12 or b in (6, 7))
                if not need_q and qblk:
                    continue
                if ("r" in _ka and b < 4) or ("v" in _ka and b in (4, 5)) or ("h" in _ka and 6 <= b < 12) or ("g" in _ka and b >= 12):
                    continue
                lo, hi = 0, 512
                if qblk and not own:
                    if first_other and not last_chunk:
                        lo, hi = 0, 2
                    elif last_chunk and not first_other:
                        lo, hi = 510, 512
                wb, wbt = wr.next()
                p.dma("sp", out=wb[:], in_=G.w_in[:, b * 512:(b + 1) * 512].rearrange("(kc q) n -> q kc n", q=128), writes=[wbt])
                if b in (4, 5):
                    for i in range(4):
                        pi = p.nextps()

                        def mmv(e):
                            for kc in range(16):
                                ins = e.matmul(ps[pi][:, :], hnT[s][:, kc, i * 128:(i + 1) * 128], wb[:, kc, :],
                                               start=(kc == 0), stop=(kc == 15))
                            return ins
                        p.op("pe", mmv, [wbt] + hnT_t[s], [pst[pi]])
                        ob, obt = osb.next()
                        copy_op(p, "act", ob[:], ps[pi][:, :], [pst[pi]], [obt])
                        r0 = c * 512 + i * 128
                        p.dma("pool", out=J.V[r0:r0 + 128, (b - 4) * 512:(b - 3) * 512], in_=ob[:], reads=[obt])
                    continue
                for j in range(4):
                    g = b * 4 + j
                    pi = p.nextps()

                    def mmf(e):
                        for kc in range(16):
                            ins = e.matmul(ps[pi][:, lo:hi], wb[:, kc, j * 128:(j + 1) * 128], hnT[s][:, kc, lo:hi],
                                           start=(kc == 0), stop=(kc == 15))
                        return ins
                    p.op("pe", mmf, [wbt] + hnT_t[s], [pst[pi]])
                    ob, obt = osb.next()
                    if b < 4:
                        head = g % 8
                        qr, qrt = qraw.next()
                        copy_op(p, "act", qr[:, lo:hi], ps[pi][:, lo:hi], [pst[pi]], [qrt])
                        pr = p.nextps()
                        p.op("pe", lambda e: e.matmul(ps[pr][:, lo:hi], G.rotm[:], qr[:, lo:hi], start=True, stop=True), [qrt, G.cst], [pst[pr]])
                        t1, t1t = t1r.next()
                        t2, t2t = t2r.next()
                        tt_op(p, "dve", t1[:, lo:hi], ps[pi][:, lo:hi], cs[:, lo:hi], ALU.mult, [pst[pi], cst_], [t1t])
                        tt_op(p, "dve", t2[:, lo:hi], ps[pr][:, lo:hi], sn[:, lo:hi], ALU.mult, [pst[pr], snt], [t2t])
                        tt_op(p, "dve" if "P" in _ka else "pool", ob[:, lo:hi], t1[:, lo:hi], t2[:, lo:hi], ALU.add, [t1t, t2t], [obt])
                        if b >= 2:
                            p.dma("pool", out=J.kT[head, :, c * 512:(c + 1) * 512], in_=ob[:], reads=[obt])
                        else:
                            qstore(lambda a, bb: J.qT[head, :, a:bb], ob, obt)
                    elif b < 12:
                        copy_op(p, "dve", ob[:, lo:hi], ps[pi][:, lo:hi], [pst[pi]], [obt])
                        r0 = (g - 24) * 128
                        p.dma("pool", out=J.xh[r0:r0 + 128, c * 512 + lo:c * 512 + hi], in_=ob[:, lo:hi], reads=[obt])
                    else:
                        gi = g - 48
                        p.op("act", lambda e: e.activation(out=ob[:, lo:hi], in_=ps[pi][:, lo:hi], func=AF.Sigmoid, bias=G.gateb[:, gi:gi + 1]),
                             [pst[pi], G.cst], [obt])
                        qstore(lambda a, bb: J.gates[gi * 128:(gi + 1) * 128, a:bb], ob, obt)


def phase_B1(G, J):
    p = G.p
    L, Lq, NQ = J.L, J.Lq, J.NQ
    SEG = min(2048, Lq)
    m_wrap = G.flags[:, 0:1]
    m_mid = G.flags[:, 1:2]
    with ExitStack() as es:
        A = mk_alloc(G, es)
        xin = mk_ring(A, 3, [128, SEG + 2], BF16)
        cv = mk_ring(A, 3, [128, SEG], F32)
        ub = mk_ring(A, 2, [128, SEG], BF16)
        hx = mk_ring(A, 2, [128, 8], BF16)
        ho = mk_ring(A, 2, [128, 4], F32)
        hb = mk_ring(A, 2, [128, 2], BF16)

        def conv_seg(row0, gi, s0, pm, nm):
            xi, xit = xin.next()
            pv = (s0 - 1) % L
            nx = (s0 + SEG) % L
            p.dma("sp", out=xi[:, 1:SEG + 1], in_=J.xh[row0:row0 + 128, s0:s0 + SEG], writes=[xit])
            p.dma("sp", out=xi[:, 0:1], in_=J.xh[row0:row0 + 128, pv:pv + 1], joins=[xit])
            p.dma("sp", out=xi[:, SEG + 1:SEG + 2], in_=J.xh[row0:row0 + 128, nx:nx + 1], joins=[xit])
            if pm is not None:
                p.op("dve", lambda e: e.tensor_scalar(out=xi[:, 0:1], in0=xi[:, 0:1], scalar1=pm, scalar2=None, op0=ALU.mult),
                     [xit, G.cst], [xit])
            if nm is not None:
                p.op("dve", lambda e: e.tensor_scalar(out=xi[:, SEG + 1:SEG + 2], in0=xi[:, SEG + 1:SEG + 2], scalar1=nm, scalar2=None,
                                                      op0=ALU.mult), [xit, G.cst], [xit])
            c_, ct_ = cv.next()
            w0 = G.icw[:, gi, 0:1]
            w1 = G.icw[:, gi, 1:2]
            w2 = G.icw[:, gi, 2:3]
            bb = G.icb[:, gi:gi + 1]
            p.op("act", lambda e: e.activation(out=c_[:], in_=xi[:, 1:SEG + 1], func=AF.Identity, scale=w1, bias=bb),
                 [xit, G.cst], [ct_])
            p.op("dve", lambda e: e.scalar_tensor_tensor(out=c_[:], in0=xi[:, 0:SEG], scalar=w0, in1=c_[:], op0=ALU.mult, op1=ALU.add),
                 [xit, G.cst], [ct_])
            p.op("dve", lambda e: e.scalar_tensor_tensor(out=c_[:], in0=xi[:, 2:SEG + 2], scalar=w2, in1=c_[:], op0=ALU.mult, op1=ALU.add),
                 [xit, G.cst], [ct_])
            return c_, ct_

        def masks(s0):
            pm = m_wrap if s0 == 0 else (m_mid if s0 == Lq else None)
            e_ = s0 + SEG
            nm = m_wrap if e_ == L else (m_mid if e_ == Lq else None)
            return pm, nm

        for g in range(8):
            for s0 in range(0, L, SEG):
                pm, nm = masks(s0)
                c1, c1t = conv_seg(CH + g * 128, 8 + g, s0, pm, nm)
                c2, c2t = conv_seg(2 * CH + g * 128, 16 + g, s0, pm, nm)
                u, ut = ub.next()
                tt_op(p, "pool", u[:], c1[:], c2[:], ALU.mult, [c1t, c2t], [ut])
                p.dma("pool", out=J.uT[g * 128:(g + 1) * 128, s0:s0 + SEG], in_=u[:], reads=[ut])
            for s0 in range(0, Lq, SEG):
                pm, nm = masks(s0)
                c0, c0t = conv_seg(g * 128, g, s0, pm, nm)
                u, ut = ub.next()
                copy_op(p, "act", u[:], c0[:], [c0t], [ut])
                p.dma("pool", out=J.x0c[g * 128:(g + 1) * 128, 1 + s0:1 + s0 + SEG], in_=u[:], reads=[ut])
            h, ht = hx.next()
            p.dma("sp", out=h[:, 0:2], in_=J.xh[g * 128:(g + 1) * 128, L - 2:L], writes=[ht])
            p.dma("sp", out=h[:, 2:3], in_=J.xh[g * 128:(g + 1) * 128, 0:1], joins=[ht])
            p.dma("sp", out=h[:, 3:6], in_=J.xh[g * 128:(g + 1) * 128, Lq - 1:Lq + 2], joins=[ht])
            o, ot = ho.next()
            w0 = G.icw[:, g, 0:1]
            w1 = G.icw[:, g, 1:2]
            w2 = G.icw[:, g, 2:3]
            bb = G.icb[:, g:g + 1]
            hv_ = h[:, 0:6].rearrange("p (a b) -> p a b", b=3)
            p.op("dve", lambda e: e.tensor_scalar(out=o[:, 0:2], in0=hv_[:, :, 1], scalar1=w1, scalar2=bb, op0=ALU.mult, op1=ALU.add),
                 [ht, G.cst], [ot])
            p.op("dve", lambda e: e.scalar_tensor_tensor(out=o[:, 0:2], in0=hv_[:, :, 0], scalar=w0, in1=o[:, 0:2], op0=ALU.mult, op1=ALU.add),
                 [ht, G.cst], [ot])
            p.op("dve", lambda e: e.scalar_tensor_tensor(out=o[:, 0:2], in0=hv_[:, :, 2], scalar=w2, in1=o[:, 0:2], op0=ALU.mult, op1=ALU.add),
                 [ht, G.cst], [ot])
            ob_, obt_ = hb.next()
            copy_op(p, "dve", ob_[:], o[:, 0:2], [ot], [obt_])
            p.dma("pool", out=J.x0c[g * 128:(g + 1) * 128, 0:1], in_=ob_[:, 0:1], reads=[obt_])
            p.dma("pool", out=J.x0c[g * 128:(g + 1) * 128, NQ - 1:NQ], in_=ob_[:, 1:2], reads=[obt_])


def phase_C(G, J):
    p = G.p
    L, Lq, NQ = J.L, J.Lq, J.NQ
    ps, pst = G.ps, G.pst
    nkt = L // 128
    qchunks = [(a, min(a + 512, NQ)) for a in range(0, NQ, 512)]
    scale = 64.0 ** -0.5
    nlam = G.lam[:, 0:1]
    with ExitStack() as es:
        A = mk_alloc(G, es)
        kr = mk_ring(A, 2, [128, L], BF16)
        vr = mk_ring(A, 2, [128, nkt, 128], BF16)
        qr = mk_ring(A, 2, [128, NQ], BF16)
        er = mk_ring(A, 3, [128, 2, 512], BF16)
        rz = mk_ring(A, 2, [128, 512], F32)
        rbr = mk_ring(A, 2, [128, 512], F32)
        sel = A([64, 256], F32)
        selt = Tok()
        p.op("dve", lambda e: e.memset(sel[:], 0.0), [], [selt])
        p.op("dve", lambda e: e.memset(sel[0:32, 0:128], 1.0 / 32), [], [selt])
        p.op("dve", lambda e: e.memset(sel[32:64, 128:256], 1.0 / 32), [], [selt])
        a0r = mk_ring(A, 2, [128, 512], F32)
        a1r = mk_ring(A, 2, [128, 512], F32)
        sqr = mk_ring(A, 2, [128, 512], BF16)
        rsr = mk_ring(A, 2, [128, 512], F32)
        obr = mk_ring(A, 2, [128, 512], BF16)
        psall = G.psall
        for h in range(NH):
            k, kt_ = kr.next()
            p.dma("sp", out=k[:], in_=J.kT[h, :, :], writes=[kt_])
            v, vt = vr.next()
            p.dma("sp", out=v[:], in_=J.V[:, h * 128:(h + 1) * 128].rearrange("(t q) d -> q t d", q=128), writes=[vt])
            q, qt = qr.next()
            p.dma("sp", out=q[:], in_=J.qT[h, :, :], writes=[qt])
            for (a, b) in qchunks:
                n = b - a

                def s_pair(kt, slot):
                    b0 = 4 + 2 * slot

                    def mm(e):
                        e.matmul(ps[b0][:, 0:n], k[0:64, kt * 128:(kt + 1) * 128], q[0:64, a:b], start=True, stop=True)
                        return e.matmul(ps[b0 + 1][:, 0:n], k[64:128, kt * 128:(kt + 1) * 128], q[64:128, a:b], start=True, stop=True)
                    p.op("pe", mm, [kt_, qt], [pst[b0], pst[b0 + 1]])
                    e_, et = er.next()
                    src = psall[:, b0 * 512:(b0 + 2) * 512].rearrange("p (m n) -> p m n", m=2)[:, :, 0:n]
                    p.op("act", lambda e: e.activation(out=e_[:, :, 0:n], in_=src, func=AF.Exp, scale=scale),
                         [pst[b0], pst[b0 + 1]], [et])
                    return e_, et

                def av_mm(kt, e_, et):
                    def mm(e):
                        for m in range(2):
                            e.matmul(ps[m][:, 0:n], v[:, kt, :], e_[:, m, 0:n], start=(kt == 0), stop=(kt == nkt - 1))
                        for m in range(2):
                            ins = e.matmul(ps[2][32 * m:32 * m + 32, 0:n], G.ones_bf[:, 0:32], e_[:, m, 0:n],
                                           start=(kt == 0), stop=(kt == nkt - 1), tile_position=(0, 32 * m))
                        return ins
                    if kt == 0:
                        p.op("pe", mm, [vt, et, G.cst], [pst[0], pst[1], pst[2]])
                    else:
                        p.op("pe", mm, [vt, et, G.cst], [], [pst[0], pst[1], pst[2]])

                cur = s_pair(0, 0)
                for kt in range(nkt):
                    nxt = s_pair(kt + 1, (kt + 1) % 2) if kt + 1 < nkt else None
                    av_mm(kt, *cur)
                    cur = nxt
                r_, rt = rz.next()
                a0, a0t = a0r.next()
                a1, a1t = a1r.next()
                p.op("dve", lambda e: e.reciprocal(out=r_[0:64, 0:n], in_=ps[2][0:64, 0:n]), [pst[2]], [rt])
                for m, (am, amt) in enumerate(((a0, a0t), (a1, a1t))):
                    pb = 3 if m == 0 else 4
                    p.op("pe", lambda e: e.matmul(ps[pb][:, 0:n], sel[:, m * 128:(m + 1) * 128], r_[0:64, 0:n], start=True, stop=True),
                         [rt, selt], [pst[pb]])
                    rb, rbt = rbr.next()
                    copy_op(p, "act", rb[:, 0:n], ps[pb][:, 0:n], [pst[pb]], [rbt])
                    tt_op(p, "dve", am[:, 0:n], ps[m][:, 0:n], rb[:, 0:n], ALU.mult, [pst[m], rbt], [amt])
                p.op("dve", lambda e: e.scalar_tensor_tensor(out=a0[:, 0:n], in0=a1[:, 0:n], scalar=nlam, in1=a0[:, 0:n],
                                                              op0=ALU.mult, op1=ALU.add), [a1t, G.cst], [a0t])
                sq, sqt = sqr.next()
                tt_op(p, "pool", sq[:, 0:n], a0[:, 0:n], a0[:, 0:n], ALU.mult, [a0t], [sqt])
                si = 4
                p.op("pe", lambda e: e.matmul(ps[si][:, 0:n], G.ones_bf[:], sq[:, 0:n], start=True, stop=True), [sqt, G.cst], [pst[si]])
                rs, rst = rsr.next()
                rstd_ops(p, ps[si][:, 0:n], rs[:, 0:n], rs[:, 0:n], rst, 1.0 / 128, extra_reads=[pst[si]])
                tt_op(p, "pool", a0[:, 0:n], a0[:, 0:n], rs[:, 0:n], ALU.mult, [rst], [a0t])
                o, ot = obr.next()
                p.op("dve", lambda e: e.tensor_scalar(out=o[:, 0:n], in0=a0[:, 0:n], scalar1=G.subg[:, 0:1], scalar2=1.0 - LAMBDA_INIT,
                                                      op0=ALU.mult, op1=ALU.mult), [a0t, G.cst], [ot])
                p.dma("pool", out=J.attn[h * 128:(h + 1) * 128, a:b], in_=o[:, 0:n], reads=[ot])


def phase_D(G, J):
    p = G.p
    L, Lq, NQ = J.L, J.Lq, J.NQ
    ps, pst = G.ps, G.pst
    chunks = []
    p0 = 0
    while p0 < Lq:
        chunks.append((p0, min(QW, Lq - p0) + 2))
        p0 += QW
    with ExitStack() as es:
        A = mk_alloc(G, es)
        raw = A([128, 44 * 512], BF16)
        rawt = Tok()
        gT = raw[:, :].rearrange("p (k n) -> p k n", k=44)
        at_ = raw[:, 0:8 * 512].rearrange("p (k n) -> p k n", k=8)
        hy_ = raw[:, 8 * 512:16 * 512].rearrange("p (k n) -> p k n", k=8)
        mg = raw[:, 16 * 512:32 * 512].rearrange("p (k n) -> p k n", k=16)
        g0r = mk_ring(A, 2, [128, 512], BF16)
        g1r = mk_ring(A, 2, [128, 512], BF16)
        m0r = mk_ring(A, 1, [128, 512], F32)
        m1r = mk_ring(A, 1, [128, 512], F32)
        x1 = A([128, 4, D], F32)
        x1t = [Tok() for _ in range(4)]
        hnr = mk_ring(A, 2, [128, D], BF16)
        st = mk_ring(A, 4, [128, 4], F32)
        hn2T = A([128, 16, 512], BF16)
        hn2t = Tok()
        wr = mk_ring(A, 3, [128, 16 * 512], BF16)
        ugr = mk_ring(A, 2, [128, 512], F32)
        uvr = mk_ring(A, 2, [128, 512], F32)
        sgr = mk_ring(A, 2, [128, 512], F32)
        normf = A([128, D], F32)
        w0m = A([128, 88], F32)
        w2m = A([128, 88], F32)
        ct = Tok()
        p.dma("sp", out=normf[:], in_=G.normf_d[:, :], writes=[ct])
        p.op("dve", lambda e: e.tensor_scalar(out=w0m[:], in0=G.fcw[:, :, 0], scalar1=G.flags[:, 0:1], scalar2=None, op0=ALU.mult),
             [G.cst], [ct])
        p.op("dve", lambda e: e.tensor_scalar(out=w2m[:], in0=G.fcw[:, :, 2], scalar1=G.flags[:, 1:2], scalar2=None, op0=ALU.mult),
             [G.cst], [ct])

        for ci, (p0, n) in enumerate(chunks):
            first_c = (ci == 0)
            last_c = (ci == len(chunks) - 1)
            ntile = (n + 127) // 128
            tns = [min(128, n - 128 * i) for i in range(ntile)]
            p.dma("sp", out=at_[:, :, 0:n], in_=J.attn[:, p0:p0 + n].rearrange("(g q) n -> q g n", q=128), writes=[rawt])
            p.dma("sp", out=hy_[:, :, 0:n], in_=J.hy[:, p0:p0 + n].rearrange("(g q) n -> q g n", q=128), joins=[rawt])
            for i in range(ntile):
                tn = tns[i]
                pr0 = p0 + 128 * i
                if pr0 == 0:
                    p.dma("sp", out=x1[0:1, i, :], in_=J.x[L - 1:L, :], writes=[x1t[i]])
                    if tn > 1:
                        p.dma("sp", out=x1[1:tn, i, :], in_=J.x[0:tn - 1, :], joins=[x1t[i]])
                else:
                    p.dma("sp", out=x1[0:tn, i, :], in_=J.x[pr0 - 1:pr0 - 1 + tn, :], writes=[x1t[i]])
            for blk in range(4):
                wa, wat = wr.next()
                wh, wht = wr.next()
                wa_v = wa[:, 0:8 * 512].rearrange("p (k n) -> p k n", k=8)
                wh_v = wh[:, 0:8 * 512].rearrange("p (k n) -> p k n", k=8)
                p.dma("sp", out=wa_v, in_=G.w_ao[:, blk * 512:(blk + 1) * 512].rearrange("(kc q) n -> q kc n", q=128), writes=[wat])
                p.dma("sp", out=wh_v, in_=G.w_ho[:, blk * 512:(blk + 1) * 512].rearrange("(kc q) n -> q kc n", q=128), writes=[wht])
                for j in range(4):
                    og = blk * 4 + j
                    g0, g0t = g0r.next()
                    g1, g1t = g1r.next()
                    p.dma("sp", out=g0[:, 0:n], in_=J.gates[og * 128:(og + 1) * 128, p0:p0 + n], writes=[g0t])
                    p.dma("sp", out=g1[:, 0:n], in_=J.gates[D + og * 128:D + (og + 1) * 128, p0:p0 + n], writes=[g1t])
                    pa, pb = p.nextps(), p.nextps()

                    def mma(e):
                        for kc in range(8):
                            e.matmul(ps[pa][:, 0:n], wa_v[:, kc, j * 128:(j + 1) * 128], at_[:, kc, 0:n], start=(kc == 0), stop=(kc == 7))
                        for kc in range(8):
                            ins = e.matmul(ps[pb][:, 0:n], wh_v[:, kc, j * 128:(j + 1) * 128], hy_[:, kc, 0:n], start=(kc == 0), stop=(kc == 7))
                        return ins
                    p.op("pe", mma, [wat, wht, rawt], [pst[pa], pst[pb]])
                    m0, m0t = m0r.next()
                    m1, m1t = m1r.next()
                    tt_op(p, "dve", m0[:, 0:n], ps[pa][:, 0:n], g0[:, 0:n], ALU.mult, [pst[pa], g0t], [m0t])
                    tt_op(p, "dve", m1[:, 0:n], ps[pb][:, 0:n], g1[:, 0:n], ALU.mult, [pst[pb], g1t], [m1t])
                    tt_op(p, "pool", mg[:, og, 0:n], m0[:, 0:n], m1[:, 0:n], ALU.add, [m0t, m1t], joins=[rawt])
            for cb in range(4):
                wb, wbt = wr.next()
                wb_v = wb[:, :].rearrange("p (k n) -> p k n", k=16)
                p.dma("sp", out=wb_v, in_=G.w_out[:, cb * 512:(cb + 1) * 512].rearrange("(kc q) n -> q kc n", q=128), writes=[wbt])
                for i in range(ntile):
                    tn = tns[i]
                    pi = p.nextps()

                    def mmo(e):
                        for kc in range(16):
                            ins = e.matmul(ps[pi][0:tn, :], mg[:, kc, i * 128:i * 128 + tn], wb_v[:, kc, :], start=(kc == 0), stop=(kc == 15))
                        return ins
                    p.op("pe", mmo, [wbt, rawt], [pst[pi]])
                    dst = x1[0:tn, i, cb * 512:(cb + 1) * 512]
                    tt_op(p, "dve", dst, ps[pi][0:tn, :], dst, ALU.add, [pst[pi]], [x1t[i]])
            for i in range(ntile):
                tn = tns[i]
                hn, hnt = hnr.next()
                sv, svt = st.next()
                p.op("dve", lambda e: e.memset(sv[:, 0:1], 0.0), [], [svt])
                p.op("act", lambda e: e.activation(out=hn[0:tn, :], in_=x1[0:tn, i, :], func=AF.Square, accum_out=sv[0:tn, 0:1]),
                     [x1t[i]], [hnt, svt])
                rstd_ops(p, sv[0:tn, 0:1], sv[0:tn, 1:2], sv[0:tn, 2:3], svt, 1.0 / D)
                p.op("act", lambda e: e.activation(out=hn[0:tn, :], in_=x1[0:tn, i, :], func=AF.Copy, scale=sv[0:tn, 2:3]),
                     [x1t[i], svt], [hnt])
                for half in range(2):
                    def tr(e):
                        for k in range(8):
                            kc = half * 8 + k
                            ins = e.transpose(G.psb[:, k * 128:k * 128 + tn], hn[0:tn, kc * 128:(kc + 1) * 128], G.ident[0:tn, 0:tn])
                        return ins
                    p.op("pe", tr, [hnt, G.cst], [G.psbt])
                    copy_op(p, "dve" if half == 0 else "act", hn2T[:, half * 8:(half + 1) * 8, i * 128:i * 128 + tn],
                            G.psb[:, :].rearrange("p (k t) -> p k t", k=8)[:, :, 0:tn], [G.psbt],
                            [hn2t] if (i == 0 and half == 0) else [], [] if (i == 0 and half == 0) else [hn2t])
            for bg in range(11):
                wg, wgt = wr.next()
                wv, wvt = wr.next()
                wg_v = wg[:, :].rearrange("p (k n) -> p k n", k=16)
                wv_v = wv[:, :].rearrange("p (k n) -> p k n", k=16)
                p.dma("sp", out=wg_v, in_=G.w_up[:, bg * 512:(bg + 1) * 512].rearrange("(kc q) n -> q kc n", q=128), writes=[wgt])
                p.dma("sp", out=wv_v, in_=G.w_up[:, DFF + bg * 512:DFF + (bg + 1) * 512].rearrange("(kc q) n -> q kc n", q=128), writes=[wvt])
                for j in range(4):
                    kk = bg * 4 + j
                    res = []
                    for (w_v, wt_, zg, ring) in ((wg_v, wgt, kk, ugr), (wv_v, wvt, 44 + kk, uvr)):
                        pi = p.nextps()

                        def mmu(e):
                            for kc in range(16):
                                ins = e.matmul(ps[pi][:, 0:n], w_v[:, kc, j * 128:(j + 1) * 128], hn2T[:, kc, 0:n], start=(kc == 0), stop=(kc == 15))
                            return ins
                        p.op("pe", mmu, [wt_, hn2t], [pst[pi]])
                        u, ut = ring.next()
                        z = ps[pi]
                        p.op("act", lambda e: e.activation(out=u[:, 1:n - 1], in_=z[:, 1:n - 1], func=AF.Identity,
                                                           scale=G.fcw[:, zg, 1:2], bias=G.fcb[:, zg:zg + 1]), [pst[pi], G.cst], [ut])
                        lo = 2 if first_c else 1
                        hi = n - 2 if last_c else n - 1
                        if n - 1 > lo:
                            p.op("dve", lambda e: e.scalar_tensor_tensor(out=u[:, lo:n - 1], in0=z[:, lo - 1:n - 2], scalar=G.fcw[:, zg, 0:1],
                                                                         in1=u[:, lo:n - 1], op0=ALU.mult, op1=ALU.add), [pst[pi], G.cst], [ut])
                        if first_c:
                            p.op("dve", lambda e: e.scalar_tensor_tensor(out=u[:, 1:2], in0=z[:, 0:1], scalar=w0m[:, zg:zg + 1],
                                                                         in1=u[:, 1:2], op0=ALU.mult, op1=ALU.add), [pst[pi], ct], [ut])
                        if hi > 1:
                            p.op("dve", lambda e: e.scalar_tensor_tensor(out=u[:, 1:hi], in0=z[:, 2:hi + 1], scalar=G.fcw[:, zg, 2:3],
                                                                         in1=u[:, 1:hi], op0=ALU.mult, op1=ALU.add), [pst[pi], G.cst], [ut])
                        if last_c:
                            p.op("dve", lambda e: e.scalar_tensor_tensor(out=u[:, n - 2:n - 1], in0=z[:, n - 1:n], scalar=w2m[:, zg:zg + 1],
                                                                         in1=u[:, n - 2:n - 1], op0=ALU.mult, op1=ALU.add), [pst[pi], ct], [ut])
                        res.append((u, ut))
                    (ug, ugt), (uv, uvt) = res
                    sg, sgt = sgr.next()
                    p.op("act", lambda e: e.activation(out=sg[:, 1:n - 1], in_=ug[:, 1:n - 1], func=AF.Silu), [ugt], [sgt])
                    tt_op(p, "pool", gT[:, kk, 1:n - 1], sg[:, 1:n - 1], uv[:, 1:n - 1], ALU.mult, [sgt, uvt], joins=[rawt])
                    p.op("pool", lambda e: e.memset(gT[:, kk, 0:1], 0.0), [], [], [rawt])
                    p.op("pool", lambda e: e.memset(gT[:, kk, n - 1:n], 0.0), [], [], [rawt])
            for cb in range(4):
                banks = [p.nextps() for _ in range(ntile)]
                for q4 in range(4):
                    wd, wdt = wr.next()
                    wd_v = wd[:, 0:11 * 512].rearrange("p (k n) -> p k n", k=11)
                    p.dma("sp", out=wd_v, in_=G.w_dn[q4 * 1408:(q4 + 1) * 1408, cb * 512:(cb + 1) * 512].rearrange("(kc q) n -> q kc n", q=128),
                          writes=[wdt])
                    for i in range(ntile):
                        tn = tns[i]
                        pi = banks[i]

                        def mmd(e):
                            for kc in range(11):
                                kk = q4 * 11 + kc
                                ins = e.matmul(ps[pi][0:tn, :], gT[:, kk, i * 128:i * 128 + tn], wd_v[:, kc, :], start=(kk == 0), stop=(kk == 43))
                            return ins
                        if q4 == 0:
                            p.op("pe", mmd, [wdt, rawt], [pst[pi]])
                        else:
                            p.op("pe", mmd, [wdt, rawt], [], [pst[pi]])
                for i in range(ntile):
                    tn = tns[i]
                    pi = banks[i]
                    dst = x1[0:tn, i, cb * 512:(cb + 1) * 512]
                    tt_op(p, "dve", dst, ps[pi][0:tn, :], dst, ALU.add, [pst[pi]], [x1t[i]])
            for i in range(ntile):
                tn = tns[i]
                hn, hnt = hnr.next()
                sv, svt = st.next()
                p.op("dve", lambda e: e.memset(sv[:, 0:1], 0.0), [], [svt])
                p.op("act", lambda e: e.activation(out=hn[0:tn, :], in_=x1[0:tn, i, :], func=AF.Square, accum_out=sv[0:tn, 0:1]),
                     [x1t[i]], [hnt, svt])
                rstd_ops(p, sv[0:tn, 0:1], sv[0:tn, 1:2], sv[0:tn, 2:3], svt, 1.0 / D)
                p.op("act", lambda e: e.activation(out=x1[0:tn, i, :], in_=x1[0:tn, i, :], func=AF.Copy, scale=sv[0:tn, 2:3]),
                     [svt], [x1t[i]])
                tt_op(p, "dve", x1[0:tn, i, :], x1[0:tn, i, :], normf[0:tn, :], ALU.mult, [ct], [x1t[i]])
                r_lo = 1 if i == 0 else 0
                r_hi = tn - 1 if i == ntile - 1 else tn
                if r_hi > r_lo:
                    o0 = p0 + 128 * i + r_lo - 1
                    p.dma("pool", out=J.y[o0:o0 + (r_hi - r_lo), :], in_=x1[r_lo:r_hi, i, :], reads=[x1t[i]])


def host_consts(L, h):
    Lq = L // 2
    N2 = L // 64
    N = 2 * L
    f32 = np.float32
    pos = ((np.arange(L) + h * Lq) % L).astype(f32)
    inv = (10000.0 ** (-np.arange(0, 64, 2, dtype=f32) / 64.0)).astype(f32)
    pidx = np.arange(128) % 64 % 32
    ang = (pos[None, :] * inv[pidx][:, None]).astype(f32)
    c = {}
    c["cos"] = np.cos(ang).astype(f32)
    c["sin"] = np.sin(ang).astype(f32)
    t = np.linspace(0.0, 1.0, L, dtype=f32)[:, None]
    w = (2.0 * math.pi * np.arange(L, dtype=f32)[:, None] / L).astype(f32)
    f = np.linspace(1e-4, 15, 16, dtype=f32)[None]
    z = np.concatenate([t, np.cos(f * w), -np.sin(f * w)], -1).astype(f32)
    vidx = np.concatenate([np.arange(L), [0], np.arange(L - 1, 0, -1)])
    c["zT"] = np.ascontiguousarray(z[vidx].T)
    c["trow"] = np.ascontiguousarray(np.tile(t[vidx].T, (128, 1))).astype(f32)
    n1 = np.arange(64, dtype=np.float64)[:, None]
    k1 = np.arange(128, dtype=np.float64)[None, :]
    n1f = np.arange(128, dtype=np.float64)[:, None]
    th = 2 * np.pi * n1f * k1 / 128
    F1f = np.concatenate([np.cos(th), -np.sin(th)], 1)
    n1t = ((np.arange(64) + 32 * h) % 64).astype(np.float64)[:, None]
    th = 2 * np.pi * n1t * k1 / 128
    F1u = np.concatenate([np.cos(th), -np.sin(th)], 1)
    c["F1f"] = F1f.astype(NPBF)
    c["F1u"] = F1u.astype(NPBF)
    th0 = 2 * np.pi * n1f * k1 / 128
    c["F1fb"] = np.concatenate([np.sin(th0), np.cos(th0)], 1).astype(NPBF)
    c["F1ub"] = np.concatenate([np.sin(th), np.cos(th)], 1).astype(NPBF)
    k1v = np.arange(128, dtype=np.float64)[:, None, None]
    n2v = np.arange(N2, dtype=np.float64)[None, :, None]
    k2v = np.arange(N2, dtype=np.float64)[None, None, :]
    th = 2 * np.pi * n2v * (k1v + 128 * k2v) / N
    Gt = np.stack([np.cos(th), -np.sin(th), np.sin(th)], 2)
    c["G"] = np.ascontiguousarray(Gt).astype(NPBF)
    ph = 2 * np.pi * np.arange(N2, dtype=np.float64)[:, None] * np.arange(N2, dtype=np.float64)[None, :] / N2
    c["T1"] = np.concatenate([np.cos(ph), np.sin(ph), -np.sin(ph), np.cos(ph)], 1).astype(NPBF)
    n1rot = np.array([63] + list(range(32)) + [32])
    n1true = ((n1rot + 32 * h) % 64).astype(np.float64)
    th = 2 * np.pi * (N2 * n1true[None, None, :] + np.arange(N2, dtype=np.float64)[None, :, None]) * \
        np.arange(128, dtype=np.float64)[:, None, None] / N
    c["Ginv"] = np.ascontiguousarray(np.stack([np.cos(th) / N, -np.sin(th) / N], 2)).astype(NPBF)
    return c


def shared_inputs(inp):
    f32 = np.float32
    a = lambda x: np.ascontiguousarray(np.asarray(x, dtype=f32))
    m = {}
    m["w_in"] = a(inp["w_in"][0])
    m["w_ao"] = a(inp["w_attn_out"][0])
    m["w_ho"] = a(inp["w_hyena_out"][0])
    m["w_out"] = a(inp["w_out"][0])
    m["w_up"] = a(inp["w_up"][0])
    m["w_dn"] = a(inp["w_down"][0])
    m["norm1"] = a(np.asarray(inp["norm1"][0]).reshape(16, 128).T)
    m["norm2"] = a(np.asarray(inp["norm2"][0]).reshape(16, 128).T)
    m["normf"] = a(np.tile(np.asarray(inp["norm_f"])[None, :], (128, 1)))
    m["gateb"] = a(np.asarray(inp["gate_b"][0]).reshape(32, 128).T)
    m["icw"] = a(np.asarray(inp["in_conv_w"][0]).reshape(3, 24, 128).transpose(2, 1, 0))
    m["icb"] = a(np.asarray(inp["in_conv_b"][0]).reshape(24, 128).T)
    m["fcw"] = a(np.asarray(inp["ffn_conv_w"][0]).reshape(3, 88, 128).transpose(2, 1, 0))
    m["fcb"] = a(np.asarray(inp["ffn_conv_b"][0]).reshape(88, 128).T)
    m["subg"] = a(np.asarray(inp["subln_g"][0]).reshape(128, 1))
    m["hbias"] = a(np.tile(np.asarray(inp["hyena_bias"][0])[None, :], (128, 1)))
    m["hbiasp"] = a(np.asarray(inp["hyena_bias"][0]).reshape(8, 128).T)
    m["lamv"] = a(np.stack([np.asarray(inp[k][0]) for k in ("lambda_q1", "lambda_k1", "lambda_q2", "lambda_k2")], 1))
    m["fw1"] = a(inp["filt_w1"][0])
    m["fw2"] = a(inp["filt_w2"][0])
    m["fw3"] = a(inp["filt_w3"][0])
    m["fw4"] = a(inp["filt_w4"][0])
    m["fvec"] = a(np.stack([np.asarray(inp[k][0]) for k in ("filt_b1", "filt_b2", "filt_b3", "filt_freq")], 1))
    max_decay = math.log(1e-2) / 0.3
    min_decay = math.log(1e-2) / 1.5
    deltas = np.abs(np.linspace(min_decay, max_decay, CH, dtype=f32))
    m["ndelta"] = a((-deltas).reshape(8, 128).T)
    m["ident"] = np.eye(128, dtype=f32).astype(NPBF)
    R = np.zeros((128, 128), f32)
    for blk in range(2):
        for d in range(64):
            if d < 32:
                R[blk * 64 + d + 32, blk * 64 + d] = -1.0
            else:
                R[blk * 64 + d - 32, blk * 64 + d] = 1.0
    m["rotm"] = R.astype(NPBF)
    return m


_NC_CACHE = {}


def run(inp, seqs, Ls, n_cores, trace=False, runner=None):
    key = tuple(Ls)
    if key not in _NC_CACHE:
        _NC_CACHE[key] = build(Ls)
    nc = _NC_CACHE[key]
    sh = shared_inputs(inp)
    in_maps = []
    for c in range(n_cores):
        pair, h = c // 2, c % 2
        m = dict(sh)
        m["flags"] = np.tile(np.array([[1.0 - (h == 0), 1.0 * (h == 0)]], np.float32), (128, 1))
        for j, L in enumerate(Ls):
            s = "_%d" % j
            x = np.asarray(seqs[j][pair], dtype=np.float32)
            m["x" + s] = np.ascontiguousarray(np.roll(x, -h * (L // 2), axis=0))
            for k, v in host_consts(L, h).items():
                m[k + s] = v
        in_maps.append(m)
    if runner is not None:
        res = runner(nc, in_maps)
    else:
        res = run_bass_kernel_spmd(nc, in_maps, core_ids=list(range(n_cores)), trace=trace)
    outs = [np.zeros(np.asarray(seqs[j]).shape, np.float32) for j in range(len(Ls))]
    for c in range(n_cores):
        pair, h = c // 2, c % 2
        for j, L in enumerate(Ls):
            Lq = L // 2
            outs[j][pair, h * Lq:(h + 1) * Lq] = res.results[c]["y_%d" % j]
    return outs, res


def kernel(**inputs):
    xs = np.asarray(inputs["x_sample"])
    xp = np.asarray(inputs["x_prompt"])
    outs, _ = run(inputs, [xs, xp], [xs.shape[1], xp.shape[1]], 8)
    return (outs[1], outs[0])
```

```python
import math
from contextlib import ExitStack

import numpy as np
import ml_dtypes

import concourse.bass as bass
import concourse.mybir as mybir
from concourse.bass_utils import run_bass_kernel_spmd

F32 = mybir.dt.float32
BF16 = mybir.dt.bfloat16
AF = mybir.ActivationFunctionType
ALU = mybir.AluOpType
NPBF = ml_dtypes.bfloat16

D = 2048
NH = 8
CH = 1024
DFF = 5632
INW = 10240
EPS = 1e-6
LAMBDA_INIT = 0.8 - 0.6 * math.exp(0.0)
QW = 510


class Tok:
    __slots__ = ("w", "r", "excl")

    def __init__(self, excl=False):
        self.w = {}
        self.r = {}
        self.excl = excl


class Prog:
    def __init__(self, nc):
        self.nc = nc
        self.engs = {"pe": nc.tensor, "act": nc.scalar, "dve": nc.vector,
                     "pool": nc.gpsimd, "sp": nc.sync}
        self.sems = {}
        self.cnt = {}
        for e in self.engs:
            self.sems[e] = nc.semaphore("s_" + e).__enter__()
            self.cnt[e] = 0
        self.NS = 12
        for q in ("sp", "pool"):
            for i in range(self.NS):
                k = "d%s%d" % (q, i)
                self.sems[k] = nc.semaphore(k).__enter__()
                self.cnt[k] = 0
        self.rr = {"sp": 0, "pool": 0}
        self.seen = {e: {} for e in self.engs}
        self.uid = 0
        self.psi = -1

    def nextps(self):
        self.psi = (self.psi + 1) % 7
        return self.psi

    def name(self, base="t"):
        self.uid += 1
        return "%s%d" % (base, self.uid)

    def _wait(self, e, deps):
        for k, v in deps.items():
            if self.seen[e].get(k, 0) < v:
                self.engs[e].wait_ge(self.sems[k], v)
                self.seen[e][k] = v

    @staticmethod
    def _deps(reads, writes, joins=(), eng=None):
        d = {}
        for t in reads:
            for k, v in t.w.items():
                if d.get(k, 0) < v:
                    d[k] = v
            if t.excl:
                for k, v in t.r.items():
                    if k != eng and d.get(k, 0) < v:
                        d[k] = v
        for t in writes:
            for m in (t.w, t.r):
                for k, v in m.items():
                    if d.get(k, 0) < v:
                        d[k] = v
        for t in joins:
            for k, v in t.r.items():
                if d.get(k, 0) < v:
                    d[k] = v
        if eng == "pe":
            d.pop("pe", None)
        return d

    @staticmethod
    def _commit(reads, writes, k, v, joins=()):
        for t in reads:
            if t.r.get(k, 0) < v:
                t.r[k] = v
        for t in writes:
            t.w = {k: v}
            t.r = {}
        for t in joins:
            if t.w.get(k, 0) < v:
                t.w[k] = v

    def op(self, e, fn, reads=(), writes=(), joins=()):
        self._wait(e, self._deps(reads, writes, joins, e))
        ins = fn(self.engs[e])
        self.cnt[e] += 1
        ins.then_inc(self.sems[e], 1)
        self._commit(reads, writes, e, self.cnt[e], joins)

    def dma(self, q, out, in_, reads=(), writes=(), joins=()):
        i = self.rr[q]
        self.rr[q] = (i + 1) % self.NS
        k = "d%s%d" % (q, i)
        deps = self._deps(reads, writes, joins)
        if self.cnt[k] > 0:
            deps[k] = max(deps.get(k, 0), self.cnt[k])
        self._wait(q, deps)
        self.engs[q].dma_start(out=out, in_=in_, allow_slow_non_contiguous=True).then_inc(self.sems[k], 16)
        self.cnt[k] += 16
        self._commit(reads, writes, k, self.cnt[k], joins)

    def barrier(self):
        allv = {k: v for k, v in self.cnt.items() if v > 0}
        for e in self.engs:
            self._wait(e, allv)


class Ring:
    def __init__(self, bufs):
        self.bufs = bufs
        self.toks = [Tok() for _ in bufs]
        self.i = -1

    def next(self):
        self.i = (self.i + 1) % len(self.bufs)
        return self.bufs[self.i], self.toks[self.i]


class Job:
    pass


def build(Ls):
    nc = bass.Bass("TRN2", target_bir_lowering=False)
    p = Prog(nc)

    def din(name, shape, dt=F32):
        return nc.dram_tensor(name, list(shape), dt, kind="ExternalInput").ap()

    def dscr(name, shape, dt=BF16):
        return nc.dram_tensor(name, list(shape), dt).ap()

    w_in_f = din("w_in", [D, INW])
    w_ao_f = din("w_ao", [CH, D])
    w_ho_f = din("w_ho", [CH, D])
    w_out_f = din("w_out", [D, D])
    w_up_f = din("w_up", [D, 2 * DFF])
    w_dn_f = din("w_dn", [DFF, D])
    norm1_d = din("norm1", [128, 16])
    norm2_d = din("norm2", [128, 16])
    normf_d = din("normf", [128, D])
    gateb_d = din("gateb", [128, 32])
    icw_d = din("icw", [128, 24, 3])
    icb_d = din("icb", [128, 24])
    fcw_d = din("fcw", [128, 88, 3])
    fcb_d = din("fcb", [128, 88])
    subg_d = din("subg", [128, 1])
    hbias_d = din("hbias", [128, CH])
    hbiasp_d = din("hbiasp", [128, 8])
    lamv_d = din("lamv", [64, 4])
    fw1_d = din("fw1", [33, 64])
    fw2_d = din("fw2", [64, 64])
    fw3_d = din("fw3", [64, 64])
    fw4_d = din("fw4", [64, 2 * CH])
    fvec_d = din("fvec", [64, 4])
    ndelta_d = din("ndelta", [128, 8])
    ident_d = din("ident", [128, 128], BF16)
    rotm_d = din("rotm", [128, 128], BF16)
    flags_d = din("flags", [128, 2])

    w_in = dscr("w_in_b", [D, INW])
    w_ao = dscr("w_ao_b", [CH, D])
    w_ho = dscr("w_ho_b", [CH, D])
    w_out = dscr("w_out_b", [D, D])
    w_up = dscr("w_up_b", [D, 2 * DFF])
    w_dn = dscr("w_dn_b", [DFF, D])

    jobs = []
    for ji, L in enumerate(Ls):
        J = Job()
        J.i = ji
        J.L = L
        J.Lq = L // 2
        J.NQ = J.Lq + 2
        J.N2 = L // 64
        N2 = J.N2
        s = "_%d" % ji
        J.x = din("x" + s, [L, D])
        J.cos = din("cos" + s, [128, L])
        J.sin = din("sin" + s, [128, L])
        J.zT = din("zT" + s, [33, 2 * L])
        J.trow = din("trow" + s, [128, 2 * L])
        J.F1u = din("F1u" + s, [64, 256], BF16)
        J.F1f = din("F1f" + s, [128, 256], BF16)
        J.F1ub = din("F1ub" + s, [64, 256], BF16)
        J.F1fb = din("F1fb" + s, [128, 256], BF16)
        J.G = din("G" + s, [128, N2, 3, N2], BF16)
        J.T1 = din("T1" + s, [N2, 4 * N2], BF16)
        J.Ginv = din("Ginv" + s, [128, N2, 2, 34], BF16)
        J.y = nc.dram_tensor("y" + s, [J.Lq, D], F32, kind="ExternalOutput").ap()
        J.qT = dscr("qT" + s, [NH, 128, J.NQ])
        J.kT = dscr("kT" + s, [NH, 128, L])
        J.V = dscr("V" + s, [L, CH])
        J.xh = dscr("xh" + s, [3 * CH, L])
        J.gates = dscr("gates" + s, [2 * D, J.NQ])
        J.hT = dscr("hT" + s, [CH, 2 * L])
        J.Fs = dscr("Fs" + s, [CH // 64, N2, 128, 2, 64])
        J.uT = dscr("uT" + s, [CH, L])
        J.x0c = dscr("x0c" + s, [CH, J.NQ])
        J.hy = dscr("hy" + s, [CH, J.NQ])
        J.attn = dscr("attn" + s, [CH, J.NQ])
        jobs.append(J)

    def galloc(shape, dt):
        return nc.alloc_sbuf_tensor(p.name("g"), list(shape), dt)

    ident = galloc([128, 128], BF16)
    rotm = galloc([128, 128], BF16)
    ones_bf = galloc([128, 128], BF16)
    ones_f = galloc([128, 128], F32)
    norm1 = galloc([128, 16], F32)
    norm2 = galloc([128, 16], F32)
    gateb = galloc([128, 32], F32)
    icw = galloc([128, 24, 3], F32)
    icb = galloc([128, 24], F32)
    fcw = galloc([128, 88, 3], F32)
    fcb = galloc([128, 88], F32)
    subg = galloc([128, 1], F32)
    flags = galloc([128, 2], F32)
    lamv = galloc([64, 4], F32)
    lam = galloc([128, 4], F32)
    cst = Tok()

    psall = nc.alloc_psum_tensor("psall", [128, 4096], F32)
    ps = [psall[:, i * 512:(i + 1) * 512] for i in range(8)]
    pst = [Tok(excl=True) for _ in range(8)]
    psb = ps[7].bitcast(BF16)
    psbt = pst[7]

    for dst, src in ((ident, ident_d), (rotm, rotm_d), (norm1, norm1_d), (norm2, norm2_d),
                     (gateb, gateb_d), (icb, icb_d), (fcb, fcb_d), (subg, subg_d),
                     (flags, flags_d), (lamv, lamv_d)):
        p.dma("sp", out=dst[:], in_=src[:, :], writes=[cst])
    p.dma("sp", out=icw[:], in_=icw_d[:, :, :], writes=[cst])
    p.dma("sp", out=fcw[:], in_=fcw_d[:, :, :], writes=[cst])
    p.op("dve", lambda e: e.memset(ones_bf[:], 1.0), writes=[cst])
    p.op("dve", lambda e: e.memset(ones_f[:], 1.0), writes=[cst])
    p.barrier()

    prod = galloc([64, 2], F32)
    p.op("dve", lambda e: e.tensor_tensor(out=prod[:, 0:1], in0=lamv[:, 0:1], in1=lamv[:, 1:2], op=ALU.mult), writes=[cst])
    p.op("dve", lambda e: e.tensor_tensor(out=prod[:, 1:2], in0=lamv[:, 2:3], in1=lamv[:, 3:4], op=ALU.mult), writes=[cst])
    p.barrier()
    p.op("pe", lambda e: e.matmul(ps[0][:, 0:2], ones_f[0:64, :], prod[:, :], start=True, stop=True), writes=[pst[0]])
    p.op("act", lambda e: e.activation(out=lam[:, 1:3], in_=ps[0][:, 0:2], func=AF.Exp), reads=[pst[0]], writes=[cst])
    p.barrier()
    p.op("dve", lambda e: e.scalar_tensor_tensor(out=lam[:, 0:1], in0=lam[:, 2:3], scalar=-LAMBDA_INIT,
                                                 in1=lam[:, 1:2], op0=ALU.add, op1=ALU.subtract), writes=[cst])
    p.barrier()

    G = Job()
    G.nc = nc
    G.p = p
    G.ps, G.pst, G.psb, G.psbt, G.psall = ps, pst, psb, psbt, psall
    G.ident, G.rotm, G.ones_bf, G.ones_f = ident, rotm, ones_bf, ones_f
    G.norm1, G.norm2, G.gateb, G.icw, G.icb, G.fcw, G.fcb = norm1, norm2, gateb, icw, icb, fcw, fcb
    G.subg, G.flags, G.lam, G.cst = subg, flags, lam, cst
    G.w_in, G.w_ao, G.w_ho, G.w_out, G.w_up, G.w_dn = w_in, w_ao, w_ho, w_out, w_up, w_dn
    G.normf_d, G.hbias_d = normf_d, hbias_d
    G.hbiasp_d = hbiasp_d
    G.fw = (fw1_d, fw2_d, fw3_d, fw4_d, fvec_d, ndelta_d)

    G.wes = ExitStack()
    G.wgen = phase_W(G, [(w_in_f, w_in, D, INW, norm1), (w_ao_f, w_ao, CH, D, None), (w_ho_f, w_ho, CH, D, None),
                         (w_out_f, w_out, D, D, None), (w_up_f, w_up, D, 2 * DFF, norm2), (w_dn_f, w_dn, DFF, D, None)], G.wes)
    pump(G, 2)
    import os as _os
    _ph = _os.environ.get("KPH", "FfABbCD")
    for J in jobs:
        for letter, fn in (("F", lambda: phase_F(G, J)), ("f", lambda: phase_fft(G, J, filt=True)), ("A", lambda: phase_A(G, J)),
                           ("B", lambda: phase_B1(G, J)), ("b", lambda: phase_fft(G, J, filt=False)), ("C", lambda: phase_C(G, J)),
                           ("D", lambda: phase_D(G, J))):
            if letter == "A" and G.wes is not None:
                pump(G, 10 ** 6)
                p.barrier()
                G.wes.close()
                G.wes = None
            if letter in _ph:
                fn()
                p.barrier()
    return nc


def mk_alloc(G, es):
    def alloc(shape, dt):
        return es.enter_context(G.nc.sbuf_tensor(G.p.name("s"), list(shape), dt))
    return alloc


def mk_ring(alloc, n, shape, dt):
    return Ring([alloc(shape, dt) for _ in range(n)])


def phase_W(G, items, es):
    p = G.p
    CW = 2048
    if True:
        A = mk_alloc(G, es)
        rf = mk_ring(A, 3, [128, CW], F32)
        rb = mk_ring(A, 3, [128, CW], BF16)
        n = 0
        for src, dst, K, N, gain in items:
            for kc in range(K // 128):
                for c0 in range(0, N, CW):
                    cw = min(CW, N - c0)
                    wf, tf = rf.next()
                    wb, tb = rb.next()
                    p.dma("sp", out=wf[:, :cw], in_=src[kc * 128:(kc + 1) * 128, c0:c0 + cw], writes=[tf])
                    eng = "act" if n % 2 == 0 else "dve"
                    n += 1
                    if gain is None:
                        if eng == "act":
                            p.op("act", lambda e: e.activation(out=wb[:, :cw], in_=wf[:, :cw], func=AF.Copy), [tf], [tb])
                        else:
                            p.op("dve", lambda e: e.tensor_copy(out=wb[:, :cw], in_=wf[:, :cw]), [tf], [tb])
                    else:
                        g = gain[:, kc:kc + 1]
                        if eng == "act":
                            p.op("act", lambda e: e.activation(out=wb[:, :cw], in_=wf[:, :cw], func=AF.Copy, scale=g), [tf, G.cst], [tb])
                        else:
                            p.op("dve", lambda e: e.tensor_scalar(out=wb[:, :cw], in0=wf[:, :cw], scalar1=g, scalar2=None, op0=ALU.mult), [tf, G.cst], [tb])
                    p.dma("pool", out=dst[kc * 128:(kc + 1) * 128, c0:c0 + cw], in_=wb[:, :cw], reads=[tb])
                    yield


def pump(G, k):
    g = getattr(G, "wgen", None)
    if g is None:
        return
    for _ in range(k):
        try:
            next(g)
        except StopIteration:
            G.wgen = None
            return


def rstd_ops(p, src, tmp, dst, tok, scale, extra_reads=()):
    p.op("dve", lambda e: e.tensor_scalar(out=tmp, in0=src, scalar1=scale, scalar2=EPS, op0=ALU.mult, op1=ALU.add),
         list(extra_reads) + [tok], [tok])
    p.op("act", lambda e: e.activation(out=tmp, in_=tmp, func=AF.Sqrt), [tok], [tok])
    p.op("dve", lambda e: e.reciprocal(out=dst, in_=tmp), [tok], [tok])


def copy_op(p, eng, out, in_, reads, writes=(), joins=()):
    if eng == "act":
        p.op("act", lambda e: e.activation(out=out, in_=in_, func=AF.Copy), reads, writes, joins)
    else:
        p.op(eng, lambda e: e.tensor_copy(out=out, in_=in_), reads, writes, joins)


def tt_op(p, eng, out, in0, in1, op, reads, writes=(), joins=()):
    p.op(eng, lambda e: e.tensor_tensor(out=out, in0=in0, in1=in1, op=op), reads, writes, joins)


def phase_F(G, J):
    p = G.p
    L = J.L
    fw1_d, fw2_d, fw3_d, fw4_d, fvec_d, ndelta_d = G.fw
    ps, pst = G.ps, G.pst
    with ExitStack() as es:
        A = mk_alloc(G, es)
        w1 = A([33, 64], F32)
        w2 = A([64, 64], F32)
        w3 = A([64, 64], F32)
        w4 = A([64, 2 * CH], F32)
        fv = A([64, 8], F32)
        nd = A([128, 8], F32)
        ct = Tok()
        for dst, src in ((w1, fw1_d), (w2, fw2_d), (w3, fw3_d), (w4, fw4_d), (nd, ndelta_d)):
            p.dma("sp", out=dst[:], in_=src[:, :], writes=[ct])
        p.dma("sp", out=fv[:, 0:4], in_=fvec_d[:, :], writes=[ct])
        for i in range(3):
            tt_op(p, "dve", fv[:, 4 + i:5 + i], fv[:, i:i + 1], fv[:, 3:4], ALU.mult, [ct], [ct])
        zr = mk_ring(A, 2, [33, 512], F32)
        tr = mk_ring(A, 2, [128, 512], F32)
        pre = mk_ring(A, 2, [64, 512], F32)
        sa = mk_ring(A, 2, [64, 512], F32)
        sb = mk_ring(A, 2, [64, 512], F32)
        hr = mk_ring(A, 3, [64, 512], F32)
        dec = mk_ring(A, 2, [128, 512], F32)
        ob = mk_ring(A, 3, [128, 512], BF16)
        wl = (w1, w2, w3)
        hbp = A([128, 8], F32)
        p.dma("sp", out=hbp[:], in_=G.hbiasp_d[:, :], writes=[ct])
        for c in range(2 * L // 512):
            pump(G, 4)
            z, zt = zr.next()
            p.dma("sp", out=z[:], in_=J.zT[:, c * 512:(c + 1) * 512], writes=[zt])
            trw, trt = tr.next()
            p.dma("sp", out=trw[:], in_=J.trow[:, c * 512:(c + 1) * 512], writes=[trt])
            hin, hint = z, zt
            kdim = 33
            for li in range(3):
                pi = p.nextps()
                w = wl[li]
                p.op("pe", lambda e: e.matmul(ps[pi][0:64, :], w[0:kdim, :], hin[0:kdim, :], start=True, stop=True),
                     [ct, hint], [pst[pi]])
                pr, prt = pre.next()
                p.op("dve", lambda e: e.tensor_scalar(out=pr[:], in0=ps[pi][0:64, :], scalar1=fv[:, 3:4], scalar2=fv[:, 4 + li:5 + li],
                                                      op0=ALU.mult, op1=ALU.add), [pst[pi], ct], [prt])
                a, at = sa.next()
                b, bt = sb.next()
                p.op("act", lambda e: e.activation(out=a[:], in_=pr[:], func=AF.Sin, scale=0.25), [prt], [at])
                p.op("act", lambda e: e.activation(out=b[:], in_=pr[:], func=AF.Sin, scale=0.125), [prt], [bt])
                tt_op(p, "dve", b[:], b[:], b[:], ALU.mult, [bt], [bt])
                p.op("dve", lambda e: e.tensor_scalar(out=b[:], in0=b[:], scalar1=-2.0, scalar2=1.0, op0=ALU.mult, op1=ALU.add), [bt], [bt])
                tt_op(p, "dve", b[:], b[:], a[:], ALU.mult, [bt, at], [bt])
                tt_op(p, "dve", a[:], a[:], a[:], ALU.mult, [at, bt], [at])
                p.op("dve", lambda e: e.tensor_scalar(out=a[:], in0=a[:], scalar1=-2.0, scalar2=1.0, op0=ALU.mult, op1=ALU.add), [at], [at])
                h, ht = hr.next()
                p.op("dve", lambda e: e.scalar_tensor_tensor(out=h[:], in0=b[:], scalar=4.0, in1=a[:], op0=ALU.mult, op1=ALU.mult),
                     [at, bt], [ht])
                hin, hint = h, ht
                kdim = 64
            bwd = (c >= L // 512)
            for blk in (range(8, 16) if bwd else range(8)):
                pi = p.nextps()
                p.op("pe", lambda e: e.matmul(ps[pi][:, :], w4[:, blk * 128:(blk + 1) * 128], hin[:, :], start=True, stop=True),
                     [ct, hint], [pst[pi]])
                d, dt_ = dec.next()
                p.op("act", lambda e: e.activation(out=d[:], in_=trw[:], func=AF.Exp, scale=nd[:, blk % 8:blk % 8 + 1]), [trt, ct], [dt_])
                o, ot = ob.next()
                tt_op(p, "dve", o[:], ps[pi][:, :], d[:], ALU.mult, [pst[pi], dt_], [ot])
                if bwd and c == L // 512:
                    p.op("dve", lambda e: e.memset(o[:, 0:1], 0.0), [], [ot])
                if c == 0:
                    p.op("dve", lambda e: e.scalar_tensor_tensor(out=o[:, 0:1], in0=ps[pi][:, 0:1], scalar=d[:, 0:1], in1=hbp[:, blk:blk + 1],
                                                                 op0=ALU.mult, op1=ALU.add), [pst[pi], dt_, ct], [ot])
                p.dma("pool", out=J.hT[(blk % 8) * 128:(blk % 8 + 1) * 128, c * 512:(c + 1) * 512], in_=o[:], reads=[ot])


def phase_fft(G, J, filt):
    p = G.p
    N2, Lq, NQ = J.N2, J.Lq, J.NQ
    ps, pst = G.ps, G.pst
    GC = 64
    KB = 512 // GC
    src = J.hT if filt else J.uT
    ngrp = CH // GC
    KR = 128 if filt else 64
    psall = G.psall
    with ExitStack() as es:
        A = mk_alloc(G, es)
        F1 = A([KR, 256], BF16)
        F1b = A([KR, 256], BF16)
        ct = Tok()
        p.dma("sp", out=F1[:], in_=(J.F1f if filt else J.F1u)[:, :], writes=[ct])
        p.dma("sp", out=F1b[:], in_=(J.F1fb if filt else J.F1ub)[:, :], writes=[ct])
        xr = mk_ring(A, 2, [KR, 32, N2], BF16)
        AC = A([128, GC * 2 * 128], BF16)
        AC2 = A([128, GC * 2 * 128], BF16)
        act_ = Tok()
        A_sb = AC[0:N2, :].rearrange("p (c r k) -> p c r k", c=GC, r=2)
        A_sb2 = AC2[0:N2, :].rearrange("p (c r k) -> p c r k", c=GC, r=2)
        gr = mk_ring(A, 3, [N2, KB, 2 * N2], BF16)
        obr = mk_ring(A, 3, [N2, KB, 2, GC], BF16)
        if not filt:
            C_sb = AC[:, 0:2 * GC * N2].rearrange("p (r c n) -> p r c n", r=2, c=GC)
            Y_sb = A([N2, 2, GC, 128], BF16)
            yt = Tok()
            T1 = A([N2, 4 * N2], BF16)
            Gi = A([128, N2, 2, 34], BF16)
            p.dma("sp", out=T1[:], in_=J.T1[:, :], writes=[ct])
            p.dma("sp", out=Gi[:], in_=J.Ginv[:, :, :, :], writes=[ct])
            fr = [mk_ring(A, 2, [N2, KB, 2, GC], BF16) for _ in range(2)]
            tf = [mk_ring(A, 2, [N2, KB, GC], F32) for _ in range(4)]
            bias = A([N2, KB, GC], F32)
            bt = Tok()
            x0r = mk_ring(A, 1, [GC, NQ], BF16)
            hyr = mk_ring(A, 1, [GC, NQ], BF16)
        s1 = 0
        s3 = 0
        for g in range(ngrp):
            ch0 = g * GC
            for half in range(GC // 32):
                x, xt = xr.next()
                p.dma("sp", out=x[:], in_=src[ch0 + half * 32:ch0 + half * 32 + 32, :].rearrange("c (a b) -> a c b", a=KR), writes=[xt])
                for c2 in range(16):
                    b1 = 4 + 2 * (s1 % 2)
                    s1 += 1

                    def mm(e):
                        for u in range(2):
                            e.matmul(ps[b1][0:N2, u * 256:(u + 1) * 256], x[:, c2 * 2 + u, :], F1[:, :], start=True, stop=True)
                        for u in range(2):
                            ins = e.matmul(ps[b1 + 1][0:N2, u * 256:(u + 1) * 256], x[:, c2 * 2 + u, :], F1b[:, :], start=True, stop=True)
                        return ins
                    p.op("pe", mm, [xt, ct], [pst[b1], pst[b1 + 1]])
                    cl = half * 32 + c2 * 2
                    copy_op(p, "act", A_sb[:, cl:cl + 2, :, :],
                            ps[b1][0:N2, :].rearrange("p (c r k) -> p c r k", c=2, r=2), [pst[b1]], joins=[act_])
                    copy_op(p, "dve", A_sb2[:, cl:cl + 2, :, :],
                            ps[b1 + 1][0:N2, :].rearrange("p (c r k) -> p c r k", c=2, r=2), [pst[b1 + 1]], joins=[act_])
            for kb in range(128 // KB):
                if filt:
                    pump(G, 1)
                gt, gtt = gr.next()
                p.dma("sp", out=gt[:].rearrange("n k (t m) -> n k t m", t=2),
                      in_=J.G[kb * KB:(kb + 1) * KB, :, 0:2, :].rearrange("k n t m -> n k t m"), writes=[gtt])
                b3 = 2 * (s3 % 2)
                s3 += 1

                def mm3(e):
                    for kk in range(KB):
                        k1 = kb * KB + kk
                        o = psall[0:N2, b3 * 512 + kk * 2 * GC:b3 * 512 + (kk + 1) * 2 * GC]
                        e.matmul(o, gt[:, kk, 0:N2], A_sb[:, :, :, k1].rearrange("p c r -> p r c"), start=True, stop=False)
                        ins = e.matmul(o, gt[:, kk, N2:2 * N2], A_sb2[:, :, :, k1].rearrange("p c r -> p r c"), start=False, stop=True)
                    return ins
                p.op("pe", mm3, [gtt, act_], [pst[b3], pst[b3 + 1]])
                X = psall[0:N2, b3 * 512:(b3 + 2) * 512].rearrange("p (k r c) -> p k r c", k=KB, r=2)
                xr_v = X[:, :, 0, :]
                xi_v = X[:, :, 1, :]
                xtoks = [pst[b3], pst[b3 + 1]]
                k1a = kb * KB
                if filt:
                    o, ot = obr.next()
                    copy_op(p, "act" if kb % 2 == 0 else "dve", o[:], X, xtoks, [ot])
                    p.dma("pool", out=J.Fs[g, :, k1a:k1a + KB, :, :], in_=o[:], reads=[ot])
                else:
                    ff, fft_ = fr[0].next()
                    p.dma("sp", out=ff[:], in_=J.Fs[g, :, k1a:k1a + KB, :, :], writes=[fft_])
                    hr_, hi_ = ff[:, :, 0, :], ff[:, :, 1, :]
                    hrt = hit = fft_
                    t1, t1t = tf[2].next()
                    t2, t2t = tf[3].next()
                    yr_o = Y_sb[:, 0, :, k1a:k1a + KB].rearrange("p c k -> p k c")
                    yi_o = Y_sb[:, 1, :, k1a:k1a + KB].rearrange("p c k -> p k c")
                    tt_op(p, "dve", t1[:], xr_v, hr_, ALU.mult, xtoks + [hrt], [t1t])
                    tt_op(p, "dve", t2[:], xi_v, hi_, ALU.mult, xtoks + [hit], [t2t])
                    tt_op(p, "pool", yr_o, t1[:], t2[:], ALU.subtract, [t1t, t2t], joins=[yt])
                    t1, t1t = tf[2].next()
                    t2, t2t = tf[3].next()
                    tt_op(p, "dve", t1[:], xr_v, hi_, ALU.mult, xtoks + [hit], [t1t])
                    tt_op(p, "dve", t2[:], xi_v, hr_, ALU.mult, xtoks + [hrt], [t2t])
                    tt_op(p, "pool", yi_o, t1[:], t2[:], ALU.add, [t1t, t2t], joins=[yt])
            if filt:
                continue
            cpb = 512 // (2 * N2)
            for c0 in range(0, GC, cpb):
                pi = p.nextps()

                def mmi(e):
                    for u in range(cpb):
                        c = c0 + u
                        e.matmul(ps[pi][:, u * 2 * N2:(u + 1) * 2 * N2], Y_sb[:, 0, c, :], T1[:, 0:2 * N2], start=True, stop=False)
                        ins = e.matmul(ps[pi][:, u * 2 * N2:(u + 1) * 2 * N2], Y_sb[:, 1, c, :], T1[:, 2 * N2:4 * N2], start=False, stop=True)
                    return ins
                p.op("pe", mmi, [yt, ct, act_], [pst[pi]])
                copy_op(p, "act" if (c0 // cpb) % 2 == 0 else "dve", C_sb[:, :, c0:c0 + cpb, :],
                        ps[pi][:, 0:cpb * 2 * N2].rearrange("p (c r n) -> p r c n", c=cpb, r=2), [pst[pi]], joins=[act_])
            x0, x0t = x0r.next()
            hy, hyt = hyr.next()
            p.dma("sp", out=x0[:], in_=J.x0c[ch0:ch0 + GC, :], writes=[x0t])
            own_o = hy[:, 1:1 + Lq].rearrange("p (a b) -> p a b", b=N2)
            own_x = x0[:, 1:1 + Lq].rearrange("p (a b) -> p a b", b=N2)
            first = True
            for nb in range(0, N2, 16):
                pi = p.nextps()

                def mm2(e):
                    for u in range(16):
                        n2 = nb + u
                        e.matmul(ps[pi][0:GC, u * 32:(u + 1) * 32], C_sb[:, 0, :, n2], Gi[:, n2, 0, 1:33], start=True, stop=False)
                        ins = e.matmul(ps[pi][0:GC, u * 32:(u + 1) * 32], C_sb[:, 1, :, n2], Gi[:, n2, 1, 1:33], start=False, stop=True)
                    return ins
                p.op("pe", mm2, [act_, ct], [pst[pi]])
                w_ = [hyt] if first else []
                j_ = [] if first else [hyt]
                first = False
                tt_op(p, "dve", own_o[:, :, nb:nb + 16], ps[pi][0:GC, :].rearrange("p (b a) -> p a b", a=32),
                      own_x[:, :, nb:nb + 16], ALU.mult, [pst[pi], x0t], w_, j_)
            pi = p.nextps()

            def mmh(e):
                e.matmul(ps[pi][0:GC, 0:1], C_sb[:, 0, :, N2 - 1], Gi[:, N2 - 1, 0, 0:1], start=True, stop=False)
                e.matmul(ps[pi][0:GC, 0:1], C_sb[:, 1, :, N2 - 1], Gi[:, N2 - 1, 1, 0:1], start=False, stop=True)
                e.matmul(ps[pi][0:GC, 1:2], C_sb[:, 0, :, 0], Gi[:, 0, 0, 33:34], start=True, stop=False)
                return e.matmul(ps[pi][0:GC, 1:2], C_sb[:, 1, :, 0], Gi[:, 0, 1, 33:34], start=False, stop=True)
            p.op("pe", mmh, [act_, ct], [pst[pi]])
            tt_op(p, "dve", hy[:, 0:1], ps[pi][0:GC, 0:1], x0[:, 0:1], ALU.mult, [pst[pi], x0t], joins=[hyt])
            tt_op(p, "dve", hy[:, NQ - 1:NQ], ps[pi][0:GC, 1:2], x0[:, NQ - 1:NQ], ALU.mult, [pst[pi], x0t], joins=[hyt])
            p.dma("pool", out=J.hy[ch0:ch0 + GC, :], in_=hy[:], reads=[hyt])


def phase_A(G, J):
    p = G.p
    L, Lq, NQ = J.L, J.Lq, J.NQ
    ps, pst = G.ps, G.pst
    nchunks = L // 512
    with ExitStack() as es:
        A = mk_alloc(G, es)
        xr = mk_ring(A, 2, [128, D], F32)
        junk = mk_ring(A, 2, [128, D], BF16)
        hnr = mk_ring(A, 2, [128, D], BF16)
        st = mk_ring(A, 4, [128, 4], F32)
        hnT = [A([128, 16, 512], BF16) for _ in range(2)]
        hnT_t = [[Tok() for _ in range(8)] for _ in range(2)]
        wr = mk_ring(A, 2, [128, 16, 512], BF16)
        cosr = mk_ring(A, 2, [128, 512], F32)
        sinr = mk_ring(A, 2, [128, 512], F32)
        qraw = mk_ring(A, 2, [128, 512], BF16)
        t1r = mk_ring(A, 2, [128, 512], F32)
        t2r = mk_ring(A, 2, [128, 512], F32)
        osb = mk_ring(A, 4, [128, 512], BF16)
        for c in range(nchunks):
            s = c % 2
            own = (c * 512 < Lq)
            for i in range(4):
                r0 = c * 512 + i * 128
                xt, xtt = xr.next()
                p.dma("sp", out=xt[:], in_=J.x[r0:r0 + 128, :], writes=[xtt])
                jk, jkt = junk.next()
                sv, svt = st.next()
                p.op("dve", lambda e: e.memset(sv[:, 0:1], 0.0), [], [svt])
                p.op("act", lambda e: e.activation(out=jk[:], in_=xt[:], func=AF.Square, accum_out=sv[:, 0:1]), [xtt], [jkt, svt])
                rstd_ops(p, sv[:, 0:1], sv[:, 1:2], sv[:, 2:3], svt, 1.0 / D)
                hn, hnt = hnr.next()
                p.op("act", lambda e: e.activation(out=hn[:], in_=xt[:], func=AF.Copy, scale=sv[:, 2:3]), [xtt, svt], [hnt])
                for half in range(2):
                    def tr(e):
                        for k in range(8):
                            kc = half * 8 + k
                            ins = e.transpose(G.psb[:, k * 128:(k + 1) * 128], hn[:, kc * 128:(kc + 1) * 128], G.ident[:])
                        return ins
                    p.op("pe", tr, [hnt, G.cst], [G.psbt])
                    copy_op(p, "dve" if half == 0 else "act", hnT[s][:, half * 8:(half + 1) * 8, i * 128:(i + 1) * 128],
                            G.psb[:, :].rearrange("p (k t) -> p k t", k=8), [G.psbt], [hnT_t[s][i * 2 + half]])
            cs, cst_ = cosr.next()
            sn, snt = sinr.next()
            p.dma("sp", out=cs[:], in_=J.cos[:, c * 512:(c + 1) * 512], writes=[cst_])
            p.dma("sp", out=sn[:], in_=J.sin[:, c * 512:(c + 1) * 512], writes=[snt])

            import os as _os
            _ka = _os.environ.get("KA", "")

            def qstore(dst_fn, ob, obt):
                if "c" in _ka:
                    if own:
                        p.dma("pool", out=dst_fn(1 + c * 512, 1 + (c + 1) * 512), in_=ob[:], reads=[obt])
                    return
                if own:
                    p.dma("pool", out=dst_fn(1 + c * 512, 1 + (c + 1) * 512), in_=ob[:], reads=[obt])
                if c * 512 == Lq:
                    p.dma("pool", out=dst_fn(NQ - 1, NQ), in_=ob[:, 0:1], reads=[obt])
                if c == nchunks - 1:
                    p.dma("pool", out=dst_fn(0, 1), in_=ob[:, 511:512], reads=[obt])

            first_other = (c * 512 == Lq)
            last_chunk = (c == nchunks - 1)
            need_q = own or first_other or last_chunk
            for b in range(20):
                qblk = (b < 2 or b >= 12 or b in (6, 7))
                if not need_q and qblk:
                    continue
                if ("r" in _ka and b < 4) or ("v" in _ka and b in (4, 5)) or ("h" in _ka and 6 <= b < 12) or ("g" in _ka and b >= 12):
                    continue
                lo, hi = 0, 512
                if qblk and not own:
                    if first_other and not last_chunk:
                        lo, hi = 0, 2
                    elif last_chunk and not first_other:
                        lo, hi = 510, 512
                wb, wbt = wr.next()
                p.dma("sp", out=wb[:], in_=G.w_in[:, b * 512:(b + 1) * 512].rearrange("(kc q) n -> q kc n", q=128), writes=[wbt])
                if b in (4, 5):
                    for i in range(4):
                        pi = p.nextps()

                        def mmv(e):
                            for kc in range(16):
                                ins = e.matmul(ps[pi][:, :], hnT[s][:, kc, i * 128:(i + 1) * 128], wb[:, kc, :],
                                               start=(kc == 0), stop=(kc == 15))
                            return ins
                        p.op("pe", mmv, [wbt] + hnT_t[s], [pst[pi]])
                        ob, obt = osb.next()
                        copy_op(p, "act", ob[:], ps[pi][:, :], [pst[pi]], [obt])
                        r0 = c * 512 + i * 128
                        p.dma("pool", out=J.V[r0:r0 + 128, (b - 4) * 512:(b - 3) * 512], in_=ob[:], reads=[obt])
                    continue
                for j in range(4):
                    g = b * 4 + j
                    pi = p.nextps()

                    def mmf(e):
                        for kc in range(16):
                            ins = e.matmul(ps[pi][:, lo:hi], wb[:, kc, j * 128:(j + 1) * 128], hnT[s][:, kc, lo:hi],
                                           start=(kc == 0), stop=(kc == 15))
                        return ins
                    p.op("pe", mmf, [wbt] + hnT_t[s], [pst[pi]])
                    ob, obt = osb.next()
                    if b < 4:
                        head = g % 8
                        qr, qrt = qraw.next()
                        copy_op(p, "act", qr[:, lo:hi], ps[pi][:, lo:hi], [pst[pi]], [qrt])
                        pr = p.nextps()
                        p.op("pe", lambda e: e.matmul(ps[pr][:, lo:hi], G.rotm[:], qr[:, lo:hi], start=True, stop=True), [qrt, G.cst], [pst[pr]])
                        t1, t1t = t1r.next()
                        t2, t2t = t2r.next()
                        tt_op(p, "dve", t1[:, lo:hi], ps[pi][:, lo:hi], cs[:, lo:hi], ALU.mult, [pst[pi], cst_], [t1t])
                        tt_op(p, "dve", t2[:, lo:hi], ps[pr][:, lo:hi], sn[:, lo:hi], ALU.mult, [pst[pr], snt], [t2t])
                        tt_op(p, "dve" if "P" in _ka else "pool", ob[:, lo:hi], t1[:, lo:hi], t2[:, lo:hi], ALU.add, [t1t, t2t], [obt])
                        if b >= 2:
                            p.dma("pool", out=J.kT[head, :, c * 512:(c + 1) * 512], in_=ob[:], reads=[obt])
                        else:
                            qstore(lambda a, bb: J.qT[head, :, a:bb], ob, obt)
                    elif b < 12:
                        copy_op(p, "dve", ob[:, lo:hi], ps[pi][:, lo:hi], [pst[pi]], [obt])
                        r0 = (g - 24) * 128
                        p.dma("pool", out=J.xh[r0:r0 + 128, c * 512 + lo:c * 512 + hi], in_=ob[:, lo:hi], reads=[obt])
                    else:
                        gi = g - 48
                        p.op("act", lambda e: e.activation(out=ob[:, lo:hi], in_=ps[pi][:, lo:hi], func=AF.Sigmoid, bias=G.gateb[:, gi:gi + 1]),
                             [pst[pi], G.cst], [obt])
                        qstore(lambda a, bb: J.gates[gi * 128:(gi + 1) * 128, a:bb], ob, obt)


def phase_B1(G, J):
    p = G.p
    L, Lq, NQ = J.L, J.Lq, J.NQ
    SEG = min(2048, Lq)
    m_wrap = G.flags[:, 0:1]
    m_mid = G.flags[:, 1:2]
    with ExitStack() as es:
        A = mk_alloc(G, es)
        xin = mk_ring(A, 3, [128, SEG + 2], BF16)
        cv = mk_ring(A, 3, [128, SEG], F32)
        ub = mk_ring(A, 2, [128, SEG], BF16)
        hx = mk_ring(A, 2, [128, 8], BF16)
        ho = mk_ring(A, 2, [128, 4], F32)
        hb = mk_ring(A, 2, [128, 2], BF16)

        def conv_seg(row0, gi, s0, pm, nm):
            xi, xit = xin.next()
            pv = (s0 - 1) % L
            nx = (s0 + SEG) % L
            p.dma("sp", out=xi[:, 1:SEG + 1], in_=J.xh[row0:row0 + 128, s0:s0 + SEG], writes=[xit])
            p.dma("sp", out=xi[:, 0:1], in_=J.xh[row0:row0 + 128, pv:pv + 1], joins=[xit])
            p.dma("sp", out=xi[:, SEG + 1:SEG + 2], in_=J.xh[row0:row0 + 128, nx:nx + 1], joins=[xit])
            if pm is not None:
                p.op("dve", lambda e: e.tensor_scalar(out=xi[:, 0:1], in0=xi[:, 0:1], scalar1=pm, scalar2=None, op0=ALU.mult),
                     [xit, G.cst], [xit])
            if nm is not None:
                p.op("dve", lambda e: e.tensor_scalar(out=xi[:, SEG + 1:SEG + 2], in0=xi[:, SEG + 1:SEG + 2], scalar1=nm, scalar2=None,
                                                      op0=ALU.mult), [xit, G.cst], [xit])
            c_, ct_ = cv.next()
            w0 = G.icw[:, gi, 0:1]
            w1 = G.icw[:, gi, 1:2]
            w2 = G.icw[:, gi, 2:3]
            bb = G.icb[:, gi:gi + 1]
            p.op("act", lambda e: e.activation(out=c_[:], in_=xi[:, 1:SEG + 1], func=AF.Identity, scale=w1, bias=bb),
                 [xit, G.cst], [ct_])
            p.op("dve", lambda e: e.scalar_tensor_tensor(out=c_[:], in0=xi[:, 0:SEG], scalar=w0, in1=c_[:], op0=ALU.mult, op1=ALU.add),
                 [xit, G.cst], [ct_])
            p.op("dve", lambda e: e.scalar_tensor_tensor(out=c_[:], in0=xi[:, 2:SEG + 2], scalar=w2, in1=c_[:], op0=ALU.mult, op1=ALU.add),
                 [xit, G.cst], [ct_])
            return c_, ct_

        def masks(s0):
            pm = m_wrap if s0 == 0 else (m_mid if s0 == Lq else None)
            e_ = s0 + SEG
            nm = m_wrap if e_ == L else (m_mid if e_ == Lq else None)
            return pm, nm

        for g in range(8):
            for s0 in range(0, L, SEG):
                pm, nm = masks(s0)
                c1, c1t = conv_seg(CH + g * 128, 8 + g, s0, pm, nm)
                c2, c2t = conv_seg(2 * CH + g * 128, 16 + g, s0, pm, nm)
                u, ut = ub.next()
                tt_op(p, "pool", u[:], c1[:], c2[:], ALU.mult, [c1t, c2t], [ut])
                p.dma("pool", out=J.uT[g * 128:(g + 1) * 128, s0:s0 + SEG], in_=u[:], reads=[ut])
            for s0 in range(0, Lq, SEG):
                pm, nm = masks(s0)
                c0, c0t = conv_seg(g * 128, g, s0, pm, nm)
                u, ut = ub.next()
                copy_op(p, "act", u[:], c0[:], [c0t], [ut])
                p.dma("pool", out=J.x0c[g * 128:(g + 1) * 128, 1 + s0:1 + s0 + SEG], in_=u[:], reads=[ut])
            h, ht = hx.next()
            p.dma("sp", out=h[:, 0:2], in_=J.xh[g * 128:(g + 1) * 128, L - 2:L], writes=[ht])
            p.dma("sp", out=h[:, 2:3], in_=J.xh[g * 128:(g + 1) * 128, 0:1], joins=[ht])
            p.dma("sp", out=h[:, 3:6], in_=J.xh[g * 128:(g + 1) * 128, Lq - 1:Lq + 2], joins=[ht])
            o, ot = ho.next()
            w0 = G.icw[:, g, 0:1]
            w1 = G.icw[:, g, 1:2]
            w2 = G.icw[:, g, 2:3]
            bb = G.icb[:, g:g + 1]
            hv_ = h[:, 0:6].rearrange("p (a b) -> p a b", b=3)
            p.op("dve", lambda e: e.tensor_scalar(out=o[:, 0:2], in0=hv_[:, :, 1], scalar1=w1, scalar2=bb, op0=ALU.mult, op1=ALU.add),
                 [ht, G.cst], [ot])
            p.op("dve", lambda e: e.scalar_tensor_tensor(out=o[:, 0:2], in0=hv_[:, :, 0], scalar=w0, in1=o[:, 0:2], op0=ALU.mult, op1=ALU.add),
                 [ht, G.cst], [ot])
            p.op("dve", lambda e: e.scalar_tensor_tensor(out=o[:, 0:2], in0=hv_[:, :, 2], scalar=w2, in1=o[:, 0:2], op0=ALU.mult, op1=ALU.add),
                 [ht, G.cst], [ot])
            ob_, obt_ = hb.next()
            copy_op(p, "dve", ob_[:], o[:, 0:2], [ot], [obt_])
            p.dma("pool", out=J.x0c[g * 128:(g + 1) * 128, 0:1], in_=ob_[:, 0:1], reads=[obt_])
            p.dma("pool", out=J.x0c[g * 128:(g + 1) * 128, NQ - 1:NQ], in_=ob_[:, 1:2], reads=[obt_])


def phase_C(G, J):
    p = G.p
    L, Lq, NQ = J.L, J.Lq, J.NQ
    ps, pst = G.ps, G.pst
    nkt = L // 128
    qchunks = [(a, min(a + 512, NQ)) for a in range(0, NQ, 512)]
    scale = 64.0 ** -0.5
    nlam = G.lam[:, 0:1]
    with ExitStack() as es:
        A = mk_alloc(G, es)
        kr = mk_ring(A, 2, [128, L], BF16)
        vr = mk_ring(A, 2, [128, nkt, 128], BF16)
        qr = mk_ring(A, 2, [128, NQ], BF16)
        er = mk_ring(A, 3, [128, 2, 512], BF16)
        rz = mk_ring(A, 2, [128, 512], F32)
        rbr = mk_ring(A, 2, [128, 512], F32)
        sel = A([64, 256], F32)
        selt = Tok()
        p.op("dve", lambda e: e.memset(sel[:], 0.0), [], [selt])
        p.op("dve", lambda e: e.memset(sel[0:32, 0:128], 1.0 / 32), [], [selt])
        p.op("dve", lambda e: e.memset(sel[32:64, 128:256], 1.0 / 32), [], [selt])
        a0r = mk_ring(A, 2, [128, 512], F32)
        a1r = mk_ring(A, 2, [128, 512], F32)
        sqr = mk_ring(A, 2, [128, 512], BF16)
        rsr = mk_ring(A, 2, [128, 512], F32)
        obr = mk_ring(A, 2, [128, 512], BF16)
        psall = G.psall
        for h in range(NH):
            k, kt_ = kr.next()
            p.dma("sp", out=k[:], in_=J.kT[h, :, :], writes=[kt_])
            v, vt = vr.next()
            p.dma("sp", out=v[:], in_=J.V[:, h * 128:(h + 1) * 128].rearrange("(t q) d -> q t d", q=128), writes=[vt])
            q, qt = qr.next()
            p.dma("sp", out=q[:], in_=J.qT[h, :, :], writes=[qt])
            for (a, b) in qchunks:
                n = b - a

                def s_pair(kt, slot):
                    b0 = 4 + 2 * slot

                    def mm(e):
                        e.matmul(ps[b0][:, 0:n], k[0:64, kt * 128:(kt + 1) * 128], q[0:64, a:b], start=True, stop=True)
                        return e.matmul(ps[b0 + 1][:, 0:n], k[64:128, kt * 128:(kt + 1) * 128], q[64:128, a:b], start=True, stop=True)
                    p.op("pe", mm, [kt_, qt], [pst[b0], pst[b0 + 1]])
                    e_, et = er.next()
                    src = psall[:, b0 * 512:(b0 + 2) * 512].rearrange("p (m n) -> p m n", m=2)[:, :, 0:n]
                    p.op("act", lambda e: e.activation(out=e_[:, :, 0:n], in_=src, func=AF.Exp, scale=scale),
                         [pst[b0], pst[b0 + 1]], [et])
                    return e_, et

                def av_mm(kt, e_, et):
                    def mm(e):
                        for m in range(2):
                            e.matmul(ps[m][:, 0:n], v[:, kt, :], e_[:, m, 0:n], start=(kt == 0), stop=(kt == nkt - 1))
                        for m in range(2):
                            ins = e.matmul(ps[2][32 * m:32 * m + 32, 0:n], G.ones_bf[:, 0:32], e_[:, m, 0:n],
                                           start=(kt == 0), stop=(kt == nkt - 1), tile_position=(0, 32 * m))
                        return ins
                    if kt == 0:
                        p.op("pe", mm, [vt, et, G.cst], [pst[0], pst[1], pst[2]])
                    else:
                        p.op("pe", mm, [vt, et, G.cst], [], [pst[0], pst[1], pst[2]])

                cur = s_pair(0, 0)
                for kt in range(nkt):
                    nxt = s_pair(kt + 1, (kt + 1) % 2) if kt + 1 < nkt else None
                    av_mm(kt, *cur)
                    cur = nxt
                r_, rt = rz.next()
                a0, a0t = a0r.next()
                a1, a1t = a1r.next()
                p.op("dve", lambda e: e.reciprocal(out=r_[0:64, 0:n], in_=ps[2][0:64, 0:n]), [pst[2]], [rt])
                for m, (am, amt) in enumerate(((a0, a0t), (a1, a1t))):
                    pb = 3 if m == 0 else 4
                    p.op("pe", lambda e: e.matmul(ps[pb][:, 0:n], sel[:, m * 128:(m + 1) * 128], r_[0:64, 0:n], start=True, stop=True),
                         [rt, selt], [pst[pb]])
                    rb, rbt = rbr.next()
                    copy_op(p, "act", rb[:, 0:n], ps[pb][:, 0:n], [pst[pb]], [rbt])
                    tt_op(p, "dve", am[:, 0:n], ps[m][:, 0:n], rb[:, 0:n], ALU.mult, [pst[m], rbt], [amt])
                p.op("dve", lambda e: e.scalar_tensor_tensor(out=a0[:, 0:n], in0=a1[:, 0:n], scalar=nlam, in1=a0[:, 0:n],
                                                              op0=ALU.mult, op1=ALU.add), [a1t, G.cst], [a0t])
                sq, sqt = sqr.next()
                tt_op(p, "pool", sq[:, 0:n], a0[:, 0:n], a0[:, 0:n], ALU.mult, [a0t], [sqt])
                si = 4
                p.op("pe", lambda e: e.matmul(ps[si][:, 0:n], G.ones_bf[:], sq[:, 0:n], start=True, stop=True), [sqt, G.cst], [pst[si]])
                rs, rst = rsr.next()
                rstd_ops(p, ps[si][:, 0:n], rs[:, 0:n], rs[:, 0:n], rst, 1.0 / 128, extra_reads=[pst[si]])
                tt_op(p, "pool", a0[:, 0:n], a0[:, 0:n], rs[:, 0:n], ALU.mult, [rst], [a0t])
                o, ot = obr.next()
                p.op("dve", lambda e: e.tensor_scalar(out=o[:, 0:n], in0=a0[:, 0:n], scalar1=G.subg[:, 0:1], scalar2=1.0 - LAMBDA_INIT,
                                                      op0=ALU.mult, op1=ALU.mult), [a0t, G.cst], [ot])
                p.dma("pool", out=J.attn[h * 128:(h + 1) * 128, a:b], in_=o[:, 0:n], reads=[ot])


def phase_D(G, J):
    p = G.p
    L, Lq, NQ = J.L, J.Lq, J.NQ
    ps, pst = G.ps, G.pst
    chunks = []
    p0 = 0
    while p0 < Lq:
        chunks.append((p0, min(QW, Lq - p0) + 2))
        p0 += QW
    with ExitStack() as es:
        A = mk_alloc(G, es)
        raw = A([128, 44 * 512], BF16)
        rawt = Tok()
        gT = raw[:, :].rearrange("p (k n) -> p k n", k=44)
        at_ = raw[:, 0:8 * 512].rearrange("p (k n) -> p k n", k=8)
        hy_ = raw[:, 8 * 512:16 * 512].rearrange("p (k n) -> p k n", k=8)
        mg = raw[:, 16 * 512:32 * 512].rearrange("p (k n) -> p k n", k=16)
        g0r = mk_ring(A, 2, [128, 512], BF16)
        g1r = mk_ring(A, 2, [128, 512], BF16)
        m0r = mk_ring(A, 1, [128, 512], F32)
        m1r = mk_ring(A, 1, [128, 512], F32)
        x1 = A([128, 4, D], F32)
        x1t = [Tok() for _ in range(4)]
        hnr = mk_ring(A, 2, [128, D], BF16)
        st = mk_ring(A, 4, [128, 4], F32)
        hn2T = A([128, 16, 512], BF16)
        hn2t = Tok()
        wr = mk_ring(A, 3, [128, 16 * 512], BF16)
        ugr = mk_ring(A, 2, [128, 512], F32)
        uvr = mk_ring(A, 2, [128, 512], F32)
        sgr = mk_ring(A, 2, [128, 512], F32)
        normf = A([128, D], F32)
        w0m = A([128, 88], F32)
        w2m = A([128, 88], F32)
        ct = Tok()
        p.dma("sp", out=normf[:], in_=G.normf_d[:, :], writes=[ct])
        p.op("dve", lambda e: e.tensor_scalar(out=w0m[:], in0=G.fcw[:, :, 0], scalar1=G.flags[:, 0:1], scalar2=None, op0=ALU.mult),
             [G.cst], [ct])
        p.op("dve", lambda e: e.tensor_scalar(out=w2m[:], in0=G.fcw[:, :, 2], scalar1=G.flags[:, 1:2], scalar2=None, op0=ALU.mult),
             [G.cst], [ct])

        for ci, (p0, n) in enumerate(chunks):
            first_c = (ci == 0)
            last_c = (ci == len(chunks) - 1)
            ntile = (n + 127) // 128
            tns = [min(128, n - 128 * i) for i in range(ntile)]
            p.dma("sp", out=at_[:, :, 0:n], in_=J.attn[:, p0:p0 + n].rearrange("(g q) n -> q g n", q=128), writes=[rawt])
            p.dma("sp", out=hy_[:, :, 0:n], in_=J.hy[:, p0:p0 + n].rearrange("(g q) n -> q g n", q=128), joins=[rawt])
            for blk in range(4):
                wa, wat = wr.next()
                wh, wht = wr.next()
                wa_v = wa[:, 0:8 * 512].rearrange("p (k n) -> p k n", k=8)
                wh_v = wh[:, 0:8 * 512].rearrange("p (k n) -> p k n", k=8)
                p.dma("sp", out=wa_v, in_=G.w_ao[:, blk * 512:(blk + 1) * 512].rearrange("(kc q) n -> q kc n", q=128), writes=[wat])
                p.dma("sp", out=wh_v, in_=G.w_ho[:, blk * 512:(blk + 1) * 512].rearrange("(kc q) n -> q kc n", q=128), writes=[wht])
                for j in range(4):
                    og = blk * 4 + j
                    g0, g0t = g0r.next()
                    g1, g1t = g1r.next()
                    p.dma("sp", out=g0[:, 0:n], in_=J.gates[og * 128:(og + 1) * 128, p0:p0 + n], writes=[g0t])
                    p.dma("sp", out=g1[:, 0:n], in_=J.gates[D + og * 128:D + (og + 1) * 128, p0:p0 + n], writes=[g1t])
                    pa, pb = p.nextps(), p.nextps()

                    def mma(e):
                        for kc in range(8):
                            e.matmul(ps[pa][:, 0:n], wa_v[:, kc, j * 128:(j + 1) * 128], at_[:, kc, 0:n], start=(kc == 0), stop=(kc == 7))
                        for kc in range(8):
                            ins = e.matmul(ps[pb][:, 0:n], wh_v[:, kc, j * 128:(j + 1) * 128], hy_[:, kc, 0:n], start=(kc == 0), stop=(kc == 7))
                        return ins
                    p.op("pe", mma, [wat, wht, rawt], [pst[pa], pst[pb]])
                    m0, m0t = m0r.next()
                    m1, m1t = m1r.next()
                    tt_op(p, "dve", m0[:, 0:n], ps[pa][:, 0:n], g0[:, 0:n], ALU.mult, [pst[pa], g0t], [m0t])
                    tt_op(p, "dve", m1[:, 0:n], ps[pb][:, 0:n], g1[:, 0:n], ALU.mult, [pst[pb], g1t], [m1t])
                    tt_op(p, "pool", mg[:, og, 0:n], m0[:, 0:n], m1[:, 0:n], ALU.add, [m0t, m1t], joins=[rawt])
            for i in range(ntile):
                tn = tns[i]
                pr0 = p0 + 128 * i
                if pr0 == 0:
                    p.dma("sp", out=x1[0:1, i, :], in_=J.x[L - 1:L, :], writes=[x1t[i]])
                    if tn > 1:
                        p.dma("sp", out=x1[1:tn, i, :], in_=J.x[0:tn - 1, :], joins=[x1t[i]])
                else:
                    p.dma("sp", out=x1[0:tn, i, :], in_=J.x[pr0 - 1:pr0 - 1 + tn, :], writes=[x1t[i]])
            for cb in range(4):
                wb, wbt = wr.next()
                wb_v = wb[:, :].rearrange("p (k n) -> p k n", k=16)
                p.dma("sp", out=wb_v, in_=G.w_out[:, cb * 512:(cb + 1) * 512].rearrange("(kc q) n -> q kc n", q=128), writes=[wbt])
                for i in range(ntile):
                    tn = tns[i]
                    pi = p.nextps()

                    def mmo(e):
                        for kc in range(16):
                            ins = e.matmul(ps[pi][0:tn, :], mg[:, kc, i * 128:i * 128 + tn], wb_v[:, kc, :], start=(kc == 0), stop=(kc == 15))
                        return ins
                    p.op("pe", mmo, [wbt, rawt], [pst[pi]])
                    dst = x1[0:tn, i, cb * 512:(cb + 1) * 512]
                    tt_op(p, "dve", dst, ps[pi][0:tn, :], dst, ALU.add, [pst[pi]], [x1t[i]])
            for i in range(ntile):
                tn = tns[i]
                hn, hnt = hnr.next()
                sv, svt = st.next()
                p.op("dve", lambda e: e.memset(sv[:, 0:1], 0.0), [], [svt])
                p.op("act", lambda e: e.activation(out=hn[0:tn, :], in_=x1[0:tn, i, :], func=AF.Square, accum_out=sv[0:tn, 0:1]),
                     [x1t[i]], [hnt, svt])
                rstd_ops(p, sv[0:tn, 0:1], sv[0:tn, 1:2], sv[0:tn, 2:3], svt, 1.0 / D)
                p.op("act", lambda e: e.activation(out=hn[0:tn, :], in_=x1[0:tn, i, :], func=AF.Copy, scale=sv[0:tn, 2:3]),
                     [x1t[i], svt], [hnt])
                for half in range(2):
                    def tr(e):
                        for k in range(8):
                            kc = half * 8 + k
                            ins = e.transpose(G.psb[:, k * 128:k * 128 + tn], hn[0:tn, kc * 128:(kc + 1) * 128], G.ident[0:tn, 0:tn])
                        return ins
                    p.op("pe", tr, [hnt, G.cst], [G.psbt])
                    copy_op(p, "dve" if half == 0 else "act", hn2T[:, half * 8:(half + 1) * 8, i * 128:i * 128 + tn],
                            G.psb[:, :].rearrange("p (k t) -> p k t", k=8)[:, :, 0:tn], [G.psbt],
                            [hn2t] if (i == 0 and half == 0) else [], [] if (i == 0 and half == 0) else [hn2t])
            for bg in range(11):
                wg, wgt = wr.next()
                wv, wvt = wr.next()
                wg_v = wg[:, :].rearrange("p (k n) -> p k n", k=16)
                wv_v = wv[:, :].rearrange("p (k n) -> p k n", k=16)
                p.dma("sp", out=wg_v, in_=G.w_up[:, bg * 512:(bg + 1) * 512].rearrange("(kc q) n -> q kc n", q=128), writes=[wgt])
                p.dma("sp", out=wv_v, in_=G.w_up[:, DFF + bg * 512:DFF + (bg + 1) * 512].rearrange("(kc q) n -> q kc n", q=128), writes=[wvt])
                for j in range(4):
                    kk = bg * 4 + j
                    res = []
                    for (w_v, wt_, zg, ring) in ((wg_v, wgt, kk, ugr), (wv_v, wvt, 44 + kk, uvr)):
                        pi = p.nextps()

                        def mmu(e):
                            for kc in range(16):
                                ins = e.matmul(ps[pi][:, 0:n], w_v[:, kc, j * 128:(j + 1) * 128], hn2T[:, kc, 0:n], start=(kc == 0), stop=(kc == 15))
                            return ins
                        p.op("pe", mmu, [wt_, hn2t], [pst[pi]])
                        u, ut = ring.next()
                        z = ps[pi]
                        p.op("act", lambda e: e.activation(out=u[:, 1:n - 1], in_=z[:, 1:n - 1], func=AF.Identity,
                                                           scale=G.fcw[:, zg, 1:2], bias=G.fcb[:, zg:zg + 1]), [pst[pi], G.cst], [ut])
                        lo = 2 if first_c else 1
                        hi = n - 2 if last_c else n - 1
                        if n - 1 > lo:
                            p.op("dve", lambda e: e.scalar_tensor_tensor(out=u[:, lo:n - 1], in0=z[:, lo - 1:n - 2], scalar=G.fcw[:, zg, 0:1],
                                                                         in1=u[:, lo:n - 1], op0=ALU.mult, op1=ALU.add), [pst[pi], G.cst], [ut])
                        if first_c:
                            p.op("dve", lambda e: e.scalar_tensor_tensor(out=u[:, 1:2], in0=z[:, 0:1], scalar=w0m[:, zg:zg + 1],
                                                                         in1=u[:, 1:2], op0=ALU.mult, op1=ALU.add), [pst[pi], ct], [ut])
                        if hi > 1:
                            p.op("dve", lambda e: e.scalar_tensor_tensor(out=u[:, 1:hi], in0=z[:, 2:hi + 1], scalar=G.fcw[:, zg, 2:3],
                                                                         in1=u[:, 1:hi], op0=ALU.mult, op1=ALU.add), [pst[pi], G.cst], [ut])
                        if last_c:
                            p.op("dve", lambda e: e.scalar_tensor_tensor(out=u[:, n - 2:n - 1], in0=z[:, n - 1:n], scalar=w2m[:, zg:zg + 1],
                                                                         in1=u[:, n - 2:n - 1], op0=ALU.mult, op1=ALU.add), [pst[pi], ct], [ut])
                        res.append((u, ut))
                    (ug, ugt), (uv, uvt) = res
                    sg, sgt = sgr.next()
                    p.op("act", lambda e: e.activation(out=sg[:, 1:n - 1], in_=ug[:, 1:n - 1], func=AF.Silu), [ugt], [sgt])
                    tt_op(p, "pool", gT[:, kk, 1:n - 1], sg[:, 1:n - 1], uv[:, 1:n - 1], ALU.mult, [sgt, uvt], joins=[rawt])
                    p.op("pool", lambda e: e.memset(gT[:, kk, 0:1], 0.0), [], [], [rawt])
                    p.op("pool", lambda e: e.memset(gT[:, kk, n - 1:n], 0.0), [], [], [rawt])
            for cb in range(4):
                banks = [p.nextps() for _ in range(ntile)]
                for q4 in range(4):
                    wd, wdt = wr.next()
                    wd_v = wd[:, 0:11 * 512].rearrange("p (k n) -> p k n", k=11)
                    p.dma("sp", out=wd_v, in_=G.w_dn[q4 * 1408:(q4 + 1) * 1408, cb * 512:(cb + 1) * 512].rearrange("(kc q) n -> q kc n", q=128),
                          writes=[wdt])
                    for i in range(ntile):
                        tn = tns[i]
                        pi = banks[i]

                        def mmd(e):
                            for kc in range(11):
                                kk = q4 * 11 + kc
                                ins = e.matmul(ps[pi][0:tn, :], gT[:, kk, i * 128:i * 128 + tn], wd_v[:, kc, :], start=(kk == 0), stop=(kk == 43))
                            return ins
                        if q4 == 0:
                            p.op("pe", mmd, [wdt, rawt], [pst[pi]])
                        else:
                            p.op("pe", mmd, [wdt, rawt], [], [pst[pi]])
                for i in range(ntile):
                    tn = tns[i]
                    pi = banks[i]
                    dst = x1[0:tn, i, cb * 512:(cb + 1) * 512]
                    tt_op(p, "dve", dst, ps[pi][0:tn, :], dst, ALU.add, [pst[pi]], [x1t[i]])
            for i in range(ntile):
                tn = tns[i]
                hn, hnt = hnr.next()
                sv, svt = st.next()
                p.op("dve", lambda e: e.memset(sv[:, 0:1], 0.0), [], [svt])
                p.op("act", lambda e: e.activation(out=hn[0:tn, :], in_=x1[0:tn, i, :], func=AF.Square, accum_out=sv[0:tn, 0:1]),
                     [x1t[i]], [hnt, svt])
                rstd_ops(p, sv[0:tn, 0:1], sv[0:tn, 1:2], sv[0:tn, 2:3], svt, 1.0 / D)
                p.op("act", lambda e: e.activation(out=x1[0:tn, i, :], in_=x1[0:tn, i, :], func=AF.Copy, scale=sv[0:tn, 2:3]),
                     [svt], [x1t[i]])
                tt_op(p, "dve", x1[0:tn, i, :], x1[0:tn, i, :], normf[0:tn, :], ALU.mult, [ct], [x1t[i]])
                r_lo = 1 if i == 0 else 0
                r_hi = tn - 1 if i == ntile - 1 else tn
                if r_hi > r_lo:
                    o0 = p0 + 128 * i + r_lo - 1
                    p.dma("pool", out=J.y[o0:o0 + (r_hi - r_lo), :], in_=x1[r_lo:r_hi, i, :], reads=[x1t[i]])


def host_consts(L, h):
    Lq = L // 2
    N2 = L // 64
    N = 2 * L
    f32 = np.float32
    pos = ((np.arange(L) + h * Lq) % L).astype(f32)
    inv = (10000.0 ** (-np.arange(0, 64, 2, dtype=f32) / 64.0)).astype(f32)
    pidx = np.arange(128) % 64 % 32
    ang = (pos[None, :] * inv[pidx][:, None]).astype(f32)
    c = {}
    c["cos"] = np.cos(ang).astype(f32)
    c["sin"] = np.sin(ang).astype(f32)
    t = np.linspace(0.0, 1.0, L, dtype=f32)[:, None]
    w = (2.0 * math.pi * np.arange(L, dtype=f32)[:, None] / L).astype(f32)
    f = np.linspace(1e-4, 15, 16, dtype=f32)[None]
    z = np.concatenate([t, np.cos(f * w), -np.sin(f * w)], -1).astype(f32)
    vidx = np.concatenate([np.arange(L), [0], np.arange(L - 1, 0, -1)])
    c["zT"] = np.ascontiguousarray(z[vidx].T)
    c["trow"] = np.ascontiguousarray(np.tile(t[vidx].T, (128, 1))).astype(f32)
    n1 = np.arange(64, dtype=np.float64)[:, None]
    k1 = np.arange(128, dtype=np.float64)[None, :]
    n1f = np.arange(128, dtype=np.float64)[:, None]
    th = 2 * np.pi * n1f * k1 / 128
    F1f = np.concatenate([np.cos(th), -np.sin(th)], 1)
    n1t = ((np.arange(64) + 32 * h) % 64).astype(np.float64)[:, None]
    th = 2 * np.pi * n1t * k1 / 128
    F1u = np.concatenate([np.cos(th), -np.sin(th)], 1)
    c["F1f"] = F1f.astype(NPBF)
    c["F1u"] = F1u.astype(NPBF)
    th0 = 2 * np.pi * n1f * k1 / 128
    c["F1fb"] = np.concatenate([np.sin(th0), np.cos(th0)], 1).astype(NPBF)
    c["F1ub"] = np.concatenate([np.sin(th), np.cos(th)], 1).astype(NPBF)
    k1v = np.arange(128, dtype=np.float64)[:, None, None]
    n2v = np.arange(N2, dtype=np.float64)[None, :, None]
    k2v = np.arange(N2, dtype=np.float64)[None, None, :]
    th = 2 * np.pi * n2v * (k1v + 128 * k2v) / N
    Gt = np.stack([np.cos(th), -np.sin(th), np.sin(th)], 2)
    c["G"] = np.ascontiguousarray(Gt).astype(NPBF)
    ph = 2 * np.pi * np.arange(N2, dtype=np.float64)[:, None] * np.arange(N2, dtype=np.float64)[None, :] / N2
    c["T1"] = np.concatenate([np.cos(ph), np.sin(ph), -np.sin(ph), np.cos(ph)], 1).astype(NPBF)
    n1rot = np.array([63] + list(range(32)) + [32])
    n1true = ((n1rot + 32 * h) % 64).astype(np.float64)
    th = 2 * np.pi * (N2 * n1true[None, None, :] + np.arange(N2, dtype=np.float64)[None, :, None]) * \
        np.arange(128, dtype=np.float64)[:, None, None] / N
    c["Ginv"] = np.ascontiguousarray(np.stack([np.cos(th) / N, -np.sin(th) / N], 2)).astype(NPBF)
    return c


def shared_inputs(inp):
    f32 = np.float32
    a = lambda x: np.ascontiguousarray(np.asarray(x, dtype=f32))
    m = {}
    m["w_in"] = a(inp["w_in"][0])
    m["w_ao"] = a(inp["w_attn_out"][0])
    m["w_ho"] = a(inp["w_hyena_out"][0])
    m["w_out"] = a(inp["w_out"][0])
    m["w_up"] = a(inp["w_up"][0])
    m["w_dn"] = a(inp["w_down"][0])
    m["norm1"] = a(np.asarray(inp["norm1"][0]).reshape(16, 128).T)
    m["norm2"] = a(np.asarray(inp["norm2"][0]).reshape(16, 128).T)
    m["normf"] = a(np.tile(np.asarray(inp["norm_f"])[None, :], (128, 1)))
    m["gateb"] = a(np.asarray(inp["gate_b"][0]).reshape(32, 128).T)
    m["icw"] = a(np.asarray(inp["in_conv_w"][0]).reshape(3, 24, 128).transpose(2, 1, 0))
    m["icb"] = a(np.asarray(inp["in_conv_b"][0]).reshape(24, 128).T)
    m["fcw"] = a(np.asarray(inp["ffn_conv_w"][0]).reshape(3, 88, 128).transpose(2, 1, 0))
    m["fcb"] = a(np.asarray(inp["ffn_conv_b"][0]).reshape(88, 128).T)
    m["subg"] = a(np.asarray(inp["subln_g"][0]).reshape(128, 1))
    m["hbias"] = a(np.tile(np.asarray(inp["hyena_bias"][0])[None, :], (128, 1)))
    m["hbiasp"] = a(np.asarray(inp["hyena_bias"][0]).reshape(8, 128).T)
    m["lamv"] = a(np.stack([np.asarray(inp[k][0]) for k in ("lambda_q1", "lambda_k1", "lambda_q2", "lambda_k2")], 1))
    m["fw1"] = a(inp["filt_w1"][0])
    m["fw2"] = a(inp["filt_w2"][0])
    m["fw3"] = a(inp["filt_w3"][0])
    m["fw4"] = a(inp["filt_w4"][0])
    m["fvec"] = a(np.stack([np.asarray(inp[k][0]) for k in ("filt_b1", "filt_b2", "filt_b3", "filt_freq")], 1))
    max_decay = math.log(1e-2) / 0.3
    min_decay = math.log(1e-2) / 1.5
    deltas = np.abs(np.linspace(min_decay, max_decay, CH, dtype=f32))
    m["ndelta"] = a((-deltas).reshape(8, 128).T)
    m["ident"] = np.eye(128, dtype=f32).astype(NPBF)
    R = np.zeros((128, 128), f32)
    for blk in range(2):
        for d in range(64):
            if d < 32:
                R[blk * 64 + d + 32, blk * 64 + d] = -1.0
            else:
                R[blk * 64 + d - 32, blk * 64 + d] = 1.0
    m["rotm"] = R.astype(NPBF)
    return m


_NC_CACHE = {}


def run(inp, seqs, Ls, n_cores, trace=False, runner=None):
    key = tuple(Ls)
    if key not in _NC_CACHE:
        _NC_CACHE[key] = build(Ls)
    nc = _NC_CACHE[key]
    sh = shared_inputs(inp)
    in_maps = []
    for c in range(n_cores):
        pair, h = c // 2, c % 2
        m = dict(sh)
        m["flags"] = np.tile(np.array([[1.0 - (h == 0), 1.0 * (h == 0)]], np.float32), (128, 1))
        for j, L in enumerate(Ls):
            s = "_%d" % j
            x = np.asarray(seqs[j][pair], dtype=np.float32)
            m["x" + s] = np.ascontiguousarray(np.roll(x, -h * (L // 2), axis=0))
            for k, v in host_consts(L, h).items():
                m[k + s] = v
        in_maps.append(m)
    if runner is not None:
        res = runner(nc, in_maps)
    else:
        res = run_bass_kernel_spmd(nc, in_maps, core_ids=list(range(n_cores)), trace=trace)
    outs = [np.zeros(np.asarray(seqs[j]).shape, np.float32) for j in range(len(Ls))]
    for c in range(n_cores):
        pair, h = c // 2, c % 2
        for j, L in enumerate(Ls):
            Lq = L // 2
            outs[j][pair, h * Lq:(h + 1) * Lq] = res.results[c]["y_%d" % j]
    return outs, res


def kernel(**inputs):
    xs = np.asarray(inputs["x_sample"])
    xp = np.asarray(inputs["x_prompt"])
    outs, _ = run(inputs, [xs, xp], [xs.shape[1], xp.shape[1]], 8)
    return (outs[1], outs[0])
```
